# Optimizing a Trainium2 kernel written in Bass

```python
import jax, jax.numpy as jnp
from jax import lax
import numpy as np

D_MODEL = 1024
BATCH = 16
SEQ = 256
DEPTH = 4
DEC_BATCH = 4
DEC_SEQ = 2048
PAST_LEN = 256

GRID_W = 64
EPS = 1e-6
GLA_DK = 64
GLA_DV = 64
GLA_WIDTH = 3 * D_MODEL // 8
GLA_HEADS = GLA_WIDTH // GLA_DV
GLA_LOWRANK = 16
GLA_TAU = 16.0
GLA_CHUNK = 64
CONV_WIDTH = D_MODEL // 4
CONV_K = 3
NA_HD = 64
NA_WIDTH = 3 * D_MODEL // 8
NA_HEADS = NA_WIDTH // NA_HD
NA_MAX_ROWS = 8
NA_COLS = 16
CTX_BLOCK = 128
MIX_WIDTH = GLA_WIDTH + CONV_WIDTH + NA_WIDTH
PEER_HEADS = 8
PEER_NKEYS = 128
PEER_EXPERTS = PEER_NKEYS * PEER_NKEYS
PEER_DKEY = 128
PEER_TOPK = 16
PEER_CHUNK = 128

SPLIT_SIZES = (GLA_HEADS * GLA_DK, GLA_HEADS * GLA_DK, GLA_WIDTH, GLA_WIDTH,
               GLA_LOWRANK, GLA_LOWRANK,
               CONV_WIDTH, CONV_WIDTH, CONV_WIDTH,
               NA_WIDTH, NA_WIDTH, NA_WIDTH)
IN_WIDTH = 4 * GLA_WIDTH + 2 * GLA_LOWRANK + 3 * CONV_WIDTH + 3 * NA_WIDTH

kernel_name = "hymba_gla_conv_natten_peer_diffusion_step"


def rmsnorm(x, w):
    xf = x.astype(jnp.float32)
    y = xf * lax.rsqrt(jnp.mean(xf * xf, axis=-1, keepdims=True) + EPS) * w.astype(jnp.float32)
    return y.astype(x.dtype)


def modulate(cvec, w_ada, b_ada):
    m = (jax.nn.silu(cvec) @ w_ada + b_ada)[:, None, :]
    return jnp.split(m, 6, axis=-1)


def heads(a, n_heads):
    b, t, w = a.shape
    return a.reshape(b, t, n_heads, w // n_heads).transpose(0, 2, 1, 3)


def merge_heads(a):
    b, h, t, d = a.shape
    return a.transpose(0, 2, 1, 3).reshape(b, t, h * d)


def split_proj(h, w_in):
    z = h @ w_in
    cuts = [int(i) for i in np.cumsum(SPLIT_SIZES)[:-1]]
    return jnp.split(z, cuts, axis=-1)


def gla_chunked(q, k, v, log_a, s0):
    f32 = jnp.float32
    q, k, v, log_a = q.astype(f32), k.astype(f32), v.astype(f32), log_a.astype(f32)
    b_, h_, t_, _ = q.shape
    nc = t_ // GLA_CHUNK

    def to_chunks(a):
        return a.reshape(b_, h_, nc, GLA_CHUNK, a.shape[-1]).transpose(2, 0, 1, 3, 4)

    causal = jnp.tril(jnp.ones((GLA_CHUNK, GLA_CHUNK), dtype=bool))[:, :, None]

    def step(s, inp):
        qc, kc, vc, ac = inp
        cum = jnp.cumsum(ac, axis=-2)
        inter = jnp.einsum('bhtk,bhkv->bhtv', qc * jnp.exp(cum), s)
        diff = cum[:, :, :, None, :] - cum[:, :, None, :, :]
        decay = jnp.exp(jnp.where(causal, diff, -jnp.inf))
        att = jnp.einsum('bhtk,bhsk,bhtsk->bhts', qc, kc, decay)
        intra = jnp.einsum('bhts,bhsv->bhtv', att, vc)
        last = cum[:, :, -1:, :]
        s_new = jnp.exp(last[:, :, 0, :])[..., None] * s + jnp.einsum(
            'bhsk,bhsv->bhkv', kc * jnp.exp(last - cum), vc)
        return s_new, inter + intra

    s_fin, o = lax.scan(step, s0.astype(f32), (to_chunks(q), to_chunks(k), to_chunks(v), to_chunks(log_a)))
    o = o.transpose(1, 2, 0, 3, 4).reshape(b_, h_, t_, v.shape[-1])
    return o, s_fin


def gla_bidir(q, k, v, la_f, la_b, s0_f, s0_b):
    o_f, s_f = gla_chunked(q, k, v, la_f, s0_f)
    flip = lambda a: a[:, :, ::-1]
    o_b, s_b = gla_chunked(flip(q), flip(k), flip(v), flip(la_b), s0_b)
    return o_f + flip(o_b), s_f, s_b


def log_decay(lr, w_a, b_a):
    return jax.nn.log_sigmoid((lr @ w_a + b_a).astype(jnp.float32)) / GLA_TAU


def gla_output(o, g, w):
    o = o.transpose(0, 2, 1, 3)
    o = o * lax.rsqrt(jnp.mean(o * o, axis=-1, keepdims=True) + EPS) * w.astype(jnp.float32).reshape(GLA_HEADS, GLA_DV)
    b_, t_ = o.shape[0], o.shape[1]
    return o.reshape(b_, t_, GLA_WIDTH).astype(g.dtype) * jax.nn.silu(g)


def short_conv(u, w, b):
    up = jnp.pad(u, ((0, 0), (1, 1), (0, 0)))
    return w[0] * up[:, :-2] + w[1] * up[:, 1:-1] + w[2] * up[:, 2:] + b


def dense_attention(q, k, v):
    b_, h_, s_, d_ = q.shape
    nb = s_ // CTX_BLOCK
    qb = q.reshape(b_, h_, nb, CTX_BLOCK, d_).transpose(2, 0, 1, 3, 4)
    scale = d_ ** -0.5

    def blk(qi):
        s = jnp.einsum('bhqd,bhkd->bhqk', qi, k).astype(jnp.float32) * scale
        p = jax.nn.softmax(s, axis=-1).astype(v.dtype)
        return jnp.einsum('bhqk,bhkd->bhqd', p, v)

    o = lax.map(blk, qb)
    return o.transpose(1, 2, 0, 3, 4).reshape(b_, h_, s_, d_)


def neighbourhood_attention(q, k, v, k_ctx, v_ctx, rpb):
    b_, h_, t_, d_ = q.shape
    rows = t_ // GRID_W
    wr = min(NA_MAX_ROWS, rows)
    nwin = wr * GRID_W
    r = np.arange(rows)
    rs = np.clip(r - wr // 2, 0, rows - wr)
    key_rows = rs[:, None] + np.arange(wr)[None, :]
    cols = np.arange(GRID_W)
    cs = np.clip(cols - NA_COLS // 2, 0, GRID_W - NA_COLS)
    kcol = np.arange(nwin) % GRID_W
    col_mask = (kcol[None, :] >= cs[:, None]) & (kcol[None, :] < cs[:, None] + NA_COLS)
    dr = np.repeat(key_rows - r[:, None], GRID_W, axis=1)
    dc = np.clip(kcol[None, :] - cols[:, None], -(NA_COLS - 1), NA_COLS - 1)
    bias = rpb[:, dr[:, None, :] + NA_MAX_ROWS - 1, dc[None, :, :] + NA_COLS - 1].astype(jnp.float32)

    qg = q.reshape(b_, h_, rows, GRID_W, d_)
    kw = k.reshape(b_, h_, rows, GRID_W, d_)[:, :, key_rows].reshape(b_, h_, rows, nwin, d_)
    vw = v.reshape(b_, h_, rows, GRID_W, d_)[:, :, key_rows].reshape(b_, h_, rows, nwin, d_)
    scale = d_ ** -0.5
    s_win = jnp.einsum('bhrqd,bhrkd->bhrqk', qg, kw).astype(jnp.float32) * scale + bias
    s_win = jnp.where(col_mask[None, None, None], s_win, -1e30)
    s_ctx = jnp.einsum('bhrqd,bhcd->bhrqc', qg, k_ctx).astype(jnp.float32) * scale
    p = jax.nn.softmax(jnp.concatenate([s_win, s_ctx], axis=-1), axis=-1).astype(v.dtype)
    o = (jnp.einsum('bhrqk,bhrkd->bhrqd', p[..., :nwin], vw)
         + jnp.einsum('bhrqc,bhcd->bhrqd', p[..., nwin:], v_ctx))
    return o.reshape(b_, h_, t_, d_)


def mix_layer(h, lp, s0_f, s0_b, kv_ctx):
    qa, ka, va, ga, lrf, lrb, ch, cb, cc, qn, kn, vn = split_proj(h, lp['w_in'])
    q = heads(qa, GLA_HEADS) * GLA_DK ** -0.5
    k = heads(ka, GLA_HEADS)
    v = heads(va, GLA_HEADS)
    la_f = heads(log_decay(lrf, lp['w_af'], lp['b_af']), GLA_HEADS)
    la_b = heads(log_decay(lrb, lp['w_ab'], lp['b_ab']), GLA_HEADS)
    o_gla, s_f, s_b = gla_bidir(q, k, v, la_f, la_b, s0_f, s0_b)
    y_gla = gla_output(o_gla, ga, lp['gla_norm_w'])
    y_conv = cb * short_conv(cc * ch, lp['conv_w'], lp['conv_b'])
    qh, kh, vh = heads(qn, NA_HEADS), heads(kn, NA_HEADS), heads(vn, NA_HEADS)
    if kv_ctx is None:
        o_na = dense_attention(qh, kh, vh)
    else:
        o_na = neighbourhood_attention(qh, kh, vh, kv_ctx[0], kv_ctx[1], lp['na_rpb'])
    y = jnp.concatenate([y_gla, y_conv, merge_heads(o_na)], axis=-1) @ lp['w_out']
    return y, (kh, vh, s_f, s_b)


def peer_ffn(h, wq, k1, k2, u, v):
    b_, t_, d_ = h.shape
    n = b_ * t_
    xt = h.reshape(n, d_)
    q = (xt @ wq).reshape(n, PEER_HEADS, PEER_DKEY).astype(jnp.float32)
    half = PEER_DKEY // 2
    s1 = jnp.einsum('thd,hnd->thn', q[..., :half], k1.astype(jnp.float32))
    s2 = jnp.einsum('thd,hnd->thn', q[..., half:], k2.astype(jnp.float32))
    v1, i1 = lax.top_k(s1, PEER_TOPK)
    v2, i2 = lax.top_k(s2, PEER_TOPK)
    cand = (v1[..., :, None] + v2[..., None, :]).reshape(n, PEER_HEADS, PEER_TOPK * PEER_TOPK)
    cidx = (i1[..., :, None] * PEER_NKEYS + i2[..., None, :]).reshape(n, PEER_HEADS, PEER_TOPK * PEER_TOPK)
    top, pos = lax.top_k(cand, PEER_TOPK)
    idx = jnp.take_along_axis(cidx, pos, axis=-1)
    g = jax.nn.softmax(top, axis=-1)
    nc = n // PEER_CHUNK

    def expert_block(args):
        xc, ic, gc = args
        act = jax.nn.gelu(jnp.einsum('td,thkd->thk', xc, u[ic]))
        w = (gc * act.astype(jnp.float32)).astype(xc.dtype)
        return jnp.einsum('thk,thkd->td', w, v[ic])

    out = lax.map(expert_block, (xt.reshape(nc, PEER_CHUNK, d_),
                                 idx.reshape(nc, PEER_CHUNK, PEER_HEADS, PEER_TOPK),
                                 g.reshape(nc, PEER_CHUNK, PEER_HEADS, PEER_TOPK)))
    return out.reshape(b_, t_, d_)


def trunk_layer(x, cvec, lp, s0_f, s0_b, kv_ctx):
    sh1, sc1, g1, sh2, sc2, g2 = modulate(cvec, lp['w_ada'], lp['b_ada'])
    h = rmsnorm(x, lp['norm1_w']) * (1 + sc1) + sh1
    y, st = mix_layer(h, lp, s0_f, s0_b, kv_ctx)
    x = x + g1 * y
    h = rmsnorm(x, lp['norm2_w']) * (1 + sc2) + sh2
    x = x + g2 * peer_ffn(h, lp['peer_wq'], lp['peer_k1'], lp['peer_k2'], lp['peer_u'], lp['peer_v'])
    return x, st


def setup_inputs(seed: int = 0) -> dict:
    key = jax.random.key(seed)
    ks = jax.random.split(key, 32)
    nrm = lambda i, shape, s: jax.random.normal(ks[i], shape, jnp.float32) * s
    D = D_MODEL
    return {
        'x_prompt': nrm(0, (BATCH, SEQ, D), 1.0),
        'x_sample': nrm(1, (DEC_BATCH, DEC_SEQ, D), 1.0),
        'cache_na_k': nrm(2, (DEC_BATCH, DEPTH, NA_HEADS, PAST_LEN, NA_HD), 1.0),
        'cache_na_v': nrm(3, (DEC_BATCH, DEPTH, NA_HEADS, PAST_LEN, NA_HD), 1.0),
        'state_gla_fwd': nrm(4, (DEC_BATCH, DEPTH, GLA_HEADS, GLA_DK, GLA_DV), 1.0),
        'state_gla_bwd': nrm(5, (DEC_BATCH, DEPTH, GLA_HEADS, GLA_DK, GLA_DV), 1.0),
        'c': nrm(6, (DEC_BATCH, D), 1.0),
        'c_ctx': nrm(7, (D,), 1.0),
        'w_ada': nrm(8, (DEPTH, D, 6 * D), 0.5 * D ** -0.5),
        'b_ada': nrm(9, (DEPTH, 6 * D), 0.01),
        'norm1_w': 1.0 + nrm(10, (DEPTH, D), 0.01),
        'norm2_w': 1.0 + nrm(11, (DEPTH, D), 0.01),
        'w_in': nrm(12, (DEPTH, D, IN_WIDTH), D ** -0.5),
        'w_af': nrm(13, (DEPTH, GLA_LOWRANK, GLA_HEADS * GLA_DK), GLA_LOWRANK ** -0.5),
        'b_af': nrm(14, (DEPTH, GLA_HEADS * GLA_DK), 0.1),
        'w_ab': nrm(15, (DEPTH, GLA_LOWRANK, GLA_HEADS * GLA_DK), GLA_LOWRANK ** -0.5),
        'b_ab': nrm(16, (DEPTH, GLA_HEADS * GLA_DK), 0.1),
        'gla_norm_w': 1.0 + nrm(17, (DEPTH, GLA_WIDTH), 0.01),
        'conv_w': nrm(18, (DEPTH, CONV_K, CONV_WIDTH), CONV_K ** -0.5),
        'conv_b': nrm(19, (DEPTH, CONV_WIDTH), 0.01),
        'na_rpb': nrm(20, (DEPTH, NA_HEADS, 2 * NA_MAX_ROWS - 1, 2 * NA_COLS - 1), 0.1),
        'w_out': nrm(21, (DEPTH, MIX_WIDTH, D), MIX_WIDTH ** -0.5),
        'peer_wq': nrm(22, (DEPTH, D, PEER_HEADS * PEER_DKEY), D ** -0.5),
        'peer_k1': nrm(23, (DEPTH, PEER_HEADS, PEER_NKEYS, PEER_DKEY // 2), (PEER_DKEY // 2) ** -0.5),
        'peer_k2': nrm(24, (DEPTH, PEER_HEADS, PEER_NKEYS, PEER_DKEY // 2), (PEER_DKEY // 2) ** -0.5),
        'peer_u': nrm(25, (DEPTH, PEER_EXPERTS, D), D ** -0.5),
        'peer_v': nrm(26, (DEPTH, PEER_EXPERTS, D), 0.25),
        'final_norm_w': 1.0 + nrm(27, (D,), 0.01),
    }


def reference(x_prompt, x_sample, cache_na_k, cache_na_v, state_gla_fwd, state_gla_bwd, c, c_ctx,
              w_ada, b_ada, norm1_w, norm2_w, w_in, w_af, b_af, w_ab, b_ab, gla_norm_w,
              conv_w, conv_b, na_rpb, w_out, peer_wq, peer_k1, peer_k2, peer_u, peer_v, final_norm_w):
    x_p = x_prompt
    x_s = x_sample
    zeros_state = jnp.zeros((x_prompt.shape[0], GLA_HEADS, GLA_DK, GLA_DV), jnp.float32)
    new_k, new_v, new_sf, new_sb = [], [], [], []
    for l in range(DEPTH):
        lp = {'w_ada': w_ada[l], 'b_ada': b_ada[l], 'norm1_w': norm1_w[l], 'norm2_w': norm2_w[l],
              'w_in': w_in[l], 'w_af': w_af[l], 'b_af': b_af[l], 'w_ab': w_ab[l], 'b_ab': b_ab[l],
              'gla_norm_w': gla_norm_w[l], 'conv_w': conv_w[l], 'conv_b': conv_b[l],
              'na_rpb': na_rpb[l], 'w_out': w_out[l], 'peer_wq': peer_wq[l], 'peer_k1': peer_k1[l],
              'peer_k2': peer_k2[l], 'peer_u': peer_u[l], 'peer_v': peer_v[l]}
        x_p, (kc, vc, sf, sb) = trunk_layer(x_p, c_ctx[None, :], lp, zeros_state, zeros_state, None)
        new_k.append(kc)
        new_v.append(vc)
        new_sf.append(sf)
        new_sb.append(sb)
        x_s, _ = trunk_layer(x_s, c, lp, state_gla_fwd[:, l], state_gla_bwd[:, l],
                             (cache_na_k[:, l], cache_na_v[:, l]))
    y_prompt = rmsnorm(x_p, final_norm_w)
    y_sample = rmsnorm(x_s, final_norm_w)
    new_na_k = jnp.stack(new_k, axis=1)
    new_na_v = jnp.stack(new_v, axis=1)
    new_gla_fwd = jnp.stack(new_sf, axis=1)
    new_gla_bwd = jnp.stack(new_sb, axis=1)
    return (y_prompt, y_sample, new_na_k, new_na_v, new_gla_fwd, new_gla_bwd)
```

```python
import os
import numpy as np
from contextlib import ExitStack
import ml_dtypes
import concourse.bass as bass
import concourse.mybir as mybir
from concourse.bass_utils import run_bass_kernel_spmd


F32 = mybir.dt.float32
BF16 = mybir.dt.bfloat16
AF = mybir.ActivationFunctionType
ALU = mybir.AluOpType
AX = mybir.AxisListType
NPBF = ml_dtypes.bfloat16


class Res:
    __slots__ = ("name", "w", "rd")

    def __init__(self, name="r"):
        self.name = name
        self.w = None
        self.rd = []


class Op:
    __slots__ = ("eng", "fn", "dma", "waits", "signal", "sem", "val", "prevsame")

    def __init__(self, eng, fn, dma):
        self.eng = eng
        self.fn = fn
        self.dma = dma
        self.waits = []
        self.signal = False
        self.sem = None
        self.val = None
        self.prevsame = None


ENGINES = ("tensor", "vector", "scalar", "gpsimd", "sync")
NDMASEM = 16
SEMCH = 30000


class Prog:
    def __init__(self, nc):
        self.nc = nc
        self.streams = {e: [] for e in ENGINES}
        self.es = ExitStack()
        self.nops = 0
        self.pending = {e: [] for e in ENGINES}
        self.out_dma = []
        self.mute = False

    def sb(self, name, shape, dt):
        return self.es.enter_context(self.nc.sbuf_tensor("sb_" + name, list(shape), dt))

    def ps(self, name, shape, dt=F32):
        return self.es.enter_context(self.nc.psum_tensor(name, list(shape), dt))

    def op(self, eng, fn, reads=(), writes=(), dma=False):
        o = Op(eng, fn, dma)
        if self.mute:
            return o
        deps = []
        for r in reads:
            if r.w is not None:
                deps.append(r.w)
        for w in writes:
            if w.w is not None:
                deps.append(w.w)
            deps.extend(w.rd)
        if self.pending[eng]:
            deps.extend(self.pending[eng])
            self.pending[eng] = []
        seen = set()
        for d in deps:
            if d is o or id(d) in seen:
                continue
            seen.add(id(d))
            if d.eng == "tensor" and eng == "tensor" and not d.dma and not dma:
                continue
            d.signal = True
            o.waits.append(d)
        for r in reads:
            r.rd.append(o)
        for w in writes:
            w.w = o
            w.rd = []
        self.streams[eng].append(o)
        self.nops += 1
        if dma:
            self.out_dma.append(o)
        return o

    def barrier(self):
        lasts = []
        for e in ENGINES:
            st = self.streams[e]
            last_c = None
            for o in reversed(st):
                if not o.dma:
                    last_c = o
                    break
            if last_c is not None:
                lasts.append(last_c)
        lasts.extend(self.out_dma)
        self.out_dma = []
        for e in ENGINES:
            self.pending[e] = list(lasts)

    def dma(self, out, in_, reads=(), writes=(), q="sync", **kw):
        return self.op(q, lambda e: e.dma_start(out=out, in_=in_, **kw), reads, writes, dma=True)

    def mm(self, out, lhsT, rhs, start, stop, reads=(), writes=()):
        return self.op("tensor", lambda e: e.matmul(out, lhsT, rhs, start=start, stop=stop), reads, writes)

    def act(self, out, in_, func, reads=(), writes=(), eng="scalar", **kw):
        return self.op(eng, lambda e: e.activation(out=out, in_=in_, func=func, **kw), reads, writes)

    def tt(self, out, in0, in1, op, reads=(), writes=(), eng="vector"):
        return self.op(eng, lambda e: e.tensor_tensor(out=out, in0=in0, in1=in1, op=op), reads, writes)

    def ts(self, out, in0, s1, s2, op0, op1=None, reads=(), writes=(), eng="vector"):
        if op1 is None:
            return self.op(eng, lambda e: e.tensor_scalar(out=out, in0=in0, scalar1=s1, scalar2=None, op0=op0), reads, writes)
        return self.op(eng, lambda e: e.tensor_scalar(out=out, in0=in0, scalar1=s1, scalar2=s2, op0=op0, op1=op1), reads, writes)

    def stt(self, out, in0, scalar, in1, op0, op1, reads=(), writes=(), eng="vector"):
        return self.op(eng, lambda e: e.scalar_tensor_tensor(out=out, in0=in0, scalar=scalar, in1=in1, op0=op0, op1=op1), reads, writes)

    def cp(self, out, in_, reads=(), writes=(), eng="vector"):
        if eng == "scalar":
            return self.op(eng, lambda e: e.copy(out=out, in_=in_), reads, writes)
        return self.op(eng, lambda e: e.tensor_copy(out=out, in_=in_), reads, writes)

    def memset(self, ap, val, writes=(), eng="vector"):
        return self.op(eng, lambda e: e.memset(ap, val), (), writes)

    def emit(self):
        nc = self.nc
        es = self.es
        csems = {}
        dsems = {}
        for e in ENGINES:
            st = self.streams[e]
            nsig = sum(1 for o in st if (o.signal and not o.dma))
            csems[e] = [es.enter_context(nc.semaphore(f"c_{e}_{i}")) for i in range(nsig // SEMCH + 1)]
            ndma = sum(1 for o in st if o.dma)
            dsems[e] = [es.enter_context(nc.semaphore(f"d_{e}_{i}")) for i in range(min(NDMASEM, ndma))]
            cnt = 0
            dcnt = 0
            hist = []
            for o in st:
                if o.dma:
                    k = dcnt
                    o.sem = dsems[e][k % NDMASEM]
                    o.val = 16 * (k // NDMASEM + 1)
                    if k >= NDMASEM:
                        o.prevsame = hist[k - NDMASEM]
                    hist.append(o)
                    dcnt += 1
                elif o.signal:
                    o.sem = csems[e][cnt // SEMCH]
                    o.val = cnt % SEMCH + 1
                    cnt += 1
        finals = []
        for e in ENGINES:
            last = {}
            for o in self.streams[e]:
                if o.dma:
                    last[id(o.sem)] = (o.sem, o.val)
            finals.extend(last.values())
        streams = self.streams
        block = es.enter_context(nc.Block())

        def make(e):
            def run(engh):
                waited = {}
                for o in streams[e]:
                    need = {}
                    for d in o.waits:
                        k = id(d.sem)
                        if k not in need or need[k][1] < d.val:
                            need[k] = (d.sem, d.val)
                    if o.prevsame is not None:
                        k = id(o.sem)
                        v = o.prevsame.val
                        if k not in need or need[k][1] < v:
                            need[k] = (o.sem, v)
                    for k, (sem, val) in need.items():
                        if waited.get(k, 0) < val:
                            engh.wait_ge(sem, val)
                            waited[k] = val
                    ins = o.fn(engh)
                    if o.dma:
                        ins.then_inc(o.sem, 16)
                    elif o.signal:
                        ins.then_inc(o.sem, 1)
                if e == "sync":
                    for sem, val in finals:
                        engh.wait_ge(sem, val)
            return run

        for e in ENGINES:
            getattr(block, e)(make(e))


D = 1024
NT = 16
T = 2048
DEPTH = 4
INW = 3488
QA0, KA0, VA0, GA0, LRF0, LRB0 = 0, 384, 768, 1152, 1536, 1552
CH0, CB0, CC0 = 1568, 1824, 2080
QN0, KN0, VN0 = 2336, 2720, 3104
NTYPE = 27
ARENA_W = 24704


def na_blocks(i):
    if 2 <= i <= 13:
        return [(i - 2 + m, (i % 2) * 5 + m) for m in range(5)]
    e = {0: 0, 1: 1, 14: 2, 15: 3}[i]
    base = 0 if i < 2 else 12
    return [(base + m, 10 + e * 4 + m) for m in range(4)]


class Arena:
    def __init__(self, P):
        self.t = P.sb("arena", [128, ARENA_W], F32)
        self.off = 0

    def reset(self):
        self.off = 0

    def f32(self, shape):
        n = int(np.prod(shape[1:]))
        v = self.t[:, self.off:self.off + n]
        self.off += n
        assert self.off <= ARENA_W, self.off
        return self._shape(v, shape)

    def bf16(self, shape):
        n = int(np.prod(shape[1:]))
        assert n % 2 == 0
        v = self.t[:, self.off:self.off + n // 2].bitcast(BF16)
        self.off += n // 2
        assert self.off <= ARENA_W, self.off
        return self._shape(v, shape)

    @staticmethod
    def _shape(v, shape):
        if len(shape) == 2:
            return v
        if len(shape) == 3:
            return v.rearrange("p (a b) -> p a b", a=shape[1], b=shape[2])
        if len(shape) == 4:
            return v.rearrange("p (a b c) -> p a b c", a=shape[1], b=shape[2], c=shape[3])
        raise ValueError(shape)


def build(nlayers=DEPTH, phases="CNGP", raw=False):
    nc = bass.Bass("TRN2", target_bir_lowering=False)

    def din(name, shape, dt=F32):
        return nc.dram_tensor(name, list(shape), dt, kind="ExternalInput").ap()

    def dout(name, shape):
        return nc.dram_tensor(name, list(shape), F32, kind="ExternalOutput").ap()

    x_d = din("x", [T, D]); cvec_d = din("cvec", [D])
    kctx_d = din("kctx", [DEPTH, 6, 256, 64]); vctx_d = din("vctx", [DEPTH, 6, 256, 64])
    s0f_d = din("s0f", [DEPTH, 6, 64, 64]); s0b_d = din("s0b", [DEPTH, 6, 64, 64])
    carry_d = din("carry", [128, 1])
    nabias_d = din("nabias", [DEPTH, 6, 26, 128, 128]); namask_d = din("namask", [NTYPE, 128, 128])
    ident_d = din("ident", [128, 128], BF16)
    glam_d = din("glamask", [2, 128, 128])
    w_ada_d = din("w_ada", [DEPTH, D, 6 * D]); b_ada_d = din("b_ada", [DEPTH, 6 * D])
    n1_d = din("norm1_w", [DEPTH, D]); n2_d = din("norm2_w", [DEPTH, D])
    w_in_d = din("w_in", [DEPTH, D, INW])
    waf_d = din("w_af", [DEPTH, 16, 384]); baf_d = din("b_af", [DEPTH, 384])
    wab_d = din("w_ab", [DEPTH, 16, 384]); bab_d = din("b_ab", [DEPTH, 384])
    gnw_d = din("gla_norm_w", [DEPTH, 384])
    cw_d = din("conv_w", [DEPTH, 3, 256]); cb_d = din("conv_b", [DEPTH, 256])
    wout_d = din("w_out", [DEPTH, D, D]); wq_d = din("peer_wq", [DEPTH, D, D])
    k1_d = din("peer_k1", [DEPTH, 8, 128, 64]); k2_d = din("peer_k2", [DEPTH, 8, 128, 64])
    pu_d = din("peer_u", [DEPTH, 16384, D]); pv_d = din("peer_v", [DEPTH, 16384, D])
    fnw_d = din("final_norm_w", [D])
    y_d = dout("y", [T, D]); nk_d = dout("nk", [DEPTH, T, 384]); nv_d = dout("nv", [DEPTH, T, 384])
    sf_d = dout("sf", [DEPTH, 8, 6, 64, 64]); sbo_d = dout("sbo", [DEPTH, 8, 6, 64, 64])

    P = Prog(nc)
    NSC = dict(allow_slow_non_contiguous=True)
    x = P.sb("x", [128, NT, D], F32); rx = [Res() for _ in range(NT)]
    hT = P.sb("hT", [128, 8, T], BF16); rhT = Res()
    ident = P.sb("ident", [128, 128], BF16); rid = Res()
    identf = P.sb("identf", [128, 128], F32)
    onesf = P.sb("onesf", [128, 128], F32)
    glam = P.sb("glam", [128, 2, 128], F32)
    carry = P.sb("carry", [128, 1], F32)
    carm1 = P.sb("carm1", [128, 1], F32)
    scol = P.sb("scol", [128, 8], F32)
    modcol = P.sb("modcol", [128, 4, 8], F32); rmod = Res()
    acol = P.sb("acol", [128, 2, 8], F32)
    nwcol = P.sb("nwcol", [128, 2, 8], F32)
    gb = P.sb("gb", [128, 2, D], F32); rgb = Res()
    small = P.sb("small", [128, 64], F32); rsmall = Res()
    rconst = Res()
    ar = Arena(P)
    banks = [P.ps(f"bank{i}", [128, 512], F32) for i in range(8)]
    rbank = [Res() for _ in range(8)]
    rr = {"s": 0, "a": 0, "b": 0}

    def bank(pool):
        if pool == "s":
            i = rr["s"] % 4
        elif pool == "a":
            i = 4 + rr["a"] % 2
        else:
            i = 6 + rr["b"] % 2
        rr[pool] += 1
        return banks[i], rbank[i]

    for i in range(NT):
        P.dma(x[:, i, :], x_d[i * 128:(i + 1) * 128, :], writes=[rx[i]])
    P.dma(ident[:], ident_d[:, :], writes=[rid])
    P.dma(glam[:], glam_d.rearrange("a s t -> s a t"), writes=[rconst])
    P.dma(carry[:], carry_d[:, :], writes=[rconst])
    P.dma(scol[:], cvec_d.rearrange("(k p) -> p k", p=128), writes=[rconst], **NSC)
    P.cp(identf[:], ident[:], reads=[rid], writes=[rconst])
    P.memset(onesf[:], 1.0, writes=[rconst])
    P.ts(carm1[:], carry[:], -1.0, None, ALU.add, reads=[rconst], writes=[rconst])
    P.act(scol[:], scol[:], AF.Silu, reads=[rconst], writes=[rconst])

    def rstd_of(src_ap, n, out_ap, rsrc, scratch, rscratch):
        P.act(scratch, src_ap, AF.Square, reads=rsrc, writes=[rscratch, rsmall], accum_out=out_ap)
        P.ts(out_ap, out_ap, 1.0 / n, 1e-6, ALU.mult, ALU.add, reads=[rsmall], writes=[rsmall])
        P.act(out_ap, out_ap, AF.Sqrt, reads=[rsmall], writes=[rsmall])
        P.op("vector", lambda e: e.reciprocal(out=out_ap, in_=out_ap), [rsmall], [rsmall])

    def norm_to_hT(which):
        ar.reset()
        sq = ar.f32([128, D]); rsq = Res()
        xn = ar.bf16([128, 2, D]); rxn = [Res(), Res()]
        for i in range(NT):
            rstd_of(x[:, i, :], D, small[:, 0:1], [rx[i]], sq, rsq)
            P.ts(xn[:, i % 2, :], x[:, i, :], small[:, 0:1], None, ALU.mult, reads=[rx[i], rsmall], writes=[rxn[i % 2]])
            for hf in range(2):
                bk, rb = bank("s")
                for c4 in range(4):
                    c = hf * 4 + c4
                    P.mm(bk[:, c4 * 128:(c4 + 1) * 128], xn[:, i % 2, c * 128:(c + 1) * 128], ident[:], True, True,
                         reads=[rxn[i % 2], rid], writes=[rb])
                for c4 in range(4):
                    c = hf * 4 + c4
                    P.ts(hT[:, c, i * 128:(i + 1) * 128], bk[:, c4 * 128:(c4 + 1) * 128], acol[:, which, c:c + 1],
                         modcol[:, 2 * which, c:c + 1], ALU.mult, ALU.add, reads=[rb, rmod], writes=[rhT],
                         eng=("vector" if c4 % 2 == 0 else "gpsimd") if False else "vector")

    wst = [None, None]; rwst = [Res(), Res()]; wbf = [None, None]; rwbf = [Res(), Res()]
    wcnt = [0]

    def alloc_w():
        for k in range(2):
            wst[k] = ar.f32([128, 8, 128])
            wbf[k] = ar.bf16([128, 8, 128])

    def wcols(dram_ap, n, dst=None, rdst=None, dcol=0):
        k = wcnt[0] % 2
        wcnt[0] += 1
        P.dma(wst[k][:, :, 0:n], dram_ap.rearrange("(c p) n -> p c n", p=128), writes=[rwst[k]])
        if dst is None:
            dst, rdst, dcol = wbf[k], rwbf[k], 0
        P.cp(dst[:, :, dcol:dcol + n], wst[k][:, :, 0:n], reads=[rwst[k]], writes=[rdst], eng="gpsimd")
        return dst, rdst

    def proj_fm(wb, rwb, wcol0, m, evac):
        for tg in range(4):
            bk, rb = bank("s")
            for c in range(8):
                P.mm(bk[0:m, :], wb[:, c, wcol0:wcol0 + m], hT[:, c, tg * 512:(tg + 1) * 512], c == 0, c == 7,
                     reads=[rwb, rhT], writes=[rb])
            evac(tg, bk, rb)

    def outproj(cat, rcat, row0, nch, l):
        wo_st = ar.f32([128, D]); rwo_st = Res()
        wob = ar.bf16([128, nch, D]); rwob = Res()
        for j in range(nch):
            P.dma(wo_st[:], wout_d[l, row0 + j * 128: row0 + (j + 1) * 128, :], writes=[rwo_st])
            P.tt(wob[:, j, :], wo_st[:], gb[:, 0, :], ALU.mult, reads=[rwo_st, rgb], writes=[rwob], eng="gpsimd")
        for i in range(NT):
            for hf in range(2):
                bk, rb = bank("b")
                for j in range(nch):
                    P.mm(bk[:, :], cat[:, j, i * 128:(i + 1) * 128], wob[:, j, hf * 512:(hf + 1) * 512], j == 0, j == nch - 1,
                         reads=[rcat, rwob], writes=[rb])
                P.tt(x[:, i, hf * 512:(hf + 1) * 512], x[:, i, hf * 512:(hf + 1) * 512], bk[:, :], ALU.add,
                     reads=[rx[i], rb], writes=[rx[i]])

    for l in range(nlayers):
        P.barrier()
        ar.reset()
        wa = [ar.f32([128, 8, 512]), ar.f32([128, 8, 512])]; rwa = [Res(), Res()]
        row = ar.f32([1, 512]); rrow = Res()
        brow = ar.f32([1, 512]); rbrow = Res()
        P.dma(nwcol[:, 0, :], n1_d[l].rearrange("(k p) -> p k", p=128), writes=[rmod], **NSC)
        P.dma(nwcol[:, 1, :], n2_d[l].rearrange("(k p) -> p k", p=128), writes=[rmod], **NSC)
        for g in range(12):
            k = g % 2
            P.dma(wa[k][:], w_ada_d[l, :, g * 512:(g + 1) * 512].rearrange("(c p) n -> p c n", p=128), writes=[rwa[k]])
            P.dma(brow[0:1, :], b_ada_d[l:l + 1, g * 512:(g + 1) * 512], writes=[rbrow])
            bk, rb = bank("s")
            for c in range(8):
                P.mm(bk[0:1, :], scol[:, c:c + 1], wa[k][:, c, :], c == 0, c == 7, reads=[rconst, rwa[k]], writes=[rb])
            P.tt(row[0:1, :], bk[0:1, :], brow[0:1, :], ALU.add, reads=[rb, rbrow], writes=[rrow])
            vec = g // 2
            if vec in (0, 1, 3, 4):
                mi = {0: 0, 1: 1, 3: 2, 4: 3}[vec]
                bk2, rb2 = bank("s")
                for j in range(4):
                    P.mm(bk2[:, j:j + 1], row[0:1, j * 128:(j + 1) * 128], onesf[0:1, 0:1], True, True,
                         reads=[rrow, rconst], writes=[rb2])
                P.cp(modcol[:, mi, k * 4:(k + 1) * 4], bk2[:, 0:4], reads=[rb2], writes=[rmod])
            else:
                gi = 0 if vec == 2 else 1
                bk2, rb2 = bank("s")
                P.mm(bk2[:, :], onesf[0:1, 0:128], row[0:1, :], True, True, reads=[rrow, rconst], writes=[rb2])
                P.cp(gb[:, gi, k * 512:(k + 1) * 512], bk2[:, :], reads=[rb2], writes=[rgb])
        for w_ in range(2):
            P.stt(acol[:, w_, :], modcol[:, 2 * w_ + 1, :], 1.0, nwcol[:, w_, :], ALU.add, ALU.mult,
                  reads=[rmod], writes=[rmod])
        P.barrier()
        norm_to_hT(0)

        P.mute = "C" not in phases
        P.barrier()
        ar.reset()
        alloc_w()
        zc = ar.f32([128, 3, T]); rzc = Res()
        u = ar.f32([128, T]); ru = Res()
        yv = ar.f32([128, T]); ryv = Res()
        catC = ar.bf16([128, 2, T]); rcatC = Res()
        cwc = ar.f32([128, 2, 4]); rcwc = Res()
        wn = ar.f32([128, 2, 2]); rwn = Res()
        for j in range(2):
            for k_ in range(3):
                P.dma(cwc[:, j, k_:k_ + 1], cw_d[l, k_, j * 128:(j + 1) * 128].rearrange("(p o) -> p o", o=1), writes=[rcwc], **NSC)
            P.dma(cwc[:, j, 3:4], cb_d[l, j * 128:(j + 1) * 128].rearrange("(p o) -> p o", o=1), writes=[rcwc], **NSC)
        for j in range(2):
            P.ts(wn[:, j, 0:1], cwc[:, j, 0:1], carm1[:, 0:1], None, ALU.mult, reads=[rcwc, rconst], writes=[rwn])
            P.ts(wn[:, j, 1:2], cwc[:, j, 2:3], carm1[:, 0:1], None, ALU.mult, reads=[rcwc, rconst], writes=[rwn])
        for j in range(2):
            for wi, c0 in enumerate((CH0, CB0, CC0)):
                wb, rwb = wcols(w_in_d[l, :, c0 + j * 128:c0 + (j + 1) * 128], 128)
                proj_fm(wb, rwb, 0, 128, lambda tg, bk, rb, wi=wi: P.cp(zc[:, wi, tg * 512:(tg + 1) * 512], bk[:, :],
                                                                      reads=[rb], writes=[rzc], eng="scalar"))
            P.tt(u[:], zc[:, 2, :], zc[:, 0, :], ALU.mult, reads=[rzc], writes=[ru], eng="gpsimd")
            P.ts(yv[:], u[:], cwc[:, j, 1:2], None, ALU.mult, reads=[ru, rcwc], writes=[ryv])
            P.stt(yv[:, 1:T], u[:, 0:T - 1], cwc[:, j, 0:1], yv[:, 1:T], ALU.mult, ALU.add, reads=[ru, rcwc, ryv], writes=[ryv])
            P.stt(yv[:, 0:T - 1], u[:, 1:T], cwc[:, j, 2:3], yv[:, 0:T - 1], ALU.mult, ALU.add, reads=[ru, rcwc, ryv], writes=[ryv])
            P.stt(yv[:, 256:T:256], u[:, 255:T - 1:256], wn[:, j, 0:1], yv[:, 256:T:256], ALU.mult, ALU.add,
                  reads=[ru, rwn, ryv], writes=[ryv])
            P.stt(yv[:, 255:T - 1:256], u[:, 256:T:256], wn[:, j, 1:2], yv[:, 255:T - 1:256], ALU.mult, ALU.add,
                  reads=[ru, rwn, ryv], writes=[ryv])
            P.stt(catC[:, j, :], yv[:], cwc[:, j, 3:4], zc[:, 1, :], ALU.add, ALU.mult, reads=[ryv, rcwc, rzc], writes=[rcatC])
        outproj(catC, rcatC, 384, 2, l)

        P.mute = "N" not in phases
        P.barrier()
        ar.reset()
        alloc_w()
        namask = ar.f32([128, NTYPE, 128]); rnamask = Res()
        P.dma(namask[:], namask_d.rearrange("t k q -> k t q"), writes=[rnamask])
        nab = ar.f32([128, 26, 128]); rnab = Res()
        BM = ar.bf16([128, 2, NTYPE, 128]); rBM = Res()
        qT = ar.bf16([128, T]); rqT = Res()
        kT = ar.bf16([128, T]); rkT = Res()
        wkv = ar.bf16([128, 8, 256]); rwkv = Res()
        kvst = [ar.f32([128, 256]), ar.f32([128, 256])]; rkvst = [Res(), Res()]
        vaug = ar.bf16([128, NT, 2, 66]); rvaug = Res()
        cst = ar.f32([128, 2, 2, 64]); rcst = Res()
        kcb = ar.bf16([128, 2, 128]); rkcb = Res()
        kcT = ar.bf16([128, 256]); rkcT = Res()
        vca = ar.bf16([128, 2, 2, 66]); rvca = Res()
        pT = [ar.bf16([128, 128]) for _ in range(3)]; rpT = [Res() for _ in range(3)]
        opair = [ar.bf16([128, 128]) for _ in range(2)]; ropair = [Res(), Res()]
        catN = ar.bf16([128, 1, T]); rcatN = Res()
        P.memset(vaug[:, :, :, 64:66], 1.0, writes=[rvaug])
        P.memset(vca[:, :, :, 64:66], 1.0, writes=[rvca])
        pcnt = 0
        for p in range(3):
            wb, rwb = wcols(w_in_d[l, :, QN0 + p * 128:QN0 + (p + 1) * 128], 128)
            proj_fm(wb, rwb, 0, 128, lambda tg, bk, rb: P.act(qT[:, tg * 512:(tg + 1) * 512], bk[:, :], AF.Copy,
                                                             reads=[rb], writes=[rqT], scale=0.125))
            wb, rwb = wcols(w_in_d[l, :, KN0 + p * 128:KN0 + (p + 1) * 128], 128)
            proj_fm(wb, rwb, 0, 128, lambda tg, bk, rb: P.cp(kT[:, tg * 512:(tg + 1) * 512], bk[:, :],
                                                            reads=[rb], writes=[rkT], eng="scalar"))
            wcols(w_in_d[l, :, KN0 + p * 128:KN0 + (p + 1) * 128], 128, wkv, rwkv, 0)
            wcols(w_in_d[l, :, VN0 + p * 128:VN0 + (p + 1) * 128], 128, wkv, rwkv, 128)
            for i in range(NT):
                bk, rb = bank("s")
                for c in range(8):
                    P.mm(bk[:, 0:256], hT[:, c, i * 128:(i + 1) * 128], wkv[:, c, :], c == 0, c == 7,
                         reads=[rhT, rwkv], writes=[rb])
                kk_ = i % 2
                P.cp(kvst[kk_][:], bk[:, 0:256], reads=[rb], writes=[rkvst[kk_]], eng="scalar")
                P.dma(nk_d[l, i * 128:(i + 1) * 128, p * 128:(p + 1) * 128], kvst[kk_][:, 0:128], reads=[rkvst[kk_]], q="gpsimd")
                P.dma(nv_d[l, i * 128:(i + 1) * 128, p * 128:(p + 1) * 128], kvst[kk_][:, 128:256], reads=[rkvst[kk_]], q="gpsimd")
                P.cp(vaug[:, i, :, 0:64], kvst[kk_][:, 128:256].rearrange("p (h d) -> p h d", h=2),
                     reads=[rkvst[kk_]], writes=[rvaug])
            for kt in range(2):
                P.dma(cst[:, kt, :, :], kctx_d[l, 2 * p:2 * p + 2, kt * 128:(kt + 1) * 128, :].rearrange("h q d -> q h d"), writes=[rcst])
            P.cp(kcb[:].rearrange("p kt (h d) -> p kt h d", h=2), cst[:], reads=[rcst], writes=[rkcb])
            for kt in range(2):
                bk, rb = bank("s")
                P.mm(bk[:, 0:128], kcb[:, kt, :], ident[:], True, True, reads=[rkcb, rid], writes=[rb])
                P.cp(kcT[:, kt * 128:(kt + 1) * 128], bk[:, 0:128], reads=[rb], writes=[rkcT], eng="scalar")
            for kt in range(2):
                P.dma(cst[:, kt, :, :], vctx_d[l, 2 * p:2 * p + 2, kt * 128:(kt + 1) * 128, :].rearrange("h q d -> q h d"), writes=[rcst])
            P.cp(vca[:, :, :, 0:64], cst[:], reads=[rcst], writes=[rvca])
            for hh in range(2):
                P.dma(nab[:], nabias_d[l, 2 * p + hh].rearrange("t k q -> k t q"), writes=[rnab])
                P.tt(BM[:, hh, 0:26, :], nab[:], namask[:, 0:26, :], ALU.add, reads=[rnab, rnamask], writes=[rBM])
                P.cp(BM[:, hh, 26, :], namask[:, 26, :], reads=[rnamask], writes=[rBM])
            for i in range(NT):
                for hh in range(2):
                    hp = slice(hh * 64, hh * 64 + 64)
                    blocks = [("l", kt, ty) for kt, ty in na_blocks(i)] + [("c", 0, 26), ("c", 1, 26)]
                    bo, rbo = bank("a")
                    for bi, (src, kt, ty) in enumerate(blocks):
                        bs, rbs = bank("s")
                        ksrc, rks = (kT, rkT) if src == "l" else (kcT, rkcT)
                        P.mm(bs[:, 0:128], ksrc[hp, kt * 128:(kt + 1) * 128], qT[hp, i * 128:(i + 1) * 128], True, False,
                             reads=[rks, rqT], writes=[rbs])
                        P.mm(bs[:, 0:128], ident[:], BM[:, hh, ty, :], False, True, reads=[rid, rBM], writes=[rbs])
                        pk = pcnt % 3
                        pcnt += 1
                        P.act(pT[pk][:], bs[:, 0:128], AF.Exp, reads=[rbs], writes=[rpT[pk]])
                        vsrc, rvs = (vaug[:, kt, hh, 0:65], rvaug) if src == "l" else (vca[:, kt, hh, 0:65], rvca)
                        P.mm(bo[:, 0:65], pT[pk][:], vsrc, bi == 0, bi == len(blocks) - 1, reads=[rpT[pk], rvs], writes=[rbo])
                    P.op("vector", lambda e, bo=bo: e.reciprocal(out=small[:, 1:2], in_=bo[:, 64:65]), [rbo], [rsmall])
                    P.ts(opair[i % 2][:, hp], bo[:, 0:64], small[:, 1:2], None, ALU.mult, reads=[rbo, rsmall], writes=[ropair[i % 2]])
                bk, rb = bank("b")
                P.mm(bk[:, 0:128], opair[i % 2][:], ident[:], True, True, reads=[ropair[i % 2], rid], writes=[rb])
                P.cp(catN[:, 0, i * 128:(i + 1) * 128], bk[:, 0:128], reads=[rb], writes=[rcatN], eng="scalar")
            off_save = ar.off
            outproj(catN, rcatN, 640 + p * 128, 1, l)
            ar.off = off_save
            P.barrier()

        P.mute = "G" not in phases
        ar.reset()
        alloc_w()
        lrT = [ar.bf16([16, T]), ar.bf16([16, T])]; rlrT = Res()
        wa16 = ar.f32([16, 2, 384]); rwa16 = Res()
        wab16 = ar.bf16([16, 2, 384]); rwab16 = Res()
        P.dma(wa16[0:16, 0, :], waf_d[l], writes=[rwa16])
        P.dma(wa16[0:16, 1, :], wab_d[l], writes=[rwa16])
        P.cp(wab16[0:16], wa16[0:16], reads=[rwa16], writes=[rwab16])
        for d_ in range(2):
            wb, rwb = wcols(w_in_d[l, :, LRF0 + d_ * 16:LRF0 + (d_ + 1) * 16], 16)
            proj_fm(wb, rwb, 0, 16, lambda tg, bk, rb, d_=d_: P.cp(lrT[d_][0:16, tg * 512:(tg + 1) * 512], bk[0:16, :],
                                                                  reads=[rb], writes=[rlrT], eng="scalar"))
        gqT = ar.bf16([128, T]); rgq = Res()
        gkT = ar.bf16([128, T]); rgk = Res()
        a_off = ar.off
        A_ = ar.f32([128, T]); rA = Res()
        B_ = ar.f32([128, T]); rB = Res()
        Eb = ar.bf16([128, T]); rEb = Res()
        totc = ar.f32([128, NT]); rtotc = Res()
        _qd = ar.bf16([128, T]); _rqd = Res(); qd = [_qd, _qd]; rqd = [_rqd, _rqd]
        _kd = ar.bf16([128, T]); _rkd = Res(); kd = [_kd, _kd]; rkd = [_rkd, _rkd]
        _kdec = ar.bf16([128, T]); _rkdec = Res(); kdec = [_kdec, _kdec]; rkdec = [_rkdec, _rkdec]
        _kdt = ar.bf16([128, NT, 128]); _rkdt = Res(); kdtok = [_kdt, _kdt]; rkdtok = [_rkdt, _rkdt]
        elast = ar.f32([128, 2, NT]); rel = Res()
        wvg = ar.bf16([128, 8, 256]); rwvg = Res()
        v_b = ar.bf16([128, NT, 128]); rvb = Res()
        sg = ar.bf16([128, NT, 128]); rsg = Res()
        of = ar.f32([128, NT, 128]); rof = Res()
        ycat = sg; rycat = rsg
        catG = gqT.rearrange("p (o t) -> p o t", o=1); rcatG = rgq
        nb = ar.f32([128, 2]); rnb = Res()
        gwb = ar.f32([128, 128]); rgwb = Res()
        S = ar.f32([128, 64]); rS = Res()
        Sb = ar.bf16([128, 64]); rSb = Res()
        s0 = ar.f32([128, 2, 64]); rs0 = Res()
        sst = ar.f32([128, 2, 8, 64]); rsst = Res()
        attm = [ar.bf16([128, 2, 128]), ar.bf16([128, 2, 128])]; rattm = [Res(), Res()]
        ot = ar.f32([128, 128]); rot = Res()
        osq = ar.f32([128, 128]); rosq = Res()
        Bv = B_.rearrange("p (c j) -> p c j", j=128)
        totb = totc[:, :].unsqueeze(2).to_broadcast([128, NT, 128])
        Av = A_.rearrange("p (c j) -> p c j", j=128)
        for p in range(3):
            cs = slice(p * 128, (p + 1) * 128)
            wb, rwb = wcols(w_in_d[l, :, QA0 + p * 128:QA0 + (p + 1) * 128], 128)
            proj_fm(wb, rwb, 0, 128, lambda tg, bk, rb: P.act(gqT[:, tg * 512:(tg + 1) * 512], bk[:, :], AF.Copy,
                                                             reads=[rb], writes=[rgq], scale=0.125))
            wb, rwb = wcols(w_in_d[l, :, KA0 + p * 128:KA0 + (p + 1) * 128], 128)
            proj_fm(wb, rwb, 0, 128, lambda tg, bk, rb: P.cp(gkT[:, tg * 512:(tg + 1) * 512], bk[:, :],
                                                            reads=[rb], writes=[rgk], eng="scalar"))
            wcols(w_in_d[l, :, VA0 + p * 128:VA0 + (p + 1) * 128], 128, wvg, rwvg, 0)
            wcols(w_in_d[l, :, GA0 + p * 128:GA0 + (p + 1) * 128], 128, wvg, rwvg, 128)
            for i in range(NT):
                bk, rb = bank("s")
                for c in range(8):
                    P.mm(bk[:, 0:256], hT[:, c, i * 128:(i + 1) * 128], wvg[:, c, :], c == 0, c == 7,
                         reads=[rhT, rwvg], writes=[rb])
                P.cp(v_b[:, i, :], bk[:, 0:128], reads=[rb], writes=[rvb], eng="scalar")
                P.act(sg[:, i, :], bk[:, 128:256], AF.Silu, reads=[rb], writes=[rsg])
            if os.environ.get("GSTOP") == "b":
                P.mute = True
            P.dma(nb[:, 0:1], baf_d[l, cs].rearrange("(p o) -> p o", o=1), writes=[rnb], **NSC)
            P.dma(nb[:, 1:2], bab_d[l, cs].rearrange("(p o) -> p o", o=1), writes=[rnb], **NSC)
            P.ts(nb[:], nb[:], -1.0, None, ALU.mult, reads=[rnb], writes=[rnb])
            P.dma(gwb[:], gnw_d[l, cs].partition_broadcast(128), writes=[rgwb])
            P.dma(s0[:, 0, :], s0f_d[l, 2 * p:2 * p + 2].rearrange("h k v -> (h k) v"), writes=[rs0])
            P.dma(s0[:, 1, :], s0b_d[l, 2 * p:2 * p + 2].rearrange("h k v -> (h k) v"), writes=[rs0])
            for d_ in range(2):
                for tg in range(4):
                    bk, rb = bank("s")
                    P.mm(bk[:, :], wab16[0:16, d_, cs], lrT[d_][0:16, tg * 512:(tg + 1) * 512], True, True,
                         reads=[rwab16, rlrT], writes=[rb])
                    P.act(A_[:, tg * 512:(tg + 1) * 512], bk[:, :], AF.Exp, reads=[rb, rnb], writes=[rA],
                          scale=-1.0, bias=nb[:, d_:d_ + 1])
                P.act(A_[:], A_[:], AF.Ln, reads=[rA], writes=[rA], bias=1.0)
                for c in range(NT):
                    P.op("vector", lambda e, c=c: e.tensor_tensor_scan(out=B_[:, c * 128:(c + 1) * 128], data0=onesf[:, :],
                                                                      data1=A_[:, c * 128:(c + 1) * 128], initial=0.0,
                                                                      op0=ALU.mult, op1=ALU.add), [rA, rconst], [rB])
                P.cp(totc[:, :], B_[:, 127:T:128], reads=[rB], writes=[rtotc])
                if d_ == 0:
                    RC, rRC, FR, rFR, FRv = B_, rB, A_, rA, Av
                else:
                    P.tt(A_[:], A_[:], B_[:], ALU.subtract, reads=[rA, rB], writes=[rA], eng="gpsimd")
                    P.tt(Av, Av, totb, ALU.add, reads=[rA, rtotc], writes=[rA], eng="gpsimd")
                    RC, rRC, FR, rFR, FRv = A_, rA, B_, rB, Bv
                RCv = RC.rearrange("p (c j) -> p c j", j=128)
                P.act(Eb[:], RC[:], AF.Exp, reads=[rRC], writes=[rEb], scale=-1.0 / 16)
                P.tt(qd[d_][:], gqT[:], Eb[:], ALU.mult, reads=[rgq, rEb], writes=[rqd[d_]])
                P.act(Eb[:], RC[:], AF.Exp, reads=[rRC], writes=[rEb], scale=1.0 / 16)
                P.tt(kd[d_][:], gkT[:], Eb[:], ALU.mult, reads=[rgk, rEb], writes=[rkd[d_]])
                P.tt(FRv, RCv, totb, ALU.subtract, reads=[rRC, rtotc], writes=[rFR], eng="gpsimd")
                P.act(Eb[:], FR[:], AF.Exp, reads=[rFR], writes=[rEb], scale=1.0 / 16)
                P.tt(kdec[d_][:], gkT[:], Eb[:], ALU.mult, reads=[rgk, rEb], writes=[rkdec[d_]])
                P.act(elast[:, d_, :], totc[:, :], AF.Exp, reads=[rtotc], writes=[rel], scale=-1.0 / 16)
                for c in range(NT):
                    bk, rb = bank("s")
                    P.mm(bk[:, 0:128], kdec[d_][:, c * 128:(c + 1) * 128], ident[:], True, True, reads=[rkdec[d_], rid], writes=[rb])
                    P.cp(kdtok[d_][:, c, :], bk[:, 0:128], reads=[rb], writes=[rkdtok[d_]], eng="scalar")
                if os.environ.get("GSTOP") == "c":
                    P.mute = True
                order = list(range(NT)) if d_ == 0 else list(range(NT - 1, -1, -1))
                P.cp(S[:], s0[:, d_, :], reads=[rs0], writes=[rS])
                for n_, c in enumerate(order):
                    ccs = slice(c * 128, (c + 1) * 128)
                    if n_ > 0 and n_ % 2 == 0:
                        P.ts(S[:], S[:], carry[:, 0:1], None, ALU.mult, reads=[rS, rconst], writes=[rS])
                    P.cp(Sb[:], S[:], reads=[rS], writes=[rSb], eng="scalar")
                    am = attm[n_ % 2]; ram = rattm[n_ % 2]
                    GSK = os.environ.get("GSKIP", "")
                    _m0 = P.mute
                    bos = []
                    for hh in range(2):
                        hp = slice(hh * 64, hh * 64 + 64)
                        ba, rba = bank("s")
                        P.mm(ba[:, 0:128], kd[d_][hp, ccs], qd[d_][hp, ccs], True, True,
                             reads=[rkd[d_], rqd[d_]], writes=[rba])
                        P.tt(am[:, hh, :], ba[:, 0:128], glam[:, d_, :], ALU.mult, reads=[rba, rconst], writes=[ram])
                    for hh in range(2):
                        hp = slice(hh * 64, hh * 64 + 64)
                        bo, rbo = bank("a")
                        bos.append((bo, rbo))
                        P.mm(bo[:, 0:64], am[:, hh, :], v_b[:, c, hh * 64:(hh + 1) * 64], True, False,
                             reads=[ram, rvb], writes=[rbo])
                        P.mm(bo[:, 0:64], qd[d_][hp, ccs], Sb[hp, :], False, True,
                             reads=[rqd[d_], rSb], writes=[rbo])
                    if d_ == 0:
                        for hh in range(2):
                            P.cp(of[:, c, hh * 64:(hh + 1) * 64], bos[hh][0][:, 0:64], reads=[bos[hh][1]], writes=[rof], eng="scalar")
                    else:
                        for hh in range(2):
                            P.tt(ot[:, hh * 64:(hh + 1) * 64], of[:, c, hh * 64:(hh + 1) * 64], bos[hh][0][:, 0:64], ALU.add,
                                 reads=[rof, bos[hh][1]], writes=[rot])
                        P.tt(osq[:], ot[:], ot[:], ALU.mult, reads=[rot], writes=[rosq], eng="gpsimd")
                        P.op("vector", lambda e: e.tensor_reduce(out=small[:, 2:4], in_=osq[:].rearrange("p (h d) -> p h d", h=2),
                                                                 axis=AX.X, op=ALU.add), [rosq], [rsmall])
                        P.ts(small[:, 2:4], small[:, 2:4], 1.0 / 64, 1e-6, ALU.mult, ALU.add, reads=[rsmall], writes=[rsmall])
                        P.act(small[:, 2:4], small[:, 2:4], AF.Sqrt, reads=[rsmall], writes=[rsmall])
                        P.op("vector", lambda e: e.reciprocal(out=small[:, 2:4], in_=small[:, 2:4]), [rsmall], [rsmall])
                        P.tt(ot[:].rearrange("p (h d) -> p h d", h=2), ot[:].rearrange("p (h d) -> p h d", h=2),
                             small[:, 2:4].unsqueeze(2).to_broadcast([128, 2, 64]), ALU.mult, reads=[rot, rsmall], writes=[rot])
                        P.tt(ot[:], ot[:], gwb[:], ALU.mult, reads=[rot, rgwb], writes=[rot])
                        P.tt(ycat[:, c, :], ot[:], sg[:, c, :], ALU.mult, reads=[rot, rsg], writes=[rycat])
                    P.mute = _m0 or ("s" in GSK)
                    bd, rbd = bank("b")
                    P.mm(bd[:, 0:128], kdtok[d_][:, c, :], v_b[:, c, :], True, True, reads=[rkdtok[d_], rvb], writes=[rbd])
                    for hh in range(2):
                        hp = slice(hh * 64, hh * 64 + 64)
                        P.stt(S[hp, :], S[hp, :], elast[hp, d_, c:c + 1], bd[hp, hh * 64:(hh + 1) * 64], ALU.mult, ALU.add,
                              reads=[rS, rel, rbd], writes=[rS])
                    if n_ % 2 == 1:
                        P.cp(sst[:, d_, c // 2, :], S[:], reads=[rS], writes=[rsst], eng="gpsimd")
                    P.mute = _m0
            if os.environ.get("GSTOP") == "d":
                P.mute = True
            for c in range(NT):
                bk, rb = bank("s")
                P.mm(bk[:, 0:128], ycat[:, c, :], ident[:], True, True, reads=[rycat, rid], writes=[rb])
                P.cp(catG[:, 0, c * 128:(c + 1) * 128], bk[:, 0:128], reads=[rb], writes=[rcatG], eng="scalar")
            P.dma(sf_d[l, :, 2 * p:2 * p + 2].rearrange("g h k v -> (h k) g v"), sst[:, 0, :, :], reads=[rsst], q="gpsimd")
            P.dma(sbo_d[l, :, 2 * p:2 * p + 2].rearrange("g h k v -> (h k) g v"), sst[:, 1, :, :], reads=[rsst], q="gpsimd")
            P.barrier()
            off_save = ar.off
            ar.off = a_off
            outproj(catG, rcatG, p * 128, 1, l)
            ar.off = off_save
            P.barrier()

        P.mute = "P" not in phases
        P.barrier()
        norm_to_hT(1)
        P.barrier()
        ar.reset()
        pq = ar.bf16([128, 8, T]); rpq = Res()
        kk = ar.bf16([128, 8, 128]); rkk = Res()
        Kc = ar.bf16([128, 8, 4, 128]); rKc = Res()
        tauS = ar.f32([128, NT, 8]); rtau = Res()
        negcS = ar.f32([128, NT, 8]); rnegc = Res()
        pk_off = ar.off
        alloc_w()
        kst = ar.f32([128, 8, 128]); rkst = Res()
        kstb = ar.bf16([128, 8, 128]); rkstb = Res()
        for hq in range(8):
            wb, rwb = wcols(wq_d[l, :, hq * 128:(hq + 1) * 128], 128)
            proj_fm(wb, rwb, 0, 128, lambda tg, bk, rb, hq=hq: P.cp(pq[:, hq, tg * 512:(tg + 1) * 512], bk[:, :],
                                                                   reads=[rb], writes=[rpq], eng="scalar"))
        P.dma(kst[:, :, 0:64], k1_d[l].rearrange("h n d -> n h d"), writes=[rkst])
        P.dma(kst[:, :, 64:128], k2_d[l].rearrange("h n d -> n h d"), writes=[rkst])
        P.cp(kstb[:], kst[:], reads=[rkst], writes=[rkstb])
        for h in range(8):
            bk, rb = bank("s")
            P.mm(bk[:, 0:128], kstb[:, h, :], ident[:], True, True, reads=[rkstb, rid], writes=[rb])
            P.cp(kk[:, h, :], bk[:, 0:128], reads=[rb], writes=[rkk], eng="scalar")
        P.cp(Kc[64:128], kk[64:128, :, :].unsqueeze(2).to_broadcast([64, 8, 4, 128]), reads=[rkk], writes=[rKc], eng="gpsimd")
        ssb = ar.f32([128, 16, 128]); rssb = Res()
        tmpa = ar.f32([128, 128]); rtmpa = Res()
        tmpb = ar.f32([128, 128]); rtmpb = Res()
        v24 = ar.f32([128, 16, 24]); rv24 = Res()
        cand = ar.f32([128, 8, 576]); rcand = Res()
        cta = ar.f32([128, 576]); rcta = Res()
        ctb = ar.f32([128, 576]); rctb = Res()
        c24 = ar.f32([128, 8, 24]); rc24 = Res()
        e16 = ar.f32([128, 8, 16]); re16 = Res()
        zz = ar.f32([128, 8]); rzz = Res()
        for i in range(NT):
            tcs = slice(i * 128, (i + 1) * 128)
            for hg in range(2):
                b0, rb0 = bank("s")
                b1, rb1 = bank("s")
                for h4 in range(4):
                    h = hg * 4 + h4
                    P.mm(b0[:, h4 * 128:(h4 + 1) * 128], pq[0:64, h, tcs], kk[0:64, h, :], True, True, reads=[rpq, rkk], writes=[rb0])
                for h4 in range(4):
                    h = hg * 4 + h4
                    P.mm(b1[:, h4 * 128:(h4 + 1) * 128], pq[64:128, h, tcs], kk[64:128, h, :], True, True, reads=[rpq, rkk], writes=[rb1])
                sv = ssb[:, hg * 8:hg * 8 + 8, :].rearrange("p (h s) k -> p h s k", s=2)
                P.cp(sv[:, :, 0, :], b0[:, :].rearrange("p (a b) -> p a b", a=4), reads=[rb0], writes=[rssb], eng="scalar")
                P.cp(sv[:, :, 1, :], b1[:, :].rearrange("p (a b) -> p a b", a=4), reads=[rb1], writes=[rssb], eng="scalar")
            for r in range(16):
                P.op("vector", lambda e, r=r: e.max(out=v24[:, r, 0:8], in_=ssb[:, r, :]), [rssb], [rv24])
                P.op("vector", lambda e, r=r: e.match_replace(out=tmpa[:], in_to_replace=v24[:, r, 0:8], in_values=ssb[:, r, :], imm_value=-1e30),
                     [rssb, rv24], [rtmpa])
                P.op("vector", lambda e, r=r: e.max(out=v24[:, r, 8:16], in_=tmpa[:]), [rtmpa], [rv24])
                P.op("vector", lambda e, r=r: e.match_replace(out=tmpb[:], in_to_replace=v24[:, r, 8:16], in_values=tmpa[:], imm_value=-1e30),
                     [rtmpa, rv24], [rtmpb])
                P.op("vector", lambda e, r=r: e.max(out=v24[:, r, 16:24], in_=tmpb[:]), [rtmpb], [rv24])
            v4 = v24[:, :, :].rearrange("p (h s) k -> p h s k", s=2)
            P.tt(cand[:].rearrange("p h (a b) -> p h a b", a=24),
                 v4[:, :, 0, :].unsqueeze(3).to_broadcast([128, 8, 24, 24]),
                 v4[:, :, 1, :].unsqueeze(2).to_broadcast([128, 8, 24, 24]), ALU.add, reads=[rv24], writes=[rcand], eng="gpsimd")
            for h in range(8):
                P.op("vector", lambda e, h=h: e.max(out=c24[:, h, 0:8], in_=cand[:, h, :]), [rcand], [rc24])
                P.op("vector", lambda e, h=h: e.match_replace(out=cta[:], in_to_replace=c24[:, h, 0:8], in_values=cand[:, h, :], imm_value=-1e30),
                     [rcand, rc24], [rcta])
                P.op("vector", lambda e, h=h: e.max(out=c24[:, h, 8:16], in_=cta[:]), [rcta], [rc24])
                P.op("vector", lambda e, h=h: e.match_replace(out=ctb[:], in_to_replace=c24[:, h, 8:16], in_values=cta[:], imm_value=-1e30),
                     [rcta, rc24], [rctb])
                P.op("vector", lambda e, h=h: e.max(out=c24[:, h, 16:24], in_=ctb[:]), [rctb], [rc24])
            P.tt(tauS[:, i, :], c24[:, :, 15], c24[:, :, 16], ALU.add, reads=[rc24], writes=[rtau])
            P.ts(tauS[:, i, :], tauS[:, i, :], 0.5, None, ALU.mult, reads=[rtau], writes=[rtau])
            P.tt(e16[:], c24[:, :, 0:16], c24[:, :, 0:1].to_broadcast([128, 8, 16]), ALU.subtract, reads=[rc24], writes=[re16])
            P.act(e16[:], e16[:], AF.Exp, reads=[re16], writes=[re16])
            P.op("vector", lambda e: e.tensor_reduce(out=zz[:], in_=e16[:], axis=AX.X, op=ALU.add), [re16], [rzz])
            P.act(zz[:], zz[:], AF.Ln, reads=[rzz], writes=[rzz])
            P.tt(zz[:], zz[:], c24[:, :, 0], ALU.add, reads=[rzz, rc24], writes=[rzz])
            P.ts(negcS[:, i, :], zz[:], -1.0, None, ALU.mult, reads=[rzz], writes=[rnegc])
        P.barrier()
        ar.off = pk_off
        ust = ar.f32([128, 2, D // 2]); rust = [Res(), Res()]
        ub = ar.bf16([128, 4, D]); rub = Res()
        UT = ar.bf16([128, 8, 512]); rUT = Res()
        vst = ar.f32([128, 2, D // 2]); rvst = [Res(), Res()]
        Vb = ar.bf16([128, 4, D]); rVb = Res()
        gel = ar.bf16([128, 4, 512]); rgel = Res()
        Ebf = [ar.bf16([128, 512]), ar.bf16([128, 512])]; rEbf = [Res(), Res()]
        Gm = ar.bf16([128, 8, 512]); rGm = [Res() for _ in range(8)]
        AT = ar.bf16([128, 4, 128]); rAT = Res()
        ecnt = 0
        for quad in range(32):
            for k4 in range(4):
                r0 = (quad * 4 + k4) * 128
                for hf in range(2):
                    hs = slice(hf * 512, (hf + 1) * 512)
                    P.dma(ust[:, hf, :], pu_d[l, r0:r0 + 128, hs], writes=[rust[hf]])
                    P.cp(ub[:, k4, hs], ust[:, hf, :], reads=[rust[hf]], writes=[rub], eng="gpsimd")
                    P.dma(vst[:, hf, :], pv_d[l, r0:r0 + 128, hs], writes=[rvst[hf]])
                    P.tt(Vb[:, k4, hs], vst[:, hf, :], gb[:, 1, hs], ALU.mult, reads=[rvst[hf], rgb], writes=[rVb], eng="gpsimd")
            for c in range(8):
                bk, rb = bank("s")
                for k4 in range(4):
                    P.mm(bk[:, k4 * 128:(k4 + 1) * 128], ub[:, k4, c * 128:(c + 1) * 128], ident[:], True, True,
                         reads=[rub, rid], writes=[rb])
                P.cp(UT[:, c, :], bk[:, :], reads=[rb], writes=[rUT], eng="scalar")
            P.cp(Kc[0:64], kk[0:64, :, quad * 4:quad * 4 + 4].unsqueeze(3).to_broadcast([64, 8, 4, 128]),
                 reads=[rkk], writes=[rKc], eng="gpsimd")
            for tg in range(4):
                for k4 in range(4):
                    bk, rb = bank("s")
                    for c in range(8):
                        P.mm(bk[:, :], UT[:, c, k4 * 128:(k4 + 1) * 128], hT[:, c, tg * 512:(tg + 1) * 512], c == 0, c == 7,
                             reads=[rUT, rhT], writes=[rb])
                    P.act(gel[:, k4, :], bk[:, :], AF.Gelu, reads=[rb], writes=[rgel])
                for t4 in range(4):
                    i = tg * 4 + t4
                    tcs = slice(i * 128, (i + 1) * 128)
                    for h in range(8):
                        bk, rb = bank("s")
                        P.mm(bk[:, :], pq[:, h, tcs], Kc[:, h, :, :].rearrange("p a b -> p (a b)"), True, True,
                             reads=[rpq, rKc], writes=[rb])
                        ek = ecnt % 2
                        ecnt += 1
                        P.act(Ebf[ek][:], bk[:, :], AF.Exp, reads=[rb, rnegc], writes=[rEbf[ek]], bias=negcS[:, i, h:h + 1], scale=1.0)
                        P.stt(Gm[:, h, :], bk[:, :], tauS[:, i, h:h + 1], Ebf[ek][:], ALU.is_ge, ALU.mult,
                              reads=[rb, rtau, rEbf[ek]], writes=[rGm[h]])
                    bg, rbg = bank("a")
                    for k4 in range(4):
                        for h in range(8):
                            P.mm(bg[:, k4 * 128:(k4 + 1) * 128], Gm[:, h, k4 * 128:(k4 + 1) * 128], ident[:], h == 0, h == 7,
                                 reads=[rGm[h], rid], writes=[rbg])
                    P.tt(AT[:], gel[:, :, t4 * 128:(t4 + 1) * 128], bg[:, :].rearrange("p (a b) -> p a b", a=4), ALU.mult,
                         reads=[rgel, rbg], writes=[rAT])
                    for hf in range(2):
                        bo, rbo = bank("b")
                        for k4 in range(4):
                            P.mm(bo[:, :], AT[:, k4, :], Vb[:, k4, hf * 512:(hf + 1) * 512], k4 == 0, k4 == 3,
                                 reads=[rAT, rVb], writes=[rbo])
                        P.tt(x[:, i, hf * 512:(hf + 1) * 512], x[:, i, hf * 512:(hf + 1) * 512], bo[:, :], ALU.add,
                             reads=[rx[i], rbo], writes=[rx[i]])

    P.mute = False
    P.barrier()
    ar.reset()
    fwb = ar.f32([128, D]); rfwb = Res()
    sq = ar.f32([128, D]); rsq = Res()
    yo = [ar.f32([128, D]), ar.f32([128, D])]; ryo = [Res(), Res()]
    P.dma(fwb[:], fnw_d.partition_broadcast(128), writes=[rfwb])
    for i in range(NT):
        if raw:
            P.cp(yo[i % 2][:], x[:, i, :], reads=[rx[i]], writes=[ryo[i % 2]])
        else:
            rstd_of(x[:, i, :], D, small[:, 0:1], [rx[i]], sq, rsq)
            P.stt(yo[i % 2][:], x[:, i, :], small[:, 0:1], fwb[:], ALU.mult, ALU.mult, reads=[rx[i], rsmall, rfwb], writes=[ryo[i % 2]])
        P.dma(y_d[i * 128:(i + 1) * 128, :], yo[i % 2][:], reads=[ryo[i % 2]], q="gpsimd")
    P.emit()
    P.es.close()
    return nc, P


def _na_tables():
    a = np.arange(128)
    types = {}
    for i in range(NT):
        for kt, ty in na_blocks(i):
            types.setdefault(ty, (i, kt))
    DR = np.zeros((26, 128, 128), np.int64); DC = np.zeros((26, 128, 128), np.int64)
    msk_s = np.zeros((NTYPE, 128, 128), np.float32); msk_p = np.zeros((NTYPE, 128, 128), np.float32)
    for ty, (i, kt) in types.items():
        krow = (2 * kt + a // 64)[:, None]; kcol = (a % 64)[:, None]
        qrow = (2 * i + a // 64)[None, :]; qcol = (a % 64)[None, :]
        rs = np.clip(qrow - 4, 0, 24)
        cs = np.clip(qcol - 8, 0, 48)
        valid = (krow >= rs) & (krow < rs + 8) & (kcol >= cs) & (kcol < cs + 16)
        DR[ty] = np.clip(krow - qrow, -7, 7) + 7
        DC[ty] = np.clip(kcol - qcol, -15, 15) + 15
        msk_s[ty] = np.where(valid, 0.0, -1e30)
        msk_p[ty] = 0.0 if (kt // 2 == i // 2) else -1e30
    msk_s[26] = 0.0
    msk_p[26] = -1e30
    return DR, DC, msk_s, msk_p


_CACHE = {}


def kernel(x_prompt, x_sample, cache_na_k, cache_na_v, state_gla_fwd, state_gla_bwd, c, c_ctx,
           w_ada, b_ada, norm1_w, norm2_w, w_in, w_af, b_af, w_ab, b_ab, gla_norm_w,
           conv_w, conv_b, na_rpb, w_out, peer_wq, peer_k1, peer_k2, peer_u, peer_v, final_norm_w):
    f = lambda a: np.ascontiguousarray(np.asarray(a), dtype=np.float32)
    x_prompt, x_sample = f(x_prompt), f(x_sample)
    DR, DC, msk_s, msk_p = _na_tables()
    rpb = f(na_rpb)
    nab_s = np.ascontiguousarray(rpb[:, :, DR, DC])
    nab_p = np.zeros_like(nab_s)
    a = np.arange(128)
    glam = np.stack([(a[:, None] <= a[None, :]), (a[:, None] >= a[None, :])]).astype(np.float32)
    shared = {
        "ident": np.eye(128).astype(NPBF), "glamask": glam,
        "w_ada": f(w_ada), "b_ada": f(b_ada), "norm1_w": f(norm1_w), "norm2_w": f(norm2_w), "w_in": f(w_in),
        "w_af": f(w_af), "b_af": f(b_af), "w_ab": f(w_ab), "b_ab": f(b_ab), "gla_norm_w": f(gla_norm_w),
        "conv_w": f(conv_w), "conv_b": f(conv_b), "w_out": f(w_out), "peer_wq": f(peer_wq),
        "peer_k1": f(peer_k1), "peer_k2": f(peer_k2), "peer_u": f(peer_u), "peer_v": f(peer_v),
        "final_norm_w": f(final_norm_w),
    }
    zk = np.zeros((DEPTH, 6, 256, 64), np.float32)
    zs = np.zeros((DEPTH, 6, 64, 64), np.float32)
    in_maps = []
    for core in range(8):
        m = dict(shared)
        if core < 4:
            b = core
            m.update(x=x_sample[b], cvec=f(c)[b], kctx=f(cache_na_k)[b], vctx=f(cache_na_v)[b],
                     s0f=f(state_gla_fwd)[b], s0b=f(state_gla_bwd)[b], carry=np.ones((128, 1), np.float32),
                     nabias=nab_s, namask=msk_s)
        else:
            b0 = (core - 4) * 4
            xp = x_prompt[b0:b0 + 4].reshape(1024, D)
            m.update(x=np.ascontiguousarray(np.concatenate([xp, xp], 0)), cvec=f(c_ctx), kctx=zk, vctx=zk,
                     s0f=zs, s0b=zs, carry=np.zeros((128, 1), np.float32), nabias=nab_p, namask=msk_p)
        in_maps.append(m)
    if _CACHE.get("only_maps"):
        return in_maps
    if "nc" not in _CACHE:
        _CACHE["nc"] = build()
    nc, P = _CACHE["nc"]
    res = run_bass_kernel_spmd(nc, in_maps, core_ids=list(range(8)))
    R = res.results
    y_sample = np.stack([R[b]["y"] for b in range(4)], 0)
    y_prompt = np.concatenate([R[4 + j]["y"][:1024].reshape(4, 256, D) for j in range(4)], 0)

    def kvout(name):
        o = np.concatenate([R[4 + j][name][:, :1024].reshape(DEPTH, 4, 256, 6, 64) for j in range(4)], 1)
        return np.ascontiguousarray(o.transpose(1, 0, 3, 2, 4))

    def stout(name):
        o = np.concatenate([R[4 + j][name][:, :4] for j in range(4)], 1)
        return np.ascontiguousarray(o.transpose(1, 0, 2, 3, 4))
    return (y_prompt.astype(np.float32), y_sample.astype(np.float32), kvout("nk"), kvout("nv"), stout("sf"), stout("sbo"))
```

```python
import os
import numpy as np
from contextlib import ExitStack
import ml_dtypes
import concourse.bass as bass
import concourse.mybir as mybir
from concourse.bass_utils import run_bass_kernel_spmd


F32 = mybir.dt.float32
BF16 = mybir.dt.bfloat16
AF = mybir.ActivationFunctionType
ALU = mybir.AluOpType
AX = mybir.AxisListType
NPBF = ml_dtypes.bfloat16


class Res:
    __slots__ = ("name", "w", "rd")

    def __init__(self, name="r"):
        self.name = name
        self.w = None
        self.rd = []


class Op:
    __slots__ = ("eng", "fn", "dma", "waits", "signal", "sem", "val", "prevsame")

    def __init__(self, eng, fn, dma):
        self.eng = eng
        self.fn = fn
        self.dma = dma
        self.waits = []
        self.signal = False
        self.sem = None
        self.val = None
        self.prevsame = None


ENGINES = ("tensor", "vector", "scalar", "gpsimd", "sync")
NDMASEM = 16
SEMCH = 30000


class Prog:
    def __init__(self, nc):
        self.nc = nc
        self.streams = {e: [] for e in ENGINES}
        self.es = ExitStack()
        self.nops = 0
        self.pending = {e: [] for e in ENGINES}
        self.out_dma = []
        self.mute = False

    def sb(self, name, shape, dt):
        return self.es.enter_context(self.nc.sbuf_tensor("sb_" + name, list(shape), dt))

    def ps(self, name, shape, dt=F32):
        return self.es.enter_context(self.nc.psum_tensor(name, list(shape), dt))

    def op(self, eng, fn, reads=(), writes=(), dma=False):
        o = Op(eng, fn, dma)
        if self.mute:
            return o
        deps = []
        for r in reads:
            if r.w is not None:
                deps.append(r.w)
        for w in writes:
            if w.w is not None:
                deps.append(w.w)
            deps.extend(w.rd)
        if self.pending[eng]:
            deps.extend(self.pending[eng])
            self.pending[eng] = []
        seen = set()
        for d in deps:
            if d is o or id(d) in seen:
                continue
            seen.add(id(d))
            if d.eng == "tensor" and eng == "tensor" and not d.dma and not dma:
                continue
            d.signal = True
            o.waits.append(d)
        for r in reads:
            r.rd.append(o)
        for w in writes:
            w.w = o
            w.rd = []
        self.streams[eng].append(o)
        self.nops += 1
        if dma:
            self.out_dma.append(o)
        return o

    def barrier(self):
        lasts = []
        for e in ENGINES:
            st = self.streams[e]
            last_c = None
            for o in reversed(st):
                if not o.dma:
                    last_c = o
                    break
            if last_c is not None:
                lasts.append(last_c)
        lasts.extend(self.out_dma)
        self.out_dma = []
        for e in ENGINES:
            self.pending[e] = list(lasts)

    def dma(self, out, in_, reads=(), writes=(), q="sync", **kw):
        return self.op(q, lambda e: e.dma_start(out=out, in_=in_, **kw), reads, writes, dma=True)

    def mm(self, out, lhsT, rhs, start, stop, reads=(), writes=()):
        return self.op("tensor", lambda e: e.matmul(out, lhsT, rhs, start=start, stop=stop), reads, writes)

    def act(self, out, in_, func, reads=(), writes=(), eng="scalar", **kw):
        return self.op(eng, lambda e: e.activation(out=out, in_=in_, func=func, **kw), reads, writes)

    def tt(self, out, in0, in1, op, reads=(), writes=(), eng="vector"):
        return self.op(eng, lambda e: e.tensor_tensor(out=out, in0=in0, in1=in1, op=op), reads, writes)

    def ts(self, out, in0, s1, s2, op0, op1=None, reads=(), writes=(), eng="vector"):
        if op1 is None:
            return self.op(eng, lambda e: e.tensor_scalar(out=out, in0=in0, scalar1=s1, scalar2=None, op0=op0), reads, writes)
        return self.op(eng, lambda e: e.tensor_scalar(out=out, in0=in0, scalar1=s1, scalar2=s2, op0=op0, op1=op1), reads, writes)

    def stt(self, out, in0, scalar, in1, op0, op1, reads=(), writes=(), eng="vector"):
        return self.op(eng, lambda e: e.scalar_tensor_tensor(out=out, in0=in0, scalar=scalar, in1=in1, op0=op0, op1=op1), reads, writes)

    def cp(self, out, in_, reads=(), writes=(), eng="vector"):
        if eng == "scalar":
            return self.op(eng, lambda e: e.copy(out=out, in_=in_), reads, writes)
        return self.op(eng, lambda e: e.tensor_copy(out=out, in_=in_), reads, writes)

    def memset(self, ap, val, writes=(), eng="vector"):
        return self.op(eng, lambda e: e.memset(ap, val), (), writes)

    def emit(self):
        nc = self.nc
        es = self.es
        csems = {}
        dsems = {}
        for e in ENGINES:
            st = self.streams[e]
            nsig = sum(1 for o in st if (o.signal and not o.dma))
            csems[e] = [es.enter_context(nc.semaphore(f"c_{e}_{i}")) for i in range(nsig // SEMCH + 1)]
            ndma = sum(1 for o in st if o.dma)
            dsems[e] = [es.enter_context(nc.semaphore(f"d_{e}_{i}")) for i in range(min(NDMASEM, ndma))]
            cnt = 0
            dcnt = 0
            hist = []
            for o in st:
                if o.dma:
                    k = dcnt
                    o.sem = dsems[e][k % NDMASEM]
                    o.val = 16 * (k // NDMASEM + 1)
                    if k >= NDMASEM:
                        o.prevsame = hist[k - NDMASEM]
                    hist.append(o)
                    dcnt += 1
                elif o.signal:
                    o.sem = csems[e][cnt // SEMCH]
                    o.val = cnt % SEMCH + 1
                    cnt += 1
        finals = []
        for e in ENGINES:
            last = {}
            for o in self.streams[e]:
                if o.dma:
                    last[id(o.sem)] = (o.sem, o.val)
            finals.extend(last.values())
        streams = self.streams
        block = es.enter_context(nc.Block())

        def make(e):
            def run(engh):
                waited = {}
                for o in streams[e]:
                    need = {}
                    for d in o.waits:
                        k = id(d.sem)
                        if k not in need or need[k][1] < d.val:
                            need[k] = (d.sem, d.val)
                    if o.prevsame is not None:
                        k = id(o.sem)
                        v = o.prevsame.val
                        if k not in need or need[k][1] < v:
                            need[k] = (o.sem, v)
                    for k, (sem, val) in need.items():
                        if waited.get(k, 0) < val:
                            engh.wait_ge(sem, val)
                            waited[k] = val
                    ins = o.fn(engh)
                    if o.dma:
                        ins.then_inc(o.sem, 16)
                    elif o.signal:
                        ins.then_inc(o.sem, 1)
                if e == "sync":
                    for sem, val in finals:
                        engh.wait_ge(sem, val)
            return run

        for e in ENGINES:
            getattr(block, e)(make(e))


D = 1024
NT = 16
T = 2048
DEPTH = 4
INW = 3488
QA0, KA0, VA0, GA0, LRF0, LRB0 = 0, 384, 768, 1152, 1536, 1552
CH0, CB0, CC0 = 1568, 1824, 2080
QN0, KN0, VN0 = 2336, 2720, 3104
NTYPE = 27
ARENA_W = 25856


def na_blocks(i):
    if 2 <= i <= 13:
        return [(i - 2 + m, (i % 2) * 5 + m) for m in range(5)]
    e = {0: 0, 1: 1, 14: 2, 15: 3}[i]
    base = 0 if i < 2 else 12
    return [(base + m, 10 + e * 4 + m) for m in range(4)]


class Arena:
    def __init__(self, P):
        self.t = P.sb("arena", [128, ARENA_W], F32)
        self.off = 0

    def reset(self):
        self.off = 0

    def f32(self, shape):
        n = int(np.prod(shape[1:]))
        v = self.t[:, self.off:self.off + n]
        self.off += n
        assert self.off <= ARENA_W, self.off
        return self._shape(v, shape)

    def bf16(self, shape):
        n = int(np.prod(shape[1:]))
        assert n % 2 == 0
        v = self.t[:, self.off:self.off + n // 2].bitcast(BF16)
        self.off += n // 2
        assert self.off <= ARENA_W, self.off
        return self._shape(v, shape)

    @staticmethod
    def _shape(v, shape):
        if len(shape) == 2:
            return v
        if len(shape) == 3:
            return v.rearrange("p (a b) -> p a b", a=shape[1], b=shape[2])
        if len(shape) == 4:
            return v.rearrange("p (a b c) -> p a b c", a=shape[1], b=shape[2], c=shape[3])
        raise ValueError(shape)


def build(nlayers=DEPTH, phases="CNGP", raw=False):
    nc = bass.Bass("TRN2", target_bir_lowering=False)

    def din(name, shape, dt=F32):
        return nc.dram_tensor(name, list(shape), dt, kind="ExternalInput").ap()

    def dout(name, shape):
        return nc.dram_tensor(name, list(shape), F32, kind="ExternalOutput").ap()

    x_d = din("x", [T, D]); cvec_d = din("cvec", [D])
    kctx_d = din("kctx", [DEPTH, 6, 256, 64]); vctx_d = din("vctx", [DEPTH, 6, 256, 64])
    s0f_d = din("s0f", [DEPTH, 6, 64, 64]); s0b_d = din("s0b", [DEPTH, 6, 64, 64])
    carry_d = din("carry", [128, 1])
    nabias_d = din("nabias", [DEPTH, 6, 26, 128, 128]); namask_d = din("namask", [NTYPE, 128, 128])
    ident_d = din("ident", [128, 128], BF16)
    glam_d = din("glamask", [2, 128, 128])
    w_ada_d = din("w_ada", [DEPTH, D, 6 * D]); b_ada_d = din("b_ada", [DEPTH, 6 * D])
    n1_d = din("norm1_w", [DEPTH, D]); n2_d = din("norm2_w", [DEPTH, D])
    w_in_d = din("w_in", [DEPTH, D, INW])
    waf_d = din("w_af", [DEPTH, 16, 384]); baf_d = din("b_af", [DEPTH, 384])
    wab_d = din("w_ab", [DEPTH, 16, 384]); bab_d = din("b_ab", [DEPTH, 384])
    gnw_d = din("gla_norm_w", [DEPTH, 384])
    cw_d = din("conv_w", [DEPTH, 3, 256]); cb_d = din("conv_b", [DEPTH, 256])
    wout_d = din("w_out", [DEPTH, D, D]); wq_d = din("peer_wq", [DEPTH, D, D])
    k1_d = din("peer_k1", [DEPTH, 8, 128, 64]); k2_d = din("peer_k2", [DEPTH, 8, 128, 64])
    pu_d = din("peer_u", [DEPTH, 16384, D]); pv_d = din("peer_v", [DEPTH, 16384, D])
    fnw_d = din("final_norm_w", [D])
    y_d = dout("y", [T, D]); nk_d = dout("nk", [DEPTH, T, 384]); nv_d = dout("nv", [DEPTH, T, 384])
    sf_d = dout("sf", [DEPTH, 8, 6, 64, 64]); sbo_d = dout("sbo", [DEPTH, 8, 6, 64, 64])

    P = Prog(nc)
    NSC = dict(allow_slow_non_contiguous=True)
    x = P.sb("x", [128, NT, D], F32); rx = [Res() for _ in range(NT)]
    hT = P.sb("hT", [128, 8, T], BF16); rhT = Res()
    ident = P.sb("ident", [128, 128], BF16); rid = Res()
    identf = P.sb("identf", [128, 128], F32)
    onesf = P.sb("onesf", [128, 128], F32)
    glam = P.sb("glam", [128, 2, 128], F32)
    carry = P.sb("carry", [128, 1], F32)
    carm1 = P.sb("carm1", [128, 1], F32)
    scol = P.sb("scol", [128, 8], F32)
    modcol = P.sb("modcol", [128, 4, 8], F32); rmod = Res()
    acol = P.sb("acol", [128, 2, 8], F32)
    nwcol = P.sb("nwcol", [128, 2, 8], F32)
    gb = P.sb("gb", [128, 2, D], F32); rgb = Res()
    small = P.sb("small", [128, 64], F32); rsmall = Res()
    rconst = Res()
    ar = Arena(P)
    banks = [P.ps(f"bank{i}", [128, 512], F32) for i in range(8)]
    rbank = [Res() for _ in range(8)]
    rr = {"s": 0, "a": 0, "b": 0}

    def bank(pool):
        if pool == "s":
            i = rr["s"] % 3
        elif pool == "a":
            i = 3 + rr["a"] % 3
        else:
            i = 6 + rr["b"] % 2
        rr[pool] += 1
        return banks[i], rbank[i]

    for i in range(NT):
        P.dma(x[:, i, :], x_d[i * 128:(i + 1) * 128, :], writes=[rx[i]])
    P.dma(ident[:], ident_d[:, :], writes=[rid])
    P.dma(glam[:], glam_d.rearrange("a s t -> s a t"), writes=[rconst])
    P.dma(carry[:], carry_d[:, :], writes=[rconst])
    P.dma(scol[:], cvec_d.rearrange("(k p) -> p k", p=128), writes=[rconst], **NSC)
    P.cp(identf[:], ident[:], reads=[rid], writes=[rconst])
    P.memset(onesf[:], 1.0, writes=[rconst])
    P.ts(carm1[:], carry[:], -1.0, None, ALU.add, reads=[rconst], writes=[rconst])
    P.act(scol[:], scol[:], AF.Silu, reads=[rconst], writes=[rconst])

    def rstd_of(src_ap, n, out_ap, rsrc, scratch, rscratch):
        P.act(scratch, src_ap, AF.Square, reads=rsrc, writes=[rscratch, rsmall], accum_out=out_ap)
        P.ts(out_ap, out_ap, 1.0 / n, 1e-6, ALU.mult, ALU.add, reads=[rsmall], writes=[rsmall])
        P.act(out_ap, out_ap, AF.Sqrt, reads=[rsmall], writes=[rsmall])
        P.op("vector", lambda e: e.reciprocal(out=out_ap, in_=out_ap), [rsmall], [rsmall])

    def norm_to_hT(which):
        ar.reset()
        sq = ar.f32([128, D]); rsq = Res()
        xn = ar.bf16([128, 2, D]); rxn = [Res(), Res()]
        for i in range(NT):
            rstd_of(x[:, i, :], D, small[:, 0:1], [rx[i]], sq, rsq)
            P.ts(xn[:, i % 2, :], x[:, i, :], small[:, 0:1], None, ALU.mult, reads=[rx[i], rsmall], writes=[rxn[i % 2]])
            for hf in range(2):
                bk, rb = bank("s")
                for c4 in range(4):
                    c = hf * 4 + c4
                    P.mm(bk[:, c4 * 128:(c4 + 1) * 128], xn[:, i % 2, c * 128:(c + 1) * 128], ident[:], True, True,
                         reads=[rxn[i % 2], rid], writes=[rb])
                for c4 in range(4):
                    c = hf * 4 + c4
                    P.ts(hT[:, c, i * 128:(i + 1) * 128], bk[:, c4 * 128:(c4 + 1) * 128], acol[:, which, c:c + 1],
                         modcol[:, 2 * which, c:c + 1], ALU.mult, ALU.add, reads=[rb, rmod], writes=[rhT],
                         eng=("vector" if c4 % 2 == 0 else "gpsimd") if False else "vector")

    wst = [None, None]; rwst = [Res(), Res()]; wbf = [None, None]; rwbf = [Res(), Res()]
    wcnt = [0]

    def alloc_w():
        for k in range(2):
            wst[k] = ar.f32([128, 8, 128])
            wbf[k] = ar.bf16([128, 8, 128])

    def wcols(dram_ap, n, dst=None, rdst=None, dcol=0):
        k = wcnt[0] % 2
        wcnt[0] += 1
        P.dma(wst[k][:, :, 0:n], dram_ap.rearrange("(c p) n -> p c n", p=128), writes=[rwst[k]])
        if dst is None:
            dst, rdst, dcol = wbf[k], rwbf[k], 0
        P.cp(dst[:, :, dcol:dcol + n], wst[k][:, :, 0:n], reads=[rwst[k]], writes=[rdst], eng="gpsimd")
        return dst, rdst

    def proj_fm(wb, rwb, wcol0, m, evac):
        for tg in range(4):
            bk, rb = bank("s")
            for c in range(8):
                P.mm(bk[0:m, :], wb[:, c, wcol0:wcol0 + m], hT[:, c, tg * 512:(tg + 1) * 512], c == 0, c == 7,
                     reads=[rwb, rhT], writes=[rb])
            evac(tg, bk, rb)

    def outproj(cat, rcat, row0, nch, l):
        wo_st = ar.f32([128, D]); rwo_st = Res()
        wob = ar.bf16([128, nch, D]); rwob = Res()
        for j in range(nch):
            P.dma(wo_st[:], wout_d[l, row0 + j * 128: row0 + (j + 1) * 128, :], writes=[rwo_st])
            P.tt(wob[:, j, :], wo_st[:], gb[:, 0, :], ALU.mult, reads=[rwo_st, rgb], writes=[rwob], eng="gpsimd")
        for i in range(NT):
            for hf in range(2):
                bk, rb = bank("b")
                for j in range(nch):
                    P.mm(bk[:, :], cat[:, j, i * 128:(i + 1) * 128], wob[:, j, hf * 512:(hf + 1) * 512], j == 0, j == nch - 1,
                         reads=[rcat, rwob], writes=[rb])
                P.tt(x[:, i, hf * 512:(hf + 1) * 512], x[:, i, hf * 512:(hf + 1) * 512], bk[:, :], ALU.add,
                     reads=[rx[i], rb], writes=[rx[i]])

    for l in range(nlayers):
        P.barrier()
        ar.reset()
        wa = [ar.f32([128, 8, 512]), ar.f32([128, 8, 512])]; rwa = [Res(), Res()]
        row = ar.f32([1, 512]); rrow = Res()
        brow = ar.f32([1, 512]); rbrow = Res()
        P.dma(nwcol[:, 0, :], n1_d[l].rearrange("(k p) -> p k", p=128), writes=[rmod], **NSC)
        P.dma(nwcol[:, 1, :], n2_d[l].rearrange("(k p) -> p k", p=128), writes=[rmod], **NSC)
        for g in range(12):
            k = g % 2
            P.dma(wa[k][:], w_ada_d[l, :, g * 512:(g + 1) * 512].rearrange("(c p) n -> p c n", p=128), writes=[rwa[k]])
            P.dma(brow[0:1, :], b_ada_d[l:l + 1, g * 512:(g + 1) * 512], writes=[rbrow])
            bk, rb = bank("s")
            for c in range(8):
                P.mm(bk[0:1, :], scol[:, c:c + 1], wa[k][:, c, :], c == 0, c == 7, reads=[rconst, rwa[k]], writes=[rb])
            P.tt(row[0:1, :], bk[0:1, :], brow[0:1, :], ALU.add, reads=[rb, rbrow], writes=[rrow])
            vec = g // 2
            if vec in (0, 1, 3, 4):
                mi = {0: 0, 1: 1, 3: 2, 4: 3}[vec]
                bk2, rb2 = bank("s")
                for j in range(4):
                    P.mm(bk2[:, j:j + 1], row[0:1, j * 128:(j + 1) * 128], onesf[0:1, 0:1], True, True,
                         reads=[rrow, rconst], writes=[rb2])
                P.cp(modcol[:, mi, k * 4:(k + 1) * 4], bk2[:, 0:4], reads=[rb2], writes=[rmod])
            else:
                gi = 0 if vec == 2 else 1
                bk2, rb2 = bank("s")
                P.mm(bk2[:, :], onesf[0:1, 0:128], row[0:1, :], True, True, reads=[rrow, rconst], writes=[rb2])
                P.cp(gb[:, gi, k * 512:(k + 1) * 512], bk2[:, :], reads=[rb2], writes=[rgb])
        for w_ in range(2):
            P.stt(acol[:, w_, :], modcol[:, 2 * w_ + 1, :], 1.0, nwcol[:, w_, :], ALU.add, ALU.mult,
                  reads=[rmod], writes=[rmod])
        P.barrier()
        norm_to_hT(0)

        P.mute = "C" not in phases
        P.barrier()
        ar.reset()
        alloc_w()
        zc = ar.f32([128, 3, T]); rzc = Res()
        u = ar.f32([128, T]); ru = Res()
        yv = ar.f32([128, T]); ryv = Res()
        catC = ar.bf16([128, 2, T]); rcatC = Res()
        cwc = ar.f32([128, 2, 4]); rcwc = Res()
        wn = ar.f32([128, 2, 2]); rwn = Res()
        for j in range(2):
            for k_ in range(3):
                P.dma(cwc[:, j, k_:k_ + 1], cw_d[l, k_, j * 128:(j + 1) * 128].rearrange("(p o) -> p o", o=1), writes=[rcwc], **NSC)
            P.dma(cwc[:, j, 3:4], cb_d[l, j * 128:(j + 1) * 128].rearrange("(p o) -> p o", o=1), writes=[rcwc], **NSC)
        for j in range(2):
            P.ts(wn[:, j, 0:1], cwc[:, j, 0:1], carm1[:, 0:1], None, ALU.mult, reads=[rcwc, rconst], writes=[rwn])
            P.ts(wn[:, j, 1:2], cwc[:, j, 2:3], carm1[:, 0:1], None, ALU.mult, reads=[rcwc, rconst], writes=[rwn])
        for j in range(2):
            for wi, c0 in enumerate((CH0, CB0, CC0)):
                wb, rwb = wcols(w_in_d[l, :, c0 + j * 128:c0 + (j + 1) * 128], 128)
                proj_fm(wb, rwb, 0, 128, lambda tg, bk, rb, wi=wi: P.cp(zc[:, wi, tg * 512:(tg + 1) * 512], bk[:, :],
                                                                      reads=[rb], writes=[rzc], eng="scalar"))
            P.tt(u[:], zc[:, 2, :], zc[:, 0, :], ALU.mult, reads=[rzc], writes=[ru], eng="gpsimd")
            P.ts(yv[:], u[:], cwc[:, j, 1:2], None, ALU.mult, reads=[ru, rcwc], writes=[ryv])
            P.stt(yv[:, 1:T], u[:, 0:T - 1], cwc[:, j, 0:1], yv[:, 1:T], ALU.mult, ALU.add, reads=[ru, rcwc, ryv], writes=[ryv])
            P.stt(yv[:, 0:T - 1], u[:, 1:T], cwc[:, j, 2:3], yv[:, 0:T - 1], ALU.mult, ALU.add, reads=[ru, rcwc, ryv], writes=[ryv])
            P.stt(yv[:, 256:T:256], u[:, 255:T - 1:256], wn[:, j, 0:1], yv[:, 256:T:256], ALU.mult, ALU.add,
                  reads=[ru, rwn, ryv], writes=[ryv])
            P.stt(yv[:, 255:T - 1:256], u[:, 256:T:256], wn[:, j, 1:2], yv[:, 255:T - 1:256], ALU.mult, ALU.add,
                  reads=[ru, rwn, ryv], writes=[ryv])
            P.stt(catC[:, j, :], yv[:], cwc[:, j, 3:4], zc[:, 1, :], ALU.add, ALU.mult, reads=[ryv, rcwc, rzc], writes=[rcatC])
        outproj(catC, rcatC, 384, 2, l)

        P.mute = "N" not in phases
        P.barrier()
        ar.reset()
        alloc_w()
        namask = ar.f32([128, NTYPE, 128]); rnamask = Res()
        P.dma(namask[:], namask_d.rearrange("t k q -> k t q"), writes=[rnamask])
        nab = ar.f32([128, 26, 128]); rnab = Res()
        BM = ar.bf16([128, 2, NTYPE, 128]); rBM = Res()
        qT = ar.bf16([128, T]); rqT = Res()
        kT = ar.bf16([128, T]); rkT = Res()
        wkv = ar.bf16([128, 8, 256]); rwkv = Res()
        kvst = [ar.f32([128, 256]), ar.f32([128, 256])]; rkvst = [Res(), Res()]
        vaug = ar.bf16([128, NT, 2, 66]); rvaug = Res()
        cst = ar.f32([128, 2, 2, 64]); rcst = Res()
        kcb = ar.bf16([128, 2, 128]); rkcb = Res()
        kcT = ar.bf16([128, 256]); rkcT = Res()
        vca = ar.bf16([128, 2, 2, 66]); rvca = Res()
        pT = [ar.bf16([128, 128]) for _ in range(3)]; rpT = [Res() for _ in range(3)]
        opair = [ar.bf16([128, 128]) for _ in range(2)]; ropair = [Res(), Res()]
        catN = ar.bf16([128, 1, T]); rcatN = Res()
        P.memset(vaug[:, :, :, 64:66], 1.0, writes=[rvaug])
        P.memset(vca[:, :, :, 64:66], 1.0, writes=[rvca])
        pcnt = 0
        for p in range(3):
            wb, rwb = wcols(w_in_d[l, :, QN0 + p * 128:QN0 + (p + 1) * 128], 128)
            proj_fm(wb, rwb, 0, 128, lambda tg, bk, rb: P.act(qT[:, tg * 512:(tg + 1) * 512], bk[:, :], AF.Copy,
                                                             reads=[rb], writes=[rqT], scale=0.125))
            wb, rwb = wcols(w_in_d[l, :, KN0 + p * 128:KN0 + (p + 1) * 128], 128)
            proj_fm(wb, rwb, 0, 128, lambda tg, bk, rb: P.cp(kT[:, tg * 512:(tg + 1) * 512], bk[:, :],
                                                            reads=[rb], writes=[rkT], eng="scalar"))
            wcols(w_in_d[l, :, KN0 + p * 128:KN0 + (p + 1) * 128], 128, wkv, rwkv, 0)
            wcols(w_in_d[l, :, VN0 + p * 128:VN0 + (p + 1) * 128], 128, wkv, rwkv, 128)
            for i in range(NT):
                bk, rb = bank("s")
                for c in range(8):
                    P.mm(bk[:, 0:256], hT[:, c, i * 128:(i + 1) * 128], wkv[:, c, :], c == 0, c == 7,
                         reads=[rhT, rwkv], writes=[rb])
                kk_ = i % 2
                P.cp(kvst[kk_][:], bk[:, 0:256], reads=[rb], writes=[rkvst[kk_]], eng="scalar")
                P.dma(nk_d[l, i * 128:(i + 1) * 128, p * 128:(p + 1) * 128], kvst[kk_][:, 0:128], reads=[rkvst[kk_]], q="gpsimd")
                P.dma(nv_d[l, i * 128:(i + 1) * 128, p * 128:(p + 1) * 128], kvst[kk_][:, 128:256], reads=[rkvst[kk_]], q="gpsimd")
                P.cp(vaug[:, i, :, 0:64], kvst[kk_][:, 128:256].rearrange("p (h d) -> p h d", h=2),
                     reads=[rkvst[kk_]], writes=[rvaug])
            for kt in range(2):
                P.dma(cst[:, kt, :, :], kctx_d[l, 2 * p:2 * p + 2, kt * 128:(kt + 1) * 128, :].rearrange("h q d -> q h d"), writes=[rcst])
            P.cp(kcb[:].rearrange("p kt (h d) -> p kt h d", h=2), cst[:], reads=[rcst], writes=[rkcb])
            for kt in range(2):
                bk, rb = bank("s")
                P.mm(bk[:, 0:128], kcb[:, kt, :], ident[:], True, True, reads=[rkcb, rid], writes=[rb])
                P.cp(kcT[:, kt * 128:(kt + 1) * 128], bk[:, 0:128], reads=[rb], writes=[rkcT], eng="scalar")
            for kt in range(2):
                P.dma(cst[:, kt, :, :], vctx_d[l, 2 * p:2 * p + 2, kt * 128:(kt + 1) * 128, :].rearrange("h q d -> q h d"), writes=[rcst])
            P.cp(vca[:, :, :, 0:64], cst[:], reads=[rcst], writes=[rvca])
            for hh in range(2):
                P.dma(nab[:], nabias_d[l, 2 * p + hh].rearrange("t k q -> k t q"), writes=[rnab])
                P.tt(BM[:, hh, 0:26, :], nab[:], namask[:, 0:26, :], ALU.add, reads=[rnab, rnamask], writes=[rBM])
                P.cp(BM[:, hh, 26, :], namask[:, 26, :], reads=[rnamask], writes=[rBM])
            for i in range(NT):
                for hh in range(2):
                    hp = slice(hh * 64, hh * 64 + 64)
                    blocks = [("l", kt, ty) for kt, ty in na_blocks(i)] + [("c", 0, 26), ("c", 1, 26)]
                    bo, rbo = bank("a")
                    for bi, (src, kt, ty) in enumerate(blocks):
                        bs, rbs = bank("s")
                        ksrc, rks = (kT, rkT) if src == "l" else (kcT, rkcT)
                        P.mm(bs[:, 0:128], ksrc[hp, kt * 128:(kt + 1) * 128], qT[hp, i * 128:(i + 1) * 128], True, False,
                             reads=[rks, rqT], writes=[rbs])
                        P.mm(bs[:, 0:128], ident[:], BM[:, hh, ty, :], False, True, reads=[rid, rBM], writes=[rbs])
                        pk = pcnt % 3
                        pcnt += 1
                        P.act(pT[pk][:], bs[:, 0:128], AF.Exp, reads=[rbs], writes=[rpT[pk]])
                        vsrc, rvs = (vaug[:, kt, hh, 0:65], rvaug) if src == "l" else (vca[:, kt, hh, 0:65], rvca)
                        P.mm(bo[:, 0:65], pT[pk][:], vsrc, bi == 0, bi == len(blocks) - 1, reads=[rpT[pk], rvs], writes=[rbo])
                    P.op("vector", lambda e, bo=bo: e.reciprocal(out=small[:, 1:2], in_=bo[:, 64:65]), [rbo], [rsmall])
                    P.ts(opair[i % 2][:, hp], bo[:, 0:64], small[:, 1:2], None, ALU.mult, reads=[rbo, rsmall], writes=[ropair[i % 2]])
                bk, rb = bank("b")
                P.mm(bk[:, 0:128], opair[i % 2][:], ident[:], True, True, reads=[ropair[i % 2], rid], writes=[rb])
                P.cp(catN[:, 0, i * 128:(i + 1) * 128], bk[:, 0:128], reads=[rb], writes=[rcatN], eng="scalar")
            off_save = ar.off
            outproj(catN, rcatN, 640 + p * 128, 1, l)
            ar.off = off_save
            P.barrier()

        P.mute = "G" not in phases
        ar.reset()
        alloc_w()
        lrT = [ar.bf16([16, T]), ar.bf16([16, T])]; rlrT = Res()
        wa16 = ar.f32([16, 2, 384]); rwa16 = Res()
        wab16 = ar.bf16([16, 2, 384]); rwab16 = Res()
        P.dma(wa16[0:16, 0, :], waf_d[l], writes=[rwa16])
        P.dma(wa16[0:16, 1, :], wab_d[l], writes=[rwa16])
        P.cp(wab16[0:16], wa16[0:16], reads=[rwa16], writes=[rwab16])
        for d_ in range(2):
            wb, rwb = wcols(w_in_d[l, :, LRF0 + d_ * 16:LRF0 + (d_ + 1) * 16], 16)
            proj_fm(wb, rwb, 0, 16, lambda tg, bk, rb, d_=d_: P.cp(lrT[d_][0:16, tg * 512:(tg + 1) * 512], bk[0:16, :],
                                                                  reads=[rb], writes=[rlrT], eng="scalar"))
        gqT = ar.bf16([128, T]); rgq = Res()
        gkT = ar.bf16([128, T]); rgk = Res()
        a_off = ar.off
        A_ = ar.f32([128, T]); rA = Res()
        B_ = ar.f32([128, T]); rB = Res()
        Eb = ar.bf16([128, T]); rEb = Res()
        totc = ar.f32([128, NT]); rtotc = Res()
        _qd = ar.bf16([128, T]); _rqd = Res(); qd = [_qd, _qd]; rqd = [_rqd, _rqd]
        _kd = ar.bf16([128, T]); _rkd = Res(); kd = [_kd, _kd]; rkd = [_rkd, _rkd]
        _kdec = ar.bf16([128, T]); _rkdec = Res(); kdec = [_kdec, _kdec]; rkdec = [_rkdec, _rkdec]
        _kdt = ar.bf16([128, NT, 128]); _rkdt = Res(); kdtok = [_kdt, _kdt]; rkdtok = [_rkdt, _rkdt]
        elast = ar.f32([128, 2, NT]); rel = Res()
        wvg = ar.bf16([128, 8, 256]); rwvg = Res()
        v_b = ar.bf16([128, NT, 128]); rvb = Res()
        sg = ar.bf16([128, NT, 128]); rsg = Res()
        of = ar.f32([128, NT, 128]); rof = Res()
        ycat = sg; rycat = rsg
        catG = gqT.rearrange("p (o t) -> p o t", o=1); rcatG = rgq
        nb = ar.f32([128, 2]); rnb = Res()
        gwb = ar.f32([128, 128]); rgwb = Res()
        S = ar.f32([128, 64]); rS = Res()
        Sb = ar.bf16([128, 64]); rSb = Res()
        s0 = ar.f32([128, 2, 64]); rs0 = Res()
        sst = ar.f32([128, 2, 8, 64]); rsst = Res()
        attm = [ar.bf16([128, 2, 128]), ar.bf16([128, 2, 128])]; rattm = [Res(), Res()]
        ot = ar.f32([128, 128]); rot = Res()
        osq = ar.f32([128, 128]); rosq = Res()
        Bv = B_.rearrange("p (c j) -> p c j", j=128)
        totb = totc[:, :].unsqueeze(2).to_broadcast([128, NT, 128])
        Av = A_.rearrange("p (c j) -> p c j", j=128)
        for p in range(3):
            cs = slice(p * 128, (p + 1) * 128)
            wb, rwb = wcols(w_in_d[l, :, QA0 + p * 128:QA0 + (p + 1) * 128], 128)
            proj_fm(wb, rwb, 0, 128, lambda tg, bk, rb: P.act(gqT[:, tg * 512:(tg + 1) * 512], bk[:, :], AF.Copy,
                                                             reads=[rb], writes=[rgq], scale=0.125))
            wb, rwb = wcols(w_in_d[l, :, KA0 + p * 128:KA0 + (p + 1) * 128], 128)
            proj_fm(wb, rwb, 0, 128, lambda tg, bk, rb: P.cp(gkT[:, tg * 512:(tg + 1) * 512], bk[:, :],
                                                            reads=[rb], writes=[rgk], eng="scalar"))
            wcols(w_in_d[l, :, VA0 + p * 128:VA0 + (p + 1) * 128], 128, wvg, rwvg, 0)
            wcols(w_in_d[l, :, GA0 + p * 128:GA0 + (p + 1) * 128], 128, wvg, rwvg, 128)
            for i in range(NT):
                bk, rb = bank("s")
                for c in range(8):
                    P.mm(bk[:, 0:256], hT[:, c, i * 128:(i + 1) * 128], wvg[:, c, :], c == 0, c == 7,
                         reads=[rhT, rwvg], writes=[rb])
                P.cp(v_b[:, i, :], bk[:, 0:128], reads=[rb], writes=[rvb], eng="scalar")
                P.act(sg[:, i, :], bk[:, 128:256], AF.Silu, reads=[rb], writes=[rsg])
            if os.environ.get("GSTOP") == "b":
                P.mute = True
            P.dma(nb[:, 0:1], baf_d[l, cs].rearrange("(p o) -> p o", o=1), writes=[rnb], **NSC)
            P.dma(nb[:, 1:2], bab_d[l, cs].rearrange("(p o) -> p o", o=1), writes=[rnb], **NSC)
            P.ts(nb[:], nb[:], -1.0, None, ALU.mult, reads=[rnb], writes=[rnb])
            P.dma(gwb[:], gnw_d[l, cs].partition_broadcast(128), writes=[rgwb])
            P.dma(s0[:, 0, :], s0f_d[l, 2 * p:2 * p + 2].rearrange("h k v -> (h k) v"), writes=[rs0])
            P.dma(s0[:, 1, :], s0b_d[l, 2 * p:2 * p + 2].rearrange("h k v -> (h k) v"), writes=[rs0])
            for d_ in range(2):
                for tg in range(4):
                    bk, rb = bank("s")
                    P.mm(bk[:, :], wab16[0:16, d_, cs], lrT[d_][0:16, tg * 512:(tg + 1) * 512], True, True,
                         reads=[rwab16, rlrT], writes=[rb])
                    P.act(A_[:, tg * 512:(tg + 1) * 512], bk[:, :], AF.Exp, reads=[rb, rnb], writes=[rA],
                          scale=-1.0, bias=nb[:, d_:d_ + 1])
                P.act(A_[:], A_[:], AF.Ln, reads=[rA], writes=[rA], bias=1.0)
                for c in range(NT):
                    P.op("vector", lambda e, c=c: e.tensor_tensor_scan(out=B_[:, c * 128:(c + 1) * 128], data0=onesf[:, :],
                                                                      data1=A_[:, c * 128:(c + 1) * 128], initial=0.0,
                                                                      op0=ALU.mult, op1=ALU.add), [rA, rconst], [rB])
                P.cp(totc[:, :], B_[:, 127:T:128], reads=[rB], writes=[rtotc])
                if d_ == 0:
                    RC, rRC, FR, rFR, FRv = B_, rB, A_, rA, Av
                else:
                    P.tt(A_[:], A_[:], B_[:], ALU.subtract, reads=[rA, rB], writes=[rA], eng="gpsimd")
                    P.tt(Av, Av, totb, ALU.add, reads=[rA, rtotc], writes=[rA], eng="gpsimd")
                    RC, rRC, FR, rFR, FRv = A_, rA, B_, rB, Bv
                RCv = RC.rearrange("p (c j) -> p c j", j=128)
                P.act(Eb[:], RC[:], AF.Exp, reads=[rRC], writes=[rEb], scale=-1.0 / 16)
                P.tt(qd[d_][:], gqT[:], Eb[:], ALU.mult, reads=[rgq, rEb], writes=[rqd[d_]])
                P.act(Eb[:], RC[:], AF.Exp, reads=[rRC], writes=[rEb], scale=1.0 / 16)
                P.tt(kd[d_][:], gkT[:], Eb[:], ALU.mult, reads=[rgk, rEb], writes=[rkd[d_]])
                P.tt(FRv, RCv, totb, ALU.subtract, reads=[rRC, rtotc], writes=[rFR], eng="gpsimd")
                P.act(Eb[:], FR[:], AF.Exp, reads=[rFR], writes=[rEb], scale=1.0 / 16)
                P.tt(kdec[d_][:], gkT[:], Eb[:], ALU.mult, reads=[rgk, rEb], writes=[rkdec[d_]])
                P.act(elast[:, d_, :], totc[:, :], AF.Exp, reads=[rtotc], writes=[rel], scale=-1.0 / 16)
                for c in range(NT):
                    bk, rb = bank("s")
                    P.mm(bk[:, 0:128], kdec[d_][:, c * 128:(c + 1) * 128], ident[:], True, True, reads=[rkdec[d_], rid], writes=[rb])
                    P.cp(kdtok[d_][:, c, :], bk[:, 0:128], reads=[rb], writes=[rkdtok[d_]], eng="scalar")
                if os.environ.get("GSTOP") == "c":
                    P.mute = True
                order = list(range(NT)) if d_ == 0 else list(range(NT - 1, -1, -1))
                P.cp(S[:], s0[:, d_, :], reads=[rs0], writes=[rS])
                for n_, c in enumerate(order):
                    ccs = slice(c * 128, (c + 1) * 128)
                    if n_ > 0 and n_ % 2 == 0:
                        P.ts(S[:], S[:], carry[:, 0:1], None, ALU.mult, reads=[rS, rconst], writes=[rS])
                    P.cp(Sb[:], S[:], reads=[rS], writes=[rSb], eng="scalar")
                    am = attm[n_ % 2]; ram = rattm[n_ % 2]
                    GSK = os.environ.get("GSKIP", "")
                    _m0 = P.mute
                    bos = []
                    for hh in range(2):
                        hp = slice(hh * 64, hh * 64 + 64)
                        ba, rba = bank("s")
                        P.mm(ba[:, 0:128], kd[d_][hp, ccs], qd[d_][hp, ccs], True, True,
                             reads=[rkd[d_], rqd[d_]], writes=[rba])
                        P.tt(am[:, hh, :], ba[:, 0:128], glam[:, d_, :], ALU.mult, reads=[rba, rconst], writes=[ram])
                    for hh in range(2):
                        hp = slice(hh * 64, hh * 64 + 64)
                        bo, rbo = bank("a")
                        bos.append((bo, rbo))
                        P.mm(bo[:, 0:64], am[:, hh, :], v_b[:, c, hh * 64:(hh + 1) * 64], True, False,
                             reads=[ram, rvb], writes=[rbo])
                        P.mm(bo[:, 0:64], qd[d_][hp, ccs], Sb[hp, :], False, True,
                             reads=[rqd[d_], rSb], writes=[rbo])
                    if d_ == 0:
                        for hh in range(2):
                            P.cp(of[:, c, hh * 64:(hh + 1) * 64], bos[hh][0][:, 0:64], reads=[bos[hh][1]], writes=[rof], eng="scalar")
                    else:
                        for hh in range(2):
                            P.tt(ot[:, hh * 64:(hh + 1) * 64], of[:, c, hh * 64:(hh + 1) * 64], bos[hh][0][:, 0:64], ALU.add,
                                 reads=[rof, bos[hh][1]], writes=[rot])
                        P.tt(osq[:], ot[:], ot[:], ALU.mult, reads=[rot], writes=[rosq], eng="gpsimd")
                        P.op("vector", lambda e: e.tensor_reduce(out=small[:, 2:4], in_=osq[:].rearrange("p (h d) -> p h d", h=2),
                                                                 axis=AX.X, op=ALU.add), [rosq], [rsmall])
                        P.ts(small[:, 2:4], small[:, 2:4], 1.0 / 64, 1e-6, ALU.mult, ALU.add, reads=[rsmall], writes=[rsmall])
                        P.act(small[:, 2:4], small[:, 2:4], AF.Sqrt, reads=[rsmall], writes=[rsmall])
                        P.op("vector", lambda e: e.reciprocal(out=small[:, 2:4], in_=small[:, 2:4]), [rsmall], [rsmall])
                        P.tt(ot[:].rearrange("p (h d) -> p h d", h=2), ot[:].rearrange("p (h d) -> p h d", h=2),
                             small[:, 2:4].unsqueeze(2).to_broadcast([128, 2, 64]), ALU.mult, reads=[rot, rsmall], writes=[rot])
                        P.tt(ot[:], ot[:], gwb[:], ALU.mult, reads=[rot, rgwb], writes=[rot])
                        P.tt(ycat[:, c, :], ot[:], sg[:, c, :], ALU.mult, reads=[rot, rsg], writes=[rycat])
                    P.mute = _m0 or ("s" in GSK)
                    bd, rbd = bank("b")
                    P.mm(bd[:, 0:128], kdtok[d_][:, c, :], v_b[:, c, :], True, True, reads=[rkdtok[d_], rvb], writes=[rbd])
                    for hh in range(2):
                        hp = slice(hh * 64, hh * 64 + 64)
                        P.stt(S[hp, :], S[hp, :], elast[hp, d_, c:c + 1], bd[hp, hh * 64:(hh + 1) * 64], ALU.mult, ALU.add,
                              reads=[rS, rel, rbd], writes=[rS])
                    if n_ % 2 == 1:
                        P.cp(sst[:, d_, c // 2, :], S[:], reads=[rS], writes=[rsst], eng="gpsimd")
                    P.mute = _m0
            if os.environ.get("GSTOP") == "d":
                P.mute = True
            for c in range(NT):
                bk, rb = bank("s")
                P.mm(bk[:, 0:128], ycat[:, c, :], ident[:], True, True, reads=[rycat, rid], writes=[rb])
                P.cp(catG[:, 0, c * 128:(c + 1) * 128], bk[:, 0:128], reads=[rb], writes=[rcatG], eng="scalar")
            P.dma(sf_d[l, :, 2 * p:2 * p + 2].rearrange("g h k v -> (h k) g v"), sst[:, 0, :, :], reads=[rsst], q="gpsimd")
            P.dma(sbo_d[l, :, 2 * p:2 * p + 2].rearrange("g h k v -> (h k) g v"), sst[:, 1, :, :], reads=[rsst], q="gpsimd")
            P.barrier()
            off_save = ar.off
            ar.off = a_off
            outproj(catG, rcatG, p * 128, 1, l)
            ar.off = off_save
            P.barrier()

        P.mute = "P" not in phases
        P.barrier()
        norm_to_hT(1)
        P.barrier()
        ar.reset()
        pq = ar.bf16([128, 8, T]); rpq = Res()
        kk = ar.bf16([128, 8, 128]); rkk = Res()
        Kc = ar.bf16([128, 8, 4, 128]); rKc = Res()
        tauS = ar.f32([128, NT, 8]); rtau = Res()
        negcS = ar.f32([128, NT, 8]); rnegc = Res()
        pk_off = ar.off
        alloc_w()
        kst = ar.f32([128, 8, 128]); rkst = Res()
        kstb = ar.bf16([128, 8, 128]); rkstb = Res()
        for hq in range(8):
            wb, rwb = wcols(wq_d[l, :, hq * 128:(hq + 1) * 128], 128)
            proj_fm(wb, rwb, 0, 128, lambda tg, bk, rb, hq=hq: P.cp(pq[:, hq, tg * 512:(tg + 1) * 512], bk[:, :],
                                                                   reads=[rb], writes=[rpq], eng="scalar"))
        P.dma(kst[:, :, 0:64], k1_d[l].rearrange("h n d -> n h d"), writes=[rkst])
        P.dma(kst[:, :, 64:128], k2_d[l].rearrange("h n d -> n h d"), writes=[rkst])
        P.cp(kstb[:], kst[:], reads=[rkst], writes=[rkstb])
        for h in range(8):
            bk, rb = bank("s")
            P.mm(bk[:, 0:128], kstb[:, h, :], ident[:], True, True, reads=[rkstb, rid], writes=[rb])
            P.cp(kk[:, h, :], bk[:, 0:128], reads=[rb], writes=[rkk], eng="scalar")
        P.cp(Kc[64:128], kk[64:128, :, :].unsqueeze(2).to_broadcast([64, 8, 4, 128]), reads=[rkk], writes=[rKc], eng="gpsimd")
        ssb = ar.f32([128, 16, 128]); rssb = Res()
        tmpa = ar.f32([128, 128]); rtmpa = Res()
        tmpb = ar.f32([128, 128]); rtmpb = Res()
        v24 = ar.f32([128, 16, 24]); rv24 = Res()
        cand = ar.f32([128, 8, 576]); rcand = Res()
        cta = ar.f32([128, 576]); rcta = Res()
        ctb = ar.f32([128, 576]); rctb = Res()
        c24 = ar.f32([128, 8, 24]); rc24 = Res()
        e16 = ar.f32([128, 8, 16]); re16 = Res()
        zz = ar.f32([128, 8]); rzz = Res()
        for i in range(NT):
            tcs = slice(i * 128, (i + 1) * 128)
            for hg in range(2):
                b0, rb0 = bank("s")
                b1, rb1 = bank("s")
                for h4 in range(4):
                    h = hg * 4 + h4
                    P.mm(b0[:, h4 * 128:(h4 + 1) * 128], pq[0:64, h, tcs], kk[0:64, h, :], True, True, reads=[rpq, rkk], writes=[rb0])
                for h4 in range(4):
                    h = hg * 4 + h4
                    P.mm(b1[:, h4 * 128:(h4 + 1) * 128], pq[64:128, h, tcs], kk[64:128, h, :], True, True, reads=[rpq, rkk], writes=[rb1])
                sv = ssb[:, hg * 8:hg * 8 + 8, :].rearrange("p (h s) k -> p h s k", s=2)
                P.cp(sv[:, :, 0, :], b0[:, :].rearrange("p (a b) -> p a b", a=4), reads=[rb0], writes=[rssb], eng="scalar")
                P.cp(sv[:, :, 1, :], b1[:, :].rearrange("p (a b) -> p a b", a=4), reads=[rb1], writes=[rssb], eng="scalar")
            for r in range(16):
                P.op("vector", lambda e, r=r: e.max(out=v24[:, r, 0:8], in_=ssb[:, r, :]), [rssb], [rv24])
                P.op("vector", lambda e, r=r: e.match_replace(out=tmpa[:], in_to_replace=v24[:, r, 0:8], in_values=ssb[:, r, :], imm_value=-1e30),
                     [rssb, rv24], [rtmpa])
                P.op("vector", lambda e, r=r: e.max(out=v24[:, r, 8:16], in_=tmpa[:]), [rtmpa], [rv24])
                P.op("vector", lambda e, r=r: e.match_replace(out=tmpb[:], in_to_replace=v24[:, r, 8:16], in_values=tmpa[:], imm_value=-1e30),
                     [rtmpa, rv24], [rtmpb])
                P.op("vector", lambda e, r=r: e.max(out=v24[:, r, 16:24], in_=tmpb[:]), [rtmpb], [rv24])
            v4 = v24[:, :, :].rearrange("p (h s) k -> p h s k", s=2)
            P.tt(cand[:].rearrange("p h (a b) -> p h a b", a=24),
                 v4[:, :, 0, :].unsqueeze(3).to_broadcast([128, 8, 24, 24]),
                 v4[:, :, 1, :].unsqueeze(2).to_broadcast([128, 8, 24, 24]), ALU.add, reads=[rv24], writes=[rcand], eng="gpsimd")
            for h in range(8):
                P.op("vector", lambda e, h=h: e.max(out=c24[:, h, 0:8], in_=cand[:, h, :]), [rcand], [rc24])
                P.op("vector", lambda e, h=h: e.match_replace(out=cta[:], in_to_replace=c24[:, h, 0:8], in_values=cand[:, h, :], imm_value=-1e30),
                     [rcand, rc24], [rcta])
                P.op("vector", lambda e, h=h: e.max(out=c24[:, h, 8:16], in_=cta[:]), [rcta], [rc24])
                P.op("vector", lambda e, h=h: e.match_replace(out=ctb[:], in_to_replace=c24[:, h, 8:16], in_values=cta[:], imm_value=-1e30),
                     [rcta, rc24], [rctb])
                P.op("vector", lambda e, h=h: e.max(out=c24[:, h, 16:24], in_=ctb[:]), [rctb], [rc24])
            P.tt(tauS[:, i, :], c24[:, :, 15], c24[:, :, 16], ALU.add, reads=[rc24], writes=[rtau])
            P.ts(tauS[:, i, :], tauS[:, i, :], 0.5, None, ALU.mult, reads=[rtau], writes=[rtau])
            P.tt(e16[:], c24[:, :, 0:16], c24[:, :, 0:1].to_broadcast([128, 8, 16]), ALU.subtract, reads=[rc24], writes=[re16])
            P.act(e16[:], e16[:], AF.Exp, reads=[re16], writes=[re16])
            P.op("vector", lambda e: e.tensor_reduce(out=zz[:], in_=e16[:], axis=AX.X, op=ALU.add), [re16], [rzz])
            P.act(zz[:], zz[:], AF.Ln, reads=[rzz], writes=[rzz])
            P.tt(zz[:], zz[:], c24[:, :, 0], ALU.add, reads=[rzz, rc24], writes=[rzz])
            P.ts(negcS[:, i, :], zz[:], -1.0, None, ALU.mult, reads=[rzz], writes=[rnegc])
        P.barrier()
        ar.off = pk_off
        ust = ar.f32([128, D]); rust = [Res(), Res()]
        vst = ar.f32([128, D]); rvst = [Res(), Res()]
        ub = ar.bf16([128, 2, D]); rub = [Res(), Res()]
        UT = ar.bf16([128, 8, 512]); rUT = Res()
        Vb = ar.bf16([128, 4, D]); rVb = Res()
        gel = [ar.bf16([128, 4, 512]), ar.bf16([128, 4, 512])]; rgel = [Res(), Res()]
        Ebf = [ar.bf16([128, 512]) for _ in range(2)]; rEbf = [Res() for _ in range(2)]
        Gm = [ar.bf16([128, 8, 512]), ar.bf16([128, 8, 512])]; rGm = [[Res() for _ in range(8)] for _ in range(2)]
        AT = [ar.bf16([128, 4, 128]), ar.bf16([128, 4, 128])]; rAT = [Res(), Res()]
        Gsb = [ar.bf16([128, 512]), ar.bf16([128, 512])]; rGsb = [Res(), Res()]
        ecnt = 0
        ucnt = 0
        tl = {}

        def gate_head(i, gs, h):
            nonlocal ecnt
            tcs = slice(i * 128, (i + 1) * 128)
            bk, rb = bank("s")
            P.mm(bk[:, :], pq[:, h, tcs], Kc[:, h, :, :].rearrange("p a b -> p (a b)"), True, True,
                 reads=[rpq, rKc], writes=[rb])
            ek = ecnt % 2
            ecnt += 1
            P.act(Ebf[ek][:], bk[:, :], AF.Exp, reads=[rb, rnegc], writes=[rEbf[ek]], bias=negcS[:, i, h:h + 1], scale=1.0)
            P.stt(Gm[gs][:, h, :], bk[:, :], tauS[:, i, h:h + 1], Ebf[ek][:], ALU.is_ge, ALU.mult,
                  reads=[rb, rtau, rEbf[ek]], writes=[rGm[gs][h]])

        def tail_a(i, gs, gl, t4):
            bsum, rbsum = bank("a")
            for h in range(8):
                P.mm(bsum[:, :], ident[:], Gm[gs][:, h, :], h == 0, h == 7, reads=[rGm[gs][h], rid], writes=[rbsum])
            P.cp(Gsb[gs][:], bsum[:, :], reads=[rbsum], writes=[rGsb[gs]], eng="scalar")

        def tail_b(i, gs, gl, t4):
            bg, rbg = bank("a")
            for k4 in range(4):
                P.mm(bg[:, k4 * 128:(k4 + 1) * 128], Gsb[gs][:, k4 * 128:(k4 + 1) * 128], ident[:], True, True,
                     reads=[rGsb[gs], rid], writes=[rbg])
            P.tt(AT[gs][:], gel[gl][:, :, t4 * 128:(t4 + 1) * 128], bg[:, :].rearrange("p (a b) -> p a b", a=4), ALU.mult,
                 reads=[rgel[gl], rbg], writes=[rAT[gs]])

        def tail_c(i, gs, gl, t4):
            for hf in range(2):
                bo, rbo = bank("b")
                for k4 in range(4):
                    P.mm(bo[:, :], AT[gs][:, k4, :], Vb[:, k4, hf * 512:(hf + 1) * 512], k4 == 0, k4 == 3,
                         reads=[rAT[gs], rVb], writes=[rbo])
                P.tt(x[:, i, hf * 512:(hf + 1) * 512], x[:, i, hf * 512:(hf + 1) * 512], bo[:, :], ALU.add,
                     reads=[rx[i], rbo], writes=[rx[i]])

        for quad in range(32):
            for k4 in range(4):
                r0 = (quad * 4 + k4) * 128
                kb = k4 % 2
                for hf in range(2):
                    hs = slice(hf * 512, (hf + 1) * 512)
                    P.dma(ust[:, hs], pu_d[l, r0:r0 + 128, hs], writes=[rust[hf]])
                    P.dma(vst[:, hs], pv_d[l, r0:r0 + 128, hs], writes=[rvst[hf]])
                for hf in range(2):
                    hs = slice(hf * 512, (hf + 1) * 512)
                    P.act(ub[:, kb, hs], ust[:, hs], AF.Copy, reads=[rust[hf]], writes=[rub[kb]])
                    P.tt(Vb[:, k4, hs], vst[:, hs], gb[:, 1, hs], ALU.mult, reads=[rvst[hf], rgb], writes=[rVb], eng="gpsimd")
                for c2 in range(2):
                    bk, rb = bank("s")
                    for c4 in range(4):
                        c = c2 * 4 + c4
                        P.mm(bk[:, c4 * 128:(c4 + 1) * 128], ub[:, kb, c * 128:(c + 1) * 128], ident[:], True, True,
                             reads=[rub[kb], rid], writes=[rb])
                    P.cp(UT[:, c2 * 4:(c2 + 1) * 4, k4 * 128:(k4 + 1) * 128], bk[:, :].rearrange("p (a b) -> p a b", a=4),
                         reads=[rb], writes=[rUT], eng="vector")
            for h2 in range(2):
                P.cp(Kc[0:64, h2 * 4:(h2 + 1) * 4], kk[0:64, h2 * 4:(h2 + 1) * 4, quad * 4:quad * 4 + 4].unsqueeze(3).to_broadcast([64, 4, 4, 128]),
                     reads=[rkk], writes=[rKc], eng=("vector" if h2 == 0 else "scalar"))
            prev = None
            for tg in range(4):
                gl = tg % 2
                for k4 in range(4):
                    bk, rb = bank("s")
                    for c in range(8):
                        P.mm(bk[:, :], UT[:, c, k4 * 128:(k4 + 1) * 128], hT[:, c, tg * 512:(tg + 1) * 512], c == 0, c == 7,
                             reads=[rUT, rhT], writes=[rb])
                    P.act(gel[gl][:, k4, :], bk[:, :], AF.Gelu, reads=[rb], writes=[rgel[gl]])
                for t4 in range(4):
                    i = tg * 4 + t4
                    gs = ucnt % 2
                    ucnt += 1
                    for h in range(8):
                        gate_head(i, gs, h)
                        if prev is not None:
                            if h == 1:
                                tail_a(*prev)
                            elif h == 4:
                                tail_b(*prev)
                            elif h == 7:
                                tail_c(*prev)
                    prev = (i, gs, gl, t4)
            tail_a(*prev)
            tail_b(*prev)
            tail_c(*prev)

    P.mute = False
    P.barrier()
    ar.reset()
    fwb = ar.f32([128, D]); rfwb = Res()
    sq = ar.f32([128, D]); rsq = Res()
    yo = [ar.f32([128, D]), ar.f32([128, D])]; ryo = [Res(), Res()]
    P.dma(fwb[:], fnw_d.partition_broadcast(128), writes=[rfwb])
    for i in range(NT):
        if raw:
            P.cp(yo[i % 2][:], x[:, i, :], reads=[rx[i]], writes=[ryo[i % 2]])
        else:
            rstd_of(x[:, i, :], D, small[:, 0:1], [rx[i]], sq, rsq)
            P.stt(yo[i % 2][:], x[:, i, :], small[:, 0:1], fwb[:], ALU.mult, ALU.mult, reads=[rx[i], rsmall, rfwb], writes=[ryo[i % 2]])
        P.dma(y_d[i * 128:(i + 1) * 128, :], yo[i % 2][:], reads=[ryo[i % 2]], q="gpsimd")
    P.emit()
    P.es.close()
    return nc, P


def _na_tables():
    a = np.arange(128)
    types = {}
    for i in range(NT):
        for kt, ty in na_blocks(i):
            types.setdefault(ty, (i, kt))
    DR = np.zeros((26, 128, 128), np.int64); DC = np.zeros((26, 128, 128), np.int64)
    msk_s = np.zeros((NTYPE, 128, 128), np.float32); msk_p = np.zeros((NTYPE, 128, 128), np.float32)
    for ty, (i, kt) in types.items():
        krow = (2 * kt + a // 64)[:, None]; kcol = (a % 64)[:, None]
        qrow = (2 * i + a // 64)[None, :]; qcol = (a % 64)[None, :]
        rs = np.clip(qrow - 4, 0, 24)
        cs = np.clip(qcol - 8, 0, 48)
        valid = (krow >= rs) & (krow < rs + 8) & (kcol >= cs) & (kcol < cs + 16)
        DR[ty] = np.clip(krow - qrow, -7, 7) + 7
        DC[ty] = np.clip(kcol - qcol, -15, 15) + 15
        msk_s[ty] = np.where(valid, 0.0, -1e30)
        msk_p[ty] = 0.0 if (kt // 2 == i // 2) else -1e30
    msk_s[26] = 0.0
    msk_p[26] = -1e30
    return DR, DC, msk_s, msk_p


_CACHE = {}


def kernel(x_prompt, x_sample, cache_na_k, cache_na_v, state_gla_fwd, state_gla_bwd, c, c_ctx,
           w_ada, b_ada, norm1_w, norm2_w, w_in, w_af, b_af, w_ab, b_ab, gla_norm_w,
           conv_w, conv_b, na_rpb, w_out, peer_wq, peer_k1, peer_k2, peer_u, peer_v, final_norm_w):
    f = lambda a: np.ascontiguousarray(np.asarray(a), dtype=np.float32)
    x_prompt, x_sample = f(x_prompt), f(x_sample)
    DR, DC, msk_s, msk_p = _na_tables()
    rpb = f(na_rpb)
    nab_s = np.ascontiguousarray(rpb[:, :, DR, DC])
    nab_p = np.zeros_like(nab_s)
    a = np.arange(128)
    glam = np.stack([(a[:, None] <= a[None, :]), (a[:, None] >= a[None, :])]).astype(np.float32)
    shared = {
        "ident": np.eye(128).astype(NPBF), "glamask": glam,
        "w_ada": f(w_ada), "b_ada": f(b_ada), "norm1_w": f(norm1_w), "norm2_w": f(norm2_w), "w_in": f(w_in),
        "w_af": f(w_af), "b_af": f(b_af), "w_ab": f(w_ab), "b_ab": f(b_ab), "gla_norm_w": f(gla_norm_w),
        "conv_w": f(conv_w), "conv_b": f(conv_b), "w_out": f(w_out), "peer_wq": f(peer_wq),
        "peer_k1": f(peer_k1), "peer_k2": f(peer_k2), "peer_u": f(peer_u), "peer_v": f(peer_v),
        "final_norm_w": f(final_norm_w),
    }
    zk = np.zeros((DEPTH, 6, 256, 64), np.float32)
    zs = np.zeros((DEPTH, 6, 64, 64), np.float32)
    in_maps = []
    for core in range(8):
        m = dict(shared)
        if core < 4:
            b = core
            m.update(x=x_sample[b], cvec=f(c)[b], kctx=f(cache_na_k)[b], vctx=f(cache_na_v)[b],
                     s0f=f(state_gla_fwd)[b], s0b=f(state_gla_bwd)[b], carry=np.ones((128, 1), np.float32),
                     nabias=nab_s, namask=msk_s)
        else:
            b0 = (core - 4) * 4
            xp = x_prompt[b0:b0 + 4].reshape(1024, D)
            m.update(x=np.ascontiguousarray(np.concatenate([xp, xp], 0)), cvec=f(c_ctx), kctx=zk, vctx=zk,
                     s0f=zs, s0b=zs, carry=np.zeros((128, 1), np.float32), nabias=nab_p, namask=msk_p)
        in_maps.append(m)
    if _CACHE.get("only_maps"):
        return in_maps
    if "nc" not in _CACHE:
        _CACHE["nc"] = build()
    nc, P = _CACHE["nc"]
    res = run_bass_kernel_spmd(nc, in_maps, core_ids=list(range(8)))
    R = res.results
    y_sample = np.stack([R[b]["y"] for b in range(4)], 0)
    y_prompt = np.concatenate([R[4 + j]["y"][:1024].reshape(4, 256, D) for j in range(4)], 0)

    def kvout(name):
        o = np.concatenate([R[4 + j][name][:, :1024].reshape(DEPTH, 4, 256, 6, 64) for j in range(4)], 1)
        return np.ascontiguousarray(o.transpose(1, 0, 3, 2, 4))

    def stout(name):
        o = np.concatenate([R[4 + j][name][:, :4] for j in range(4)], 1)
        return np.ascontiguousarray(o.transpose(1, 0, 2, 3, 4))
    return (y_prompt.astype(np.float32), y_sample.astype(np.float32), kvout("nk"), kvout("nv"), stout("sf"), stout("sbo"))
```

```python
import os
import numpy as np
from contextlib import ExitStack
import ml_dtypes
import concourse.bass as bass
import concourse.mybir as mybir
from concourse.bass_utils import run_bass_kernel_spmd


F32 = mybir.dt.float32
BF16 = mybir.dt.bfloat16
AF = mybir.ActivationFunctionType
ALU = mybir.AluOpType
AX = mybir.AxisListType
NPBF = ml_dtypes.bfloat16


class Res:
    __slots__ = ("name", "w", "rd")

    def __init__(self, name="r"):
        self.name = name
        self.w = None
        self.rd = []


class Op:
    __slots__ = ("eng", "fn", "dma", "waits", "signal", "sem", "val", "prevsame")

    def __init__(self, eng, fn, dma):
        self.eng = eng
        self.fn = fn
        self.dma = dma
        self.waits = []
        self.signal = False
        self.sem = None
        self.val = None
        self.prevsame = None


ENGINES = ("tensor", "vector", "scalar", "gpsimd", "sync")
NDMASEM = 16
SEMCH = 30000


class Prog:
    def __init__(self, nc):
        self.nc = nc
        self.streams = {e: [] for e in ENGINES}
        self.es = ExitStack()
        self.nops = 0
        self.pending = {e: [] for e in ENGINES}
        self.out_dma = []
        self.mute = False

    def sb(self, name, shape, dt):
        return self.es.enter_context(self.nc.sbuf_tensor("sb_" + name, list(shape), dt))

    def ps(self, name, shape, dt=F32):
        return self.es.enter_context(self.nc.psum_tensor(name, list(shape), dt))

    def op(self, eng, fn, reads=(), writes=(), dma=False):
        o = Op(eng, fn, dma)
        if self.mute:
            return o
        deps = []
        for r in reads:
            if r.w is not None:
                deps.append(r.w)
        for w in writes:
            if w.w is not None:
                deps.append(w.w)
            deps.extend(w.rd)
        if self.pending[eng]:
            deps.extend(self.pending[eng])
            self.pending[eng] = []
        seen = set()
        for d in deps:
            if d is o or id(d) in seen:
                continue
            seen.add(id(d))
            if d.eng == "tensor" and eng == "tensor" and not d.dma and not dma:
                continue
            d.signal = True
            o.waits.append(d)
        for r in reads:
            r.rd.append(o)
        for w in writes:
            w.w = o
            w.rd = []
        self.streams[eng].append(o)
        self.nops += 1
        if dma:
            self.out_dma.append(o)
        return o

    def barrier(self):
        lasts = []
        for e in ENGINES:
            st = self.streams[e]
            last_c = None
            for o in reversed(st):
                if not o.dma:
                    last_c = o
                    break
            if last_c is not None:
                lasts.append(last_c)
        lasts.extend(self.out_dma)
        self.out_dma = []
        for e in ENGINES:
            self.pending[e] = list(lasts)

    def dma(self, out, in_, reads=(), writes=(), q="sync", **kw):
        return self.op(q, lambda e: e.dma_start(out=out, in_=in_, **kw), reads, writes, dma=True)

    def mm(self, out, lhsT, rhs, start, stop, reads=(), writes=()):
        return self.op("tensor", lambda e: e.matmul(out, lhsT, rhs, start=start, stop=stop), reads, writes)

    def act(self, out, in_, func, reads=(), writes=(), eng="scalar", **kw):
        return self.op(eng, lambda e: e.activation(out=out, in_=in_, func=func, **kw), reads, writes)

    def tt(self, out, in0, in1, op, reads=(), writes=(), eng="vector"):
        return self.op(eng, lambda e: e.tensor_tensor(out=out, in0=in0, in1=in1, op=op), reads, writes)

    def ts(self, out, in0, s1, s2, op0, op1=None, reads=(), writes=(), eng="vector"):
        if op1 is None:
            return self.op(eng, lambda e: e.tensor_scalar(out=out, in0=in0, scalar1=s1, scalar2=None, op0=op0), reads, writes)
        return self.op(eng, lambda e: e.tensor_scalar(out=out, in0=in0, scalar1=s1, scalar2=s2, op0=op0, op1=op1), reads, writes)

    def stt(self, out, in0, scalar, in1, op0, op1, reads=(), writes=(), eng="vector"):
        return self.op(eng, lambda e: e.scalar_tensor_tensor(out=out, in0=in0, scalar=scalar, in1=in1, op0=op0, op1=op1), reads, writes)

    def cp(self, out, in_, reads=(), writes=(), eng="vector"):
        if eng == "scalar":
            return self.op(eng, lambda e: e.copy(out=out, in_=in_), reads, writes)
        return self.op(eng, lambda e: e.tensor_copy(out=out, in_=in_), reads, writes)

    def memset(self, ap, val, writes=(), eng="vector"):
        return self.op(eng, lambda e: e.memset(ap, val), (), writes)

    def emit(self):
        nc = self.nc
        es = self.es
        csems = {}
        dsems = {}
        for e in ENGINES:
            st = self.streams[e]
            nsig = sum(1 for o in st if (o.signal and not o.dma))
            csems[e] = [es.enter_context(nc.semaphore(f"c_{e}_{i}")) for i in range(nsig // SEMCH + 1)]
            ndma = sum(1 for o in st if o.dma)
            dsems[e] = [es.enter_context(nc.semaphore(f"d_{e}_{i}")) for i in range(min(NDMASEM, ndma))]
            cnt = 0
            dcnt = 0
            hist = []
            for o in st:
                if o.dma:
                    k = dcnt
                    o.sem = dsems[e][k % NDMASEM]
                    o.val = 16 * (k // NDMASEM + 1)
                    if k >= NDMASEM:
                        o.prevsame = hist[k - NDMASEM]
                    hist.append(o)
                    dcnt += 1
                elif o.signal:
                    o.sem = csems[e][cnt // SEMCH]
                    o.val = cnt % SEMCH + 1
                    cnt += 1
        finals = []
        for e in ENGINES:
            last = {}
            for o in self.streams[e]:
                if o.dma:
                    last[id(o.sem)] = (o.sem, o.val)
            finals.extend(last.values())
        streams = self.streams
        block = es.enter_context(nc.Block())

        def make(e):
            def run(engh):
                waited = {}
                for o in streams[e]:
                    need = {}
                    for d in o.waits:
                        k = id(d.sem)
                        if k not in need or need[k][1] < d.val:
                            need[k] = (d.sem, d.val)
                    if o.prevsame is not None:
                        k = id(o.sem)
                        v = o.prevsame.val
                        if k not in need or need[k][1] < v:
                            need[k] = (o.sem, v)
                    for k, (sem, val) in need.items():
                        if waited.get(k, 0) < val:
                            engh.wait_ge(sem, val)
                            waited[k] = val
                    ins = o.fn(engh)
                    if o.dma:
                        ins.then_inc(o.sem, 16)
                    elif o.signal:
                        ins.then_inc(o.sem, 1)
                if e == "sync":
                    for sem, val in finals:
                        engh.wait_ge(sem, val)
            return run

        for e in ENGINES:
            getattr(block, e)(make(e))


D = 1024
NT = 16
T = 2048
DEPTH = 4
INW = 3488
QA0, KA0, VA0, GA0, LRF0, LRB0 = 0, 384, 768, 1152, 1536, 1552
CH0, CB0, CC0 = 1568, 1824, 2080
QN0, KN0, VN0 = 2336, 2720, 3104
NTYPE = 27
ARENA_W = 25856


def na_blocks(i):
    if 2 <= i <= 13:
        return [(i - 2 + m, (i % 2) * 5 + m) for m in range(5)]
    e = {0: 0, 1: 1, 14: 2, 15: 3}[i]
    base = 0 if i < 2 else 12
    return [(base + m, 10 + e * 4 + m) for m in range(4)]


class Arena:
    def __init__(self, P):
        self.t = P.sb("arena", [128, ARENA_W], F32)
        self.off = 0

    def reset(self):
        self.off = 0

    def f32(self, shape):
        n = int(np.prod(shape[1:]))
        v = self.t[:, self.off:self.off + n]
        self.off += n
        assert self.off <= ARENA_W, self.off
        return self._shape(v, shape)

    def bf16(self, shape):
        n = int(np.prod(shape[1:]))
        assert n % 2 == 0
        v = self.t[:, self.off:self.off + n // 2].bitcast(BF16)
        self.off += n // 2
        assert self.off <= ARENA_W, self.off
        return self._shape(v, shape)

    @staticmethod
    def _shape(v, shape):
        if len(shape) == 2:
            return v
        if len(shape) == 3:
            return v.rearrange("p (a b) -> p a b", a=shape[1], b=shape[2])
        if len(shape) == 4:
            return v.rearrange("p (a b c) -> p a b c", a=shape[1], b=shape[2], c=shape[3])
        raise ValueError(shape)


def build(nlayers=DEPTH, phases="CNGP", raw=False):
    nc = bass.Bass("TRN2", target_bir_lowering=False)

    def din(name, shape, dt=F32):
        return nc.dram_tensor(name, list(shape), dt, kind="ExternalInput").ap()

    def dout(name, shape):
        return nc.dram_tensor(name, list(shape), F32, kind="ExternalOutput").ap()

    x_d = din("x", [T, D]); cvec_d = din("cvec", [D])
    kctx_d = din("kctx", [DEPTH, 6, 256, 64]); vctx_d = din("vctx", [DEPTH, 6, 256, 64])
    s0f_d = din("s0f", [DEPTH, 6, 64, 64]); s0b_d = din("s0b", [DEPTH, 6, 64, 64])
    carry_d = din("carry", [128, 1])
    nabias_d = din("nabias", [DEPTH, 6, 26, 128, 128]); namask_d = din("namask", [NTYPE, 128, 128])
    ident_d = din("ident", [128, 128], BF16)
    glam_d = din("glamask", [2, 128, 128])
    w_ada_d = din("w_ada", [DEPTH, D, 6 * D]); b_ada_d = din("b_ada", [DEPTH, 6 * D])
    n1_d = din("norm1_w", [DEPTH, D]); n2_d = din("norm2_w", [DEPTH, D])
    w_in_d = din("w_in", [DEPTH, D, INW])
    waf_d = din("w_af", [DEPTH, 16, 384]); baf_d = din("b_af", [DEPTH, 384])
    wab_d = din("w_ab", [DEPTH, 16, 384]); bab_d = din("b_ab", [DEPTH, 384])
    gnw_d = din("gla_norm_w", [DEPTH, 384])
    cw_d = din("conv_w", [DEPTH, 3, 256]); cb_d = din("conv_b", [DEPTH, 256])
    wout_d = din("w_out", [DEPTH, D, D]); wq_d = din("peer_wq", [DEPTH, D, D])
    k1_d = din("peer_k1", [DEPTH, 8, 128, 64]); k2_d = din("peer_k2", [DEPTH, 8, 128, 64])
    pu_d = din("peer_u", [DEPTH, 16384, D]); pv_d = din("peer_v", [DEPTH, 16384, D])
    fnw_d = din("final_norm_w", [D])
    y_d = dout("y", [T, D]); nk_d = dout("nk", [DEPTH, T, 384]); nv_d = dout("nv", [DEPTH, T, 384])
    sf_d = dout("sf", [DEPTH, 8, 6, 64, 64]); sbo_d = dout("sbo", [DEPTH, 8, 6, 64, 64])

    P = Prog(nc)
    NSC = dict(allow_slow_non_contiguous=True)
    x = P.sb("x", [128, NT, D], F32); rx = [Res() for _ in range(NT)]
    hT = P.sb("hT", [128, 8, T], BF16); rhT = Res()
    ident = P.sb("ident", [128, 128], BF16); rid = Res()
    identf = P.sb("identf", [128, 128], F32)
    onesf = P.sb("onesf", [128, 128], F32)
    glam = P.sb("glam", [128, 2, 128], F32)
    carry = P.sb("carry", [128, 1], F32)
    carm1 = P.sb("carm1", [128, 1], F32)
    scol = P.sb("scol", [128, 8], F32)
    modcol = P.sb("modcol", [128, 4, 8], F32); rmod = Res()
    acol = P.sb("acol", [128, 2, 8], F32)
    nwcol = P.sb("nwcol", [128, 2, 8], F32)
    gb = P.sb("gb", [128, 2, D], F32); rgb = Res()
    small = P.sb("small", [128, 64], F32); rsmall = Res()
    rconst = Res()
    ar = Arena(P)
    banks = [P.ps(f"bank{i}", [128, 512], F32) for i in range(8)]
    rbank = [Res() for _ in range(8)]
    rr = {"s": 0, "a": 0, "b": 0}

    def bank(pool):
        if pool == "s":
            i = rr["s"] % 3
        elif pool == "a":
            i = 3 + rr["a"] % 3
        else:
            i = 6 + rr["b"] % 2
        rr[pool] += 1
        return banks[i], rbank[i]

    for i in range(NT):
        P.dma(x[:, i, :], x_d[i * 128:(i + 1) * 128, :], writes=[rx[i]])
    P.dma(ident[:], ident_d[:, :], writes=[rid])
    P.dma(glam[:], glam_d.rearrange("a s t -> s a t"), writes=[rconst])
    P.dma(carry[:], carry_d[:, :], writes=[rconst])
    P.dma(scol[:], cvec_d.rearrange("(k p) -> p k", p=128), writes=[rconst], **NSC)
    P.cp(identf[:], ident[:], reads=[rid], writes=[rconst])
    P.memset(onesf[:], 1.0, writes=[rconst])
    P.ts(carm1[:], carry[:], -1.0, None, ALU.add, reads=[rconst], writes=[rconst])
    P.act(scol[:], scol[:], AF.Silu, reads=[rconst], writes=[rconst])

    def rstd_of(src_ap, n, out_ap, rsrc, scratch, rscratch):
        P.act(scratch, src_ap, AF.Square, reads=rsrc, writes=[rscratch, rsmall], accum_out=out_ap)
        P.ts(out_ap, out_ap, 1.0 / n, 1e-6, ALU.mult, ALU.add, reads=[rsmall], writes=[rsmall])
        P.act(out_ap, out_ap, AF.Sqrt, reads=[rsmall], writes=[rsmall])
        P.op("vector", lambda e: e.reciprocal(out=out_ap, in_=out_ap), [rsmall], [rsmall])

    def norm_to_hT(which):
        ar.reset()
        sq = ar.f32([128, D]); rsq = Res()
        xn = ar.bf16([128, 2, D]); rxn = [Res(), Res()]
        for i in range(NT):
            rstd_of(x[:, i, :], D, small[:, 0:1], [rx[i]], sq, rsq)
            P.ts(xn[:, i % 2, :], x[:, i, :], small[:, 0:1], None, ALU.mult, reads=[rx[i], rsmall], writes=[rxn[i % 2]])
            for hf in range(2):
                bk, rb = bank("s")
                for c4 in range(4):
                    c = hf * 4 + c4
                    P.mm(bk[:, c4 * 128:(c4 + 1) * 128], xn[:, i % 2, c * 128:(c + 1) * 128], ident[:], True, True,
                         reads=[rxn[i % 2], rid], writes=[rb])
                for c4 in range(4):
                    c = hf * 4 + c4
                    P.ts(hT[:, c, i * 128:(i + 1) * 128], bk[:, c4 * 128:(c4 + 1) * 128], acol[:, which, c:c + 1],
                         modcol[:, 2 * which, c:c + 1], ALU.mult, ALU.add, reads=[rb, rmod], writes=[rhT],
                         eng=("vector" if c4 % 2 == 0 else "gpsimd") if False else "vector")

    wst = [None, None]; rwst = [Res(), Res()]; wbf = [None, None]; rwbf = [Res(), Res()]
    wcnt = [0]

    def alloc_w():
        for k in range(2):
            wst[k] = ar.f32([128, 8, 128])
            wbf[k] = ar.bf16([128, 8, 128])

    def wcols(dram_ap, n, dst=None, rdst=None, dcol=0):
        k = wcnt[0] % 2
        wcnt[0] += 1
        P.dma(wst[k][:, :, 0:n], dram_ap.rearrange("(c p) n -> p c n", p=128), writes=[rwst[k]])
        if dst is None:
            dst, rdst, dcol = wbf[k], rwbf[k], 0
        P.cp(dst[:, :, dcol:dcol + n], wst[k][:, :, 0:n], reads=[rwst[k]], writes=[rdst], eng="gpsimd")
        return dst, rdst

    def proj_fm(wb, rwb, wcol0, m, evac):
        for tg in range(4):
            bk, rb = bank("s")
            for c in range(8):
                P.mm(bk[0:m, :], wb[:, c, wcol0:wcol0 + m], hT[:, c, tg * 512:(tg + 1) * 512], c == 0, c == 7,
                     reads=[rwb, rhT], writes=[rb])
            evac(tg, bk, rb)

    def outproj(cat, rcat, row0, nch, l):
        wo_st = ar.f32([128, D]); rwo_st = Res()
        wob = ar.bf16([128, nch, D]); rwob = Res()
        for j in range(nch):
            P.dma(wo_st[:], wout_d[l, row0 + j * 128: row0 + (j + 1) * 128, :], writes=[rwo_st])
            P.tt(wob[:, j, :], wo_st[:], gb[:, 0, :], ALU.mult, reads=[rwo_st, rgb], writes=[rwob], eng="gpsimd")
        for i in range(NT):
            for hf in range(2):
                bk, rb = bank("b")
                for j in range(nch):
                    P.mm(bk[:, :], cat[:, j, i * 128:(i + 1) * 128], wob[:, j, hf * 512:(hf + 1) * 512], j == 0, j == nch - 1,
                         reads=[rcat, rwob], writes=[rb])
                P.tt(x[:, i, hf * 512:(hf + 1) * 512], x[:, i, hf * 512:(hf + 1) * 512], bk[:, :], ALU.add,
                     reads=[rx[i], rb], writes=[rx[i]])

    for l in range(nlayers):
        P.barrier()
        ar.reset()
        wa = [ar.f32([128, 8, 512]), ar.f32([128, 8, 512])]; rwa = [Res(), Res()]
        row = ar.f32([1, 512]); rrow = Res()
        brow = ar.f32([1, 512]); rbrow = Res()
        P.dma(nwcol[:, 0, :], n1_d[l].rearrange("(k p) -> p k", p=128), writes=[rmod], **NSC)
        P.dma(nwcol[:, 1, :], n2_d[l].rearrange("(k p) -> p k", p=128), writes=[rmod], **NSC)
        for g in range(12):
            k = g % 2
            P.dma(wa[k][:], w_ada_d[l, :, g * 512:(g + 1) * 512].rearrange("(c p) n -> p c n", p=128), writes=[rwa[k]])
            P.dma(brow[0:1, :], b_ada_d[l:l + 1, g * 512:(g + 1) * 512], writes=[rbrow])
            bk, rb = bank("s")
            for c in range(8):
                P.mm(bk[0:1, :], scol[:, c:c + 1], wa[k][:, c, :], c == 0, c == 7, reads=[rconst, rwa[k]], writes=[rb])
            P.tt(row[0:1, :], bk[0:1, :], brow[0:1, :], ALU.add, reads=[rb, rbrow], writes=[rrow])
            vec = g // 2
            if vec in (0, 1, 3, 4):
                mi = {0: 0, 1: 1, 3: 2, 4: 3}[vec]
                bk2, rb2 = bank("s")
                for j in range(4):
                    P.mm(bk2[:, j:j + 1], row[0:1, j * 128:(j + 1) * 128], onesf[0:1, 0:1], True, True,
                         reads=[rrow, rconst], writes=[rb2])
                P.cp(modcol[:, mi, k * 4:(k + 1) * 4], bk2[:, 0:4], reads=[rb2], writes=[rmod])
            else:
                gi = 0 if vec == 2 else 1
                bk2, rb2 = bank("s")
                P.mm(bk2[:, :], onesf[0:1, 0:128], row[0:1, :], True, True, reads=[rrow, rconst], writes=[rb2])
                P.cp(gb[:, gi, k * 512:(k + 1) * 512], bk2[:, :], reads=[rb2], writes=[rgb])
        for w_ in range(2):
            P.stt(acol[:, w_, :], modcol[:, 2 * w_ + 1, :], 1.0, nwcol[:, w_, :], ALU.add, ALU.mult,
                  reads=[rmod], writes=[rmod])
        P.barrier()
        norm_to_hT(0)

        P.mute = "C" not in phases
        P.barrier()
        ar.reset()
        alloc_w()
        zc = ar.f32([128, 3, T]); rzc = Res()
        u = ar.f32([128, T]); ru = Res()
        yv = ar.f32([128, T]); ryv = Res()
        catC = ar.bf16([128, 2, T]); rcatC = Res()
        cwc = ar.f32([128, 2, 4]); rcwc = Res()
        wn = ar.f32([128, 2, 2]); rwn = Res()
        for j in range(2):
            for k_ in range(3):
                P.dma(cwc[:, j, k_:k_ + 1], cw_d[l, k_, j * 128:(j + 1) * 128].rearrange("(p o) -> p o", o=1), writes=[rcwc], **NSC)
            P.dma(cwc[:, j, 3:4], cb_d[l, j * 128:(j + 1) * 128].rearrange("(p o) -> p o", o=1), writes=[rcwc], **NSC)
        for j in range(2):
            P.ts(wn[:, j, 0:1], cwc[:, j, 0:1], carm1[:, 0:1], None, ALU.mult, reads=[rcwc, rconst], writes=[rwn])
            P.ts(wn[:, j, 1:2], cwc[:, j, 2:3], carm1[:, 0:1], None, ALU.mult, reads=[rcwc, rconst], writes=[rwn])
        for j in range(2):
            for wi, c0 in enumerate((CH0, CB0, CC0)):
                wb, rwb = wcols(w_in_d[l, :, c0 + j * 128:c0 + (j + 1) * 128], 128)
                proj_fm(wb, rwb, 0, 128, lambda tg, bk, rb, wi=wi: P.cp(zc[:, wi, tg * 512:(tg + 1) * 512], bk[:, :],
                                                                      reads=[rb], writes=[rzc], eng="scalar"))
            P.tt(u[:], zc[:, 2, :], zc[:, 0, :], ALU.mult, reads=[rzc], writes=[ru], eng="gpsimd")
            P.ts(yv[:], u[:], cwc[:, j, 1:2], None, ALU.mult, reads=[ru, rcwc], writes=[ryv])
            P.stt(yv[:, 1:T], u[:, 0:T - 1], cwc[:, j, 0:1], yv[:, 1:T], ALU.mult, ALU.add, reads=[ru, rcwc, ryv], writes=[ryv])
            P.stt(yv[:, 0:T - 1], u[:, 1:T], cwc[:, j, 2:3], yv[:, 0:T - 1], ALU.mult, ALU.add, reads=[ru, rcwc, ryv], writes=[ryv])
            P.stt(yv[:, 256:T:256], u[:, 255:T - 1:256], wn[:, j, 0:1], yv[:, 256:T:256], ALU.mult, ALU.add,
                  reads=[ru, rwn, ryv], writes=[ryv])
            P.stt(yv[:, 255:T - 1:256], u[:, 256:T:256], wn[:, j, 1:2], yv[:, 255:T - 1:256], ALU.mult, ALU.add,
                  reads=[ru, rwn, ryv], writes=[ryv])
            P.stt(catC[:, j, :], yv[:], cwc[:, j, 3:4], zc[:, 1, :], ALU.add, ALU.mult, reads=[ryv, rcwc, rzc], writes=[rcatC])
        outproj(catC, rcatC, 384, 2, l)

        P.mute = "N" not in phases
        P.barrier()
        ar.reset()
        alloc_w()
        namask = ar.f32([128, NTYPE, 128]); rnamask = Res()
        P.dma(namask[:], namask_d.rearrange("t k q -> k t q"), writes=[rnamask])
        nab = ar.f32([128, 26, 128]); rnab = Res()
        BM = ar.bf16([128, 2, NTYPE, 128]); rBM = Res()
        qT = ar.bf16([128, T]); rqT = Res()
        kT = ar.bf16([128, T]); rkT = Res()
        wkv = ar.bf16([128, 8, 256]); rwkv = Res()
        kvst = [ar.f32([128, 256]), ar.f32([128, 256])]; rkvst = [Res(), Res()]
        vaug = ar.bf16([128, NT, 2, 66]); rvaug = Res()
        cst = ar.f32([128, 2, 2, 64]); rcst = Res()
        kcb = ar.bf16([128, 2, 128]); rkcb = Res()
        kcT = ar.bf16([128, 256]); rkcT = Res()
        vca = ar.bf16([128, 2, 2, 66]); rvca = Res()
        pT = [ar.bf16([128, 128]) for _ in range(3)]; rpT = [Res() for _ in range(3)]
        opair = [ar.bf16([128, 128]) for _ in range(2)]; ropair = [Res(), Res()]
        catN = ar.bf16([128, 1, T]); rcatN = Res()
        P.memset(vaug[:, :, :, 64:66], 1.0, writes=[rvaug])
        P.memset(vca[:, :, :, 64:66], 1.0, writes=[rvca])
        pcnt = 0
        for p in range(3):
            wb, rwb = wcols(w_in_d[l, :, QN0 + p * 128:QN0 + (p + 1) * 128], 128)
            proj_fm(wb, rwb, 0, 128, lambda tg, bk, rb: P.act(qT[:, tg * 512:(tg + 1) * 512], bk[:, :], AF.Copy,
                                                             reads=[rb], writes=[rqT], scale=0.125))
            wb, rwb = wcols(w_in_d[l, :, KN0 + p * 128:KN0 + (p + 1) * 128], 128)
            proj_fm(wb, rwb, 0, 128, lambda tg, bk, rb: P.cp(kT[:, tg * 512:(tg + 1) * 512], bk[:, :],
                                                            reads=[rb], writes=[rkT], eng="scalar"))
            wcols(w_in_d[l, :, KN0 + p * 128:KN0 + (p + 1) * 128], 128, wkv, rwkv, 0)
            wcols(w_in_d[l, :, VN0 + p * 128:VN0 + (p + 1) * 128], 128, wkv, rwkv, 128)
            for i in range(NT):
                bk, rb = bank("s")
                for c in range(8):
                    P.mm(bk[:, 0:256], hT[:, c, i * 128:(i + 1) * 128], wkv[:, c, :], c == 0, c == 7,
                         reads=[rhT, rwkv], writes=[rb])
                kk_ = i % 2
                P.cp(kvst[kk_][:], bk[:, 0:256], reads=[rb], writes=[rkvst[kk_]], eng="scalar")
                P.dma(nk_d[l, i * 128:(i + 1) * 128, p * 128:(p + 1) * 128], kvst[kk_][:, 0:128], reads=[rkvst[kk_]], q="gpsimd")
                P.dma(nv_d[l, i * 128:(i + 1) * 128, p * 128:(p + 1) * 128], kvst[kk_][:, 128:256], reads=[rkvst[kk_]], q="gpsimd")
                P.cp(vaug[:, i, :, 0:64], kvst[kk_][:, 128:256].rearrange("p (h d) -> p h d", h=2),
                     reads=[rkvst[kk_]], writes=[rvaug])
            for kt in range(2):
                P.dma(cst[:, kt, :, :], kctx_d[l, 2 * p:2 * p + 2, kt * 128:(kt + 1) * 128, :].rearrange("h q d -> q h d"), writes=[rcst])
            P.cp(kcb[:].rearrange("p kt (h d) -> p kt h d", h=2), cst[:], reads=[rcst], writes=[rkcb])
            for kt in range(2):
                bk, rb = bank("s")
                P.mm(bk[:, 0:128], kcb[:, kt, :], ident[:], True, True, reads=[rkcb, rid], writes=[rb])
                P.cp(kcT[:, kt * 128:(kt + 1) * 128], bk[:, 0:128], reads=[rb], writes=[rkcT], eng="scalar")
            for kt in range(2):
                P.dma(cst[:, kt, :, :], vctx_d[l, 2 * p:2 * p + 2, kt * 128:(kt + 1) * 128, :].rearrange("h q d -> q h d"), writes=[rcst])
            P.cp(vca[:, :, :, 0:64], cst[:], reads=[rcst], writes=[rvca])
            for hh in range(2):
                P.dma(nab[:], nabias_d[l, 2 * p + hh].rearrange("t k q -> k t q"), writes=[rnab])
                P.tt(BM[:, hh, 0:26, :], nab[:], namask[:, 0:26, :], ALU.add, reads=[rnab, rnamask], writes=[rBM])
                P.cp(BM[:, hh, 26, :], namask[:, 26, :], reads=[rnamask], writes=[rBM])
            for i in range(NT):
                for hh in range(2):
                    hp = slice(hh * 64, hh * 64 + 64)
                    blocks = [("l", kt, ty) for kt, ty in na_blocks(i)] + [("c", 0, 26), ("c", 1, 26)]
                    bo, rbo = bank("a")
                    for bi, (src, kt, ty) in enumerate(blocks):
                        bs, rbs = bank("s")
                        ksrc, rks = (kT, rkT) if src == "l" else (kcT, rkcT)
                        P.mm(bs[:, 0:128], ksrc[hp, kt * 128:(kt + 1) * 128], qT[hp, i * 128:(i + 1) * 128], True, False,
                             reads=[rks, rqT], writes=[rbs])
                        P.mm(bs[:, 0:128], ident[:], BM[:, hh, ty, :], False, True, reads=[rid, rBM], writes=[rbs])
                        pk = pcnt % 3
                        pcnt += 1
                        P.act(pT[pk][:], bs[:, 0:128], AF.Exp, reads=[rbs], writes=[rpT[pk]])
                        vsrc, rvs = (vaug[:, kt, hh, 0:65], rvaug) if src == "l" else (vca[:, kt, hh, 0:65], rvca)
                        P.mm(bo[:, 0:65], pT[pk][:], vsrc, bi == 0, bi == len(blocks) - 1, reads=[rpT[pk], rvs], writes=[rbo])
                    P.op("vector", lambda e, bo=bo: e.reciprocal(out=small[:, 1:2], in_=bo[:, 64:65]), [rbo], [rsmall])
                    P.ts(opair[i % 2][:, hp], bo[:, 0:64], small[:, 1:2], None, ALU.mult, reads=[rbo, rsmall], writes=[ropair[i % 2]])
                bk, rb = bank("b")
                P.mm(bk[:, 0:128], opair[i % 2][:], ident[:], True, True, reads=[ropair[i % 2], rid], writes=[rb])
                P.cp(catN[:, 0, i * 128:(i + 1) * 128], bk[:, 0:128], reads=[rb], writes=[rcatN], eng="scalar")
            off_save = ar.off
            outproj(catN, rcatN, 640 + p * 128, 1, l)
            ar.off = off_save
            P.barrier()

        P.mute = "G" not in phases
        ar.reset()
        alloc_w()
        lrT = [ar.bf16([16, T]), ar.bf16([16, T])]; rlrT = Res()
        wa16 = ar.f32([16, 2, 384]); rwa16 = Res()
        wab16 = ar.bf16([16, 2, 384]); rwab16 = Res()
        P.dma(wa16[0:16, 0, :], waf_d[l], writes=[rwa16])
        P.dma(wa16[0:16, 1, :], wab_d[l], writes=[rwa16])
        P.cp(wab16[0:16], wa16[0:16], reads=[rwa16], writes=[rwab16])
        for d_ in range(2):
            wb, rwb = wcols(w_in_d[l, :, LRF0 + d_ * 16:LRF0 + (d_ + 1) * 16], 16)
            proj_fm(wb, rwb, 0, 16, lambda tg, bk, rb, d_=d_: P.cp(lrT[d_][0:16, tg * 512:(tg + 1) * 512], bk[0:16, :],
                                                                  reads=[rb], writes=[rlrT], eng="scalar"))
        gqT = ar.bf16([128, T]); rgq = Res()
        gkT = ar.bf16([128, T]); rgk = Res()
        a_off = ar.off
        A_ = ar.f32([128, T]); rA = Res()
        B_ = ar.f32([128, T]); rB = Res()
        Eb = ar.bf16([128, T]); rEb = Res()
        totc = ar.f32([128, NT]); rtotc = Res()
        _qd = ar.bf16([128, T]); _rqd = Res(); qd = [_qd, _qd]; rqd = [_rqd, _rqd]
        _kd = ar.bf16([128, T]); _rkd = Res(); kd = [_kd, _kd]; rkd = [_rkd, _rkd]
        _kdec = ar.bf16([128, T]); _rkdec = Res(); kdec = [_kdec, _kdec]; rkdec = [_rkdec, _rkdec]
        _kdt = ar.bf16([128, NT, 128]); _rkdt = Res(); kdtok = [_kdt, _kdt]; rkdtok = [_rkdt, _rkdt]
        elast = ar.f32([128, 2, NT]); rel = Res()
        wvg = ar.bf16([128, 8, 256]); rwvg = Res()
        v_b = ar.bf16([128, NT, 128]); rvb = Res()
        sg = ar.bf16([128, NT, 128]); rsg = Res()
        of = ar.f32([128, NT, 128]); rof = Res()
        ycat = sg; rycat = rsg
        catG = gqT.rearrange("p (o t) -> p o t", o=1); rcatG = rgq
        nb = ar.f32([128, 2]); rnb = Res()
        gwb = ar.f32([128, 128]); rgwb = Res()
        S = ar.f32([128, 64]); rS = Res()
        Sb = ar.bf16([128, 64]); rSb = Res()
        s0 = ar.f32([128, 2, 64]); rs0 = Res()
        sst = ar.f32([128, 2, 8, 64]); rsst = Res()
        attm = [ar.bf16([128, 2, 128]), ar.bf16([128, 2, 128])]; rattm = [Res(), Res()]
        ot = ar.f32([128, 128]); rot = Res()
        osq = ar.f32([128, 128]); rosq = Res()
        Bv = B_.rearrange("p (c j) -> p c j", j=128)
        totb = totc[:, :].unsqueeze(2).to_broadcast([128, NT, 128])
        Av = A_.rearrange("p (c j) -> p c j", j=128)
        for p in range(3):
            cs = slice(p * 128, (p + 1) * 128)
            wb, rwb = wcols(w_in_d[l, :, QA0 + p * 128:QA0 + (p + 1) * 128], 128)
            proj_fm(wb, rwb, 0, 128, lambda tg, bk, rb: P.act(gqT[:, tg * 512:(tg + 1) * 512], bk[:, :], AF.Copy,
                                                             reads=[rb], writes=[rgq], scale=0.125))
            wb, rwb = wcols(w_in_d[l, :, KA0 + p * 128:KA0 + (p + 1) * 128], 128)
            proj_fm(wb, rwb, 0, 128, lambda tg, bk, rb: P.cp(gkT[:, tg * 512:(tg + 1) * 512], bk[:, :],
                                                            reads=[rb], writes=[rgk], eng="scalar"))
            wcols(w_in_d[l, :, VA0 + p * 128:VA0 + (p + 1) * 128], 128, wvg, rwvg, 0)
            wcols(w_in_d[l, :, GA0 + p * 128:GA0 + (p + 1) * 128], 128, wvg, rwvg, 128)
            for i in range(NT):
                bk, rb = bank("s")
                for c in range(8):
                    P.mm(bk[:, 0:256], hT[:, c, i * 128:(i + 1) * 128], wvg[:, c, :], c == 0, c == 7,
                         reads=[rhT, rwvg], writes=[rb])
                P.cp(v_b[:, i, :], bk[:, 0:128], reads=[rb], writes=[rvb], eng="scalar")
                P.act(sg[:, i, :], bk[:, 128:256], AF.Silu, reads=[rb], writes=[rsg])
            if os.environ.get("GSTOP") == "b":
                P.mute = True
            P.dma(nb[:, 0:1], baf_d[l, cs].rearrange("(p o) -> p o", o=1), writes=[rnb], **NSC)
            P.dma(nb[:, 1:2], bab_d[l, cs].rearrange("(p o) -> p o", o=1), writes=[rnb], **NSC)
            P.ts(nb[:], nb[:], -1.0, None, ALU.mult, reads=[rnb], writes=[rnb])
            P.dma(gwb[:], gnw_d[l, cs].partition_broadcast(128), writes=[rgwb])
            P.dma(s0[:, 0, :], s0f_d[l, 2 * p:2 * p + 2].rearrange("h k v -> (h k) v"), writes=[rs0])
            P.dma(s0[:, 1, :], s0b_d[l, 2 * p:2 * p + 2].rearrange("h k v -> (h k) v"), writes=[rs0])
            for d_ in range(2):
                for tg in range(4):
                    bk, rb = bank("s")
                    P.mm(bk[:, :], wab16[0:16, d_, cs], lrT[d_][0:16, tg * 512:(tg + 1) * 512], True, True,
                         reads=[rwab16, rlrT], writes=[rb])
                    P.act(A_[:, tg * 512:(tg + 1) * 512], bk[:, :], AF.Exp, reads=[rb, rnb], writes=[rA],
                          scale=-1.0, bias=nb[:, d_:d_ + 1])
                P.act(A_[:], A_[:], AF.Ln, reads=[rA], writes=[rA], bias=1.0)
                for c in range(NT):
                    P.op("vector", lambda e, c=c: e.tensor_tensor_scan(out=B_[:, c * 128:(c + 1) * 128], data0=onesf[:, :],
                                                                      data1=A_[:, c * 128:(c + 1) * 128], initial=0.0,
                                                                      op0=ALU.mult, op1=ALU.add), [rA, rconst], [rB])
                P.cp(totc[:, :], B_[:, 127:T:128], reads=[rB], writes=[rtotc])
                if d_ == 0:
                    RC, rRC, FR, rFR, FRv = B_, rB, A_, rA, Av
                else:
                    P.tt(A_[:], A_[:], B_[:], ALU.subtract, reads=[rA, rB], writes=[rA], eng="gpsimd")
                    P.tt(Av, Av, totb, ALU.add, reads=[rA, rtotc], writes=[rA], eng="gpsimd")
                    RC, rRC, FR, rFR, FRv = A_, rA, B_, rB, Bv
                RCv = RC.rearrange("p (c j) -> p c j", j=128)
                P.act(Eb[:], RC[:], AF.Exp, reads=[rRC], writes=[rEb], scale=-1.0 / 16)
                P.tt(qd[d_][:], gqT[:], Eb[:], ALU.mult, reads=[rgq, rEb], writes=[rqd[d_]])
                P.act(Eb[:], RC[:], AF.Exp, reads=[rRC], writes=[rEb], scale=1.0 / 16)
                P.tt(kd[d_][:], gkT[:], Eb[:], ALU.mult, reads=[rgk, rEb], writes=[rkd[d_]])
                P.tt(FRv, RCv, totb, ALU.subtract, reads=[rRC, rtotc], writes=[rFR], eng="gpsimd")
                P.act(Eb[:], FR[:], AF.Exp, reads=[rFR], writes=[rEb], scale=1.0 / 16)
                P.tt(kdec[d_][:], gkT[:], Eb[:], ALU.mult, reads=[rgk, rEb], writes=[rkdec[d_]])
                P.act(elast[:, d_, :], totc[:, :], AF.Exp, reads=[rtotc], writes=[rel], scale=-1.0 / 16)
                for c in range(NT):
                    bk, rb = bank("s")
                    P.mm(bk[:, 0:128], kdec[d_][:, c * 128:(c + 1) * 128], ident[:], True, True, reads=[rkdec[d_], rid], writes=[rb])
                    P.cp(kdtok[d_][:, c, :], bk[:, 0:128], reads=[rb], writes=[rkdtok[d_]], eng="scalar")
                if os.environ.get("GSTOP") == "c":
                    P.mute = True
                order = list(range(NT)) if d_ == 0 else list(range(NT - 1, -1, -1))
                P.cp(S[:], s0[:, d_, :], reads=[rs0], writes=[rS])
                for n_, c in enumerate(order):
                    ccs = slice(c * 128, (c + 1) * 128)
                    if n_ > 0 and n_ % 2 == 0:
                        P.ts(S[:], S[:], carry[:, 0:1], None, ALU.mult, reads=[rS, rconst], writes=[rS])
                    P.cp(Sb[:], S[:], reads=[rS], writes=[rSb], eng="scalar")
                    am = attm[n_ % 2]; ram = rattm[n_ % 2]
                    GSK = os.environ.get("GSKIP", "")
                    _m0 = P.mute
                    bos = []
                    for hh in range(2):
                        hp = slice(hh * 64, hh * 64 + 64)
                        ba, rba = bank("s")
                        P.mm(ba[:, 0:128], kd[d_][hp, ccs], qd[d_][hp, ccs], True, True,
                             reads=[rkd[d_], rqd[d_]], writes=[rba])
                        P.tt(am[:, hh, :], ba[:, 0:128], glam[:, d_, :], ALU.mult, reads=[rba, rconst], writes=[ram])
                    for hh in range(2):
                        hp = slice(hh * 64, hh * 64 + 64)
                        bo, rbo = bank("a")
                        bos.append((bo, rbo))
                        P.mm(bo[:, 0:64], am[:, hh, :], v_b[:, c, hh * 64:(hh + 1) * 64], True, False,
                             reads=[ram, rvb], writes=[rbo])
                        P.mm(bo[:, 0:64], qd[d_][hp, ccs], Sb[hp, :], False, True,
                             reads=[rqd[d_], rSb], writes=[rbo])
                    if d_ == 0:
                        for hh in range(2):
                            P.cp(of[:, c, hh * 64:(hh + 1) * 64], bos[hh][0][:, 0:64], reads=[bos[hh][1]], writes=[rof], eng="scalar")
                    else:
                        for hh in range(2):
                            P.tt(ot[:, hh * 64:(hh + 1) * 64], of[:, c, hh * 64:(hh + 1) * 64], bos[hh][0][:, 0:64], ALU.add,
                                 reads=[rof, bos[hh][1]], writes=[rot])
                        P.tt(osq[:], ot[:], ot[:], ALU.mult, reads=[rot], writes=[rosq], eng="gpsimd")
                        P.op("vector", lambda e: e.tensor_reduce(out=small[:, 2:4], in_=osq[:].rearrange("p (h d) -> p h d", h=2),
                                                                 axis=AX.X, op=ALU.add), [rosq], [rsmall])
                        P.ts(small[:, 2:4], small[:, 2:4], 1.0 / 64, 1e-6, ALU.mult, ALU.add, reads=[rsmall], writes=[rsmall])
                        P.act(small[:, 2:4], small[:, 2:4], AF.Sqrt, reads=[rsmall], writes=[rsmall])
                        P.op("vector", lambda e: e.reciprocal(out=small[:, 2:4], in_=small[:, 2:4]), [rsmall], [rsmall])
                        P.tt(ot[:].rearrange("p (h d) -> p h d", h=2), ot[:].rearrange("p (h d) -> p h d", h=2),
                             small[:, 2:4].unsqueeze(2).to_broadcast([128, 2, 64]), ALU.mult, reads=[rot, rsmall], writes=[rot])
                        P.tt(ot[:], ot[:], gwb[:], ALU.mult, reads=[rot, rgwb], writes=[rot])
                        P.tt(ycat[:, c, :], ot[:], sg[:, c, :], ALU.mult, reads=[rot, rsg], writes=[rycat])
                    P.mute = _m0 or ("s" in GSK)
                    bd, rbd = bank("b")
                    P.mm(bd[:, 0:128], kdtok[d_][:, c, :], v_b[:, c, :], True, True, reads=[rkdtok[d_], rvb], writes=[rbd])
                    for hh in range(2):
                        hp = slice(hh * 64, hh * 64 + 64)
                        P.stt(S[hp, :], S[hp, :], elast[hp, d_, c:c + 1], bd[hp, hh * 64:(hh + 1) * 64], ALU.mult, ALU.add,
                              reads=[rS, rel, rbd], writes=[rS])
                    if n_ % 2 == 1:
                        P.cp(sst[:, d_, c // 2, :], S[:], reads=[rS], writes=[rsst], eng="gpsimd")
                    P.mute = _m0
            if os.environ.get("GSTOP") == "d":
                P.mute = True
            for c in range(NT):
                bk, rb = bank("s")
                P.mm(bk[:, 0:128], ycat[:, c, :], ident[:], True, True, reads=[rycat, rid], writes=[rb])
                P.cp(catG[:, 0, c * 128:(c + 1) * 128], bk[:, 0:128], reads=[rb], writes=[rcatG], eng="scalar")
            P.dma(sf_d[l, :, 2 * p:2 * p + 2].rearrange("g h k v -> (h k) g v"), sst[:, 0, :, :], reads=[rsst], q="gpsimd")
            P.dma(sbo_d[l, :, 2 * p:2 * p + 2].rearrange("g h k v -> (h k) g v"), sst[:, 1, :, :], reads=[rsst], q="gpsimd")
            P.barrier()
            off_save = ar.off
            ar.off = a_off
            outproj(catG, rcatG, p * 128, 1, l)
            ar.off = off_save
            P.barrier()

        P.mute = "P" not in phases
        P.barrier()
        norm_to_hT(1)
        P.barrier()
        ar.reset()
        pq = ar.bf16([128, 8, T]); rpq = Res()
        kk = ar.bf16([128, 8, 128]); rkk = Res()
        Kc = ar.bf16([128, 8, 4, 128]); rKc = Res()
        tauS = ar.f32([128, NT, 8]); rtau = Res()
        negcS = ar.f32([128, NT, 8]); rnegc = Res()
        pk_off = ar.off
        alloc_w()
        kst = ar.f32([128, 8, 128]); rkst = Res()
        kstb = ar.bf16([128, 8, 128]); rkstb = Res()
        for hq in range(8):
            wb, rwb = wcols(wq_d[l, :, hq * 128:(hq + 1) * 128], 128)
            proj_fm(wb, rwb, 0, 128, lambda tg, bk, rb, hq=hq: P.cp(pq[:, hq, tg * 512:(tg + 1) * 512], bk[:, :],
                                                                   reads=[rb], writes=[rpq], eng="scalar"))
        P.dma(kst[:, :, 0:64], k1_d[l].rearrange("h n d -> n h d"), writes=[rkst])
        P.dma(kst[:, :, 64:128], k2_d[l].rearrange("h n d -> n h d"), writes=[rkst])
        P.cp(kstb[:], kst[:], reads=[rkst], writes=[rkstb])
        for h in range(8):
            bk, rb = bank("s")
            P.mm(bk[:, 0:128], kstb[:, h, :], ident[:], True, True, reads=[rkstb, rid], writes=[rb])
            P.cp(kk[:, h, :], bk[:, 0:128], reads=[rb], writes=[rkk], eng="scalar")
        P.cp(Kc[64:128], kk[64:128, :, :].unsqueeze(2).to_broadcast([64, 8, 4, 128]), reads=[rkk], writes=[rKc], eng="gpsimd")
        ssb = ar.f32([128, 16, 128]); rssb = Res()
        tmpa = ar.f32([128, 128]); rtmpa = Res()
        v16 = ar.f32([128, 16, 16]); rv16 = Res()
        cand = ar.f32([128, 8, 256]); rcand = Res()
        cta = ar.f32([128, 256]); rcta = Res()
        c16 = ar.f32([128, 8, 16]); rc16 = Res()
        e16 = ar.f32([128, 8, 16]); re16 = Res()
        zz = ar.f32([128, 8]); rzz = Res()
        for i in range(NT):
            tcs = slice(i * 128, (i + 1) * 128)
            for hg in range(2):
                b0, rb0 = bank("s")
                b1, rb1 = bank("s")
                for h4 in range(4):
                    h = hg * 4 + h4
                    P.mm(b0[:, h4 * 128:(h4 + 1) * 128], pq[0:64, h, tcs], kk[0:64, h, :], True, True, reads=[rpq, rkk], writes=[rb0])
                for h4 in range(4):
                    h = hg * 4 + h4
                    P.mm(b1[:, h4 * 128:(h4 + 1) * 128], pq[64:128, h, tcs], kk[64:128, h, :], True, True, reads=[rpq, rkk], writes=[rb1])
                sv = ssb[:, hg * 8:hg * 8 + 8, :].rearrange("p (h s) k -> p h s k", s=2)
                P.cp(sv[:, :, 0, :], b0[:, :].rearrange("p (a b) -> p a b", a=4), reads=[rb0], writes=[rssb], eng="scalar")
                P.cp(sv[:, :, 1, :], b1[:, :].rearrange("p (a b) -> p a b", a=4), reads=[rb1], writes=[rssb], eng="scalar")
            for r in range(16):
                P.op("vector", lambda e, r=r: e.max(out=v16[:, r, 0:8], in_=ssb[:, r, :]), [rssb], [rv16])
                P.op("vector", lambda e, r=r: e.match_replace(out=tmpa[:], in_to_replace=v16[:, r, 0:8], in_values=ssb[:, r, :], imm_value=-1e30),
                     [rssb, rv16], [rtmpa])
                P.op("vector", lambda e, r=r: e.max(out=v16[:, r, 8:16], in_=tmpa[:]), [rtmpa], [rv16])
            v4 = v16[:, :, :].rearrange("p (h s) k -> p h s k", s=2)
            P.tt(cand[:].rearrange("p h (a b) -> p h a b", a=16),
                 v4[:, :, 0, :].unsqueeze(3).to_broadcast([128, 8, 16, 16]),
                 v4[:, :, 1, :].unsqueeze(2).to_broadcast([128, 8, 16, 16]), ALU.add, reads=[rv16], writes=[rcand], eng="gpsimd")
            for h in range(8):
                P.op("vector", lambda e, h=h: e.max(out=c16[:, h, 0:8], in_=cand[:, h, :]), [rcand], [rc16])
                P.op("vector", lambda e, h=h: e.match_replace(out=cta[:], in_to_replace=c16[:, h, 0:8], in_values=cand[:, h, :], imm_value=-1e30),
                     [rcand, rc16], [rcta])
                P.op("vector", lambda e, h=h: e.max(out=c16[:, h, 8:16], in_=cta[:]), [rcta], [rc16])
            P.ts(tauS[:, i, :], c16[:, :, 15], -2e-5, None, ALU.add, reads=[rc16], writes=[rtau])
            P.tt(e16[:], c16[:, :, :], c16[:, :, 0:1].to_broadcast([128, 8, 16]), ALU.subtract, reads=[rc16], writes=[re16])
            P.act(e16[:], e16[:], AF.Exp, reads=[re16], writes=[re16])
            P.op("vector", lambda e: e.tensor_reduce(out=zz[:], in_=e16[:], axis=AX.X, op=ALU.add), [re16], [rzz])
            P.act(zz[:], zz[:], AF.Ln, reads=[rzz], writes=[rzz])
            P.tt(zz[:], zz[:], c16[:, :, 0], ALU.add, reads=[rzz, rc16], writes=[rzz])
            P.ts(negcS[:, i, :], zz[:], -1.0, None, ALU.mult, reads=[rzz], writes=[rnegc])
        P.tt(tauS[:], tauS[:], negcS[:], ALU.add, reads=[rtau, rnegc], writes=[rtau])
        P.act(tauS[:], tauS[:], AF.Exp, reads=[rtau], writes=[rtau])
        P.barrier()
        ar.off = pk_off
        ust = ar.f32([128, D]); rust = [Res(), Res()]
        vst = ar.f32([128, D]); rvst = [Res(), Res()]
        ub = ar.bf16([128, 2, D]); rub = [Res(), Res()]
        UT = ar.bf16([128, 8, 512]); rUT = Res()
        Vb = ar.bf16([128, 4, D]); rVb = Res()
        gel = [ar.bf16([128, 4, 512]), ar.bf16([128, 4, 512])]; rgel = [Res(), Res()]
        Ebf = [ar.f32([128, 512]) for _ in range(3)]; rEbf = [Res() for _ in range(3)]
        _Gm = ar.bf16([128, 8, 512]); _rGm = [Res() for _ in range(8)]; Gm = [_Gm, _Gm]; rGm = [_rGm, _rGm]
        _AT = ar.bf16([128, 4, 128]); _rAT = Res(); AT = [_AT, _AT]; rAT = [_rAT, _rAT]
        _Gsb = ar.bf16([128, 512]); _rGsb = Res(); Gsb = [_Gsb, _Gsb]; rGsb = [_rGsb, _rGsb]
        ecnt = 0
        ucnt = 0
        tl = {}

        def gate_head(i, gs, h):
            nonlocal ecnt
            tcs = slice(i * 128, (i + 1) * 128)
            bk, rb = bank("s")
            P.mm(bk[:, :], pq[:, h, tcs], Kc[:, h, :, :].rearrange("p a b -> p (a b)"), True, True,
                 reads=[rpq, rKc], writes=[rb])
            ek = ecnt % 3
            ecnt += 1
            P.act(Ebf[ek][:], bk[:, :], AF.Exp, reads=[rb, rnegc], writes=[rEbf[ek]], bias=negcS[:, i, h:h + 1], scale=1.0)
            P.stt(Gm[gs][:, h, :], Ebf[ek][:], tauS[:, i, h:h + 1], Ebf[ek][:], ALU.is_ge, ALU.mult,
                  reads=[rtau, rEbf[ek]], writes=[rGm[gs][h]])

        def tail_a(i, gs, gl, t4):
            bsum, rbsum = bank("a")
            for h in range(8):
                P.mm(bsum[:, :], ident[:], Gm[gs][:, h, :], h == 0, h == 7, reads=[rGm[gs][h], rid], writes=[rbsum])
            P.cp(Gsb[gs][:], bsum[:, :], reads=[rbsum], writes=[rGsb[gs]], eng="scalar")

        def tail_b(i, gs, gl, t4):
            bg, rbg = bank("a")
            for k4 in range(4):
                P.mm(bg[:, k4 * 128:(k4 + 1) * 128], Gsb[gs][:, k4 * 128:(k4 + 1) * 128], ident[:], True, True,
                     reads=[rGsb[gs], rid], writes=[rbg])
            P.tt(AT[gs][:], gel[gl][:, :, t4 * 128:(t4 + 1) * 128], bg[:, :].rearrange("p (a b) -> p a b", a=4), ALU.mult,
                 reads=[rgel[gl], rbg], writes=[rAT[gs]])

        def tail_c(i, gs, gl, t4):
            for hf in range(2):
                bo, rbo = bank("b")
                for k4 in range(4):
                    P.mm(bo[:, :], AT[gs][:, k4, :], Vb[:, k4, hf * 512:(hf + 1) * 512], k4 == 0, k4 == 3,
                         reads=[rAT[gs], rVb], writes=[rbo])
                P.tt(x[:, i, hf * 512:(hf + 1) * 512], x[:, i, hf * 512:(hf + 1) * 512], bo[:, :], ALU.add,
                     reads=[rx[i], rbo], writes=[rx[i]])

        for quad in range(32):
            for k4 in range(4):
                r0 = (quad * 4 + k4) * 128
                kb = k4 % 2
                for hf in range(2):
                    hs = slice(hf * 512, (hf + 1) * 512)
                    P.dma(ust[:, hs], pu_d[l, r0:r0 + 128, hs], writes=[rust[hf]])
                    P.dma(vst[:, hs], pv_d[l, r0:r0 + 128, hs], writes=[rvst[hf]])
                for hf in range(2):
                    hs = slice(hf * 512, (hf + 1) * 512)
                    P.act(ub[:, kb, hs], ust[:, hs], AF.Copy, reads=[rust[hf]], writes=[rub[kb]])
                    P.tt(Vb[:, k4, hs], vst[:, hs], gb[:, 1, hs], ALU.mult, reads=[rvst[hf], rgb], writes=[rVb], eng="vector")
                for c2 in range(2):
                    bk, rb = bank("s")
                    for c4 in range(4):
                        c = c2 * 4 + c4
                        P.mm(bk[:, c4 * 128:(c4 + 1) * 128], ub[:, kb, c * 128:(c + 1) * 128], ident[:], True, True,
                             reads=[rub[kb], rid], writes=[rb])
                    P.cp(UT[:, c2 * 4:(c2 + 1) * 4, k4 * 128:(k4 + 1) * 128], bk[:, :].rearrange("p (a b) -> p a b", a=4),
                         reads=[rb], writes=[rUT], eng="vector")
            for h2 in range(2):
                P.cp(Kc[0:64, h2 * 4:(h2 + 1) * 4], kk[0:64, h2 * 4:(h2 + 1) * 4, quad * 4:quad * 4 + 4].unsqueeze(3).to_broadcast([64, 4, 4, 128]),
                     reads=[rkk], writes=[rKc], eng=("vector" if h2 == 0 else "scalar"))
            prev = None
            for tg in range(4):
                gl = tg % 2
                for k4 in range(4):
                    bk, rb = bank("s")
                    for c in range(8):
                        P.mm(bk[:, :], UT[:, c, k4 * 128:(k4 + 1) * 128], hT[:, c, tg * 512:(tg + 1) * 512], c == 0, c == 7,
                             reads=[rUT, rhT], writes=[rb])
                    P.act(gel[gl][:, k4, :], bk[:, :], AF.Gelu, reads=[rb], writes=[rgel[gl]])
                for t4 in range(4):
                    i = tg * 4 + t4
                    gs = ucnt % 2
                    ucnt += 1
                    if prev is not None:
                        tail_a(*prev)
                    for h in range(8):
                        gate_head(i, gs, h)
                        if prev is not None:
                            if h == 3:
                                tail_b(*prev)
                            elif h == 7:
                                tail_c(*prev)
                    prev = (i, gs, gl, t4)
            tail_a(*prev)
            tail_b(*prev)
            tail_c(*prev)

    P.mute = False
    P.barrier()
    ar.reset()
    fwb = ar.f32([128, D]); rfwb = Res()
    sq = ar.f32([128, D]); rsq = Res()
    yo = [ar.f32([128, D]), ar.f32([128, D])]; ryo = [Res(), Res()]
    P.dma(fwb[:], fnw_d.partition_broadcast(128), writes=[rfwb])
    for i in range(NT):
        if raw:
            P.cp(yo[i % 2][:], x[:, i, :], reads=[rx[i]], writes=[ryo[i % 2]])
        else:
            rstd_of(x[:, i, :], D, small[:, 0:1], [rx[i]], sq, rsq)
            P.stt(yo[i % 2][:], x[:, i, :], small[:, 0:1], fwb[:], ALU.mult, ALU.mult, reads=[rx[i], rsmall, rfwb], writes=[ryo[i % 2]])
        P.dma(y_d[i * 128:(i + 1) * 128, :], yo[i % 2][:], reads=[ryo[i % 2]], q="gpsimd")
    P.emit()
    P.es.close()
    return nc, P


def _na_tables():
    a = np.arange(128)
    types = {}
    for i in range(NT):
        for kt, ty in na_blocks(i):
            types.setdefault(ty, (i, kt))
    DR = np.zeros((26, 128, 128), np.int64); DC = np.zeros((26, 128, 128), np.int64)
    msk_s = np.zeros((NTYPE, 128, 128), np.float32); msk_p = np.zeros((NTYPE, 128, 128), np.float32)
    for ty, (i, kt) in types.items():
        krow = (2 * kt + a // 64)[:, None]; kcol = (a % 64)[:, None]
        qrow = (2 * i + a // 64)[None, :]; qcol = (a % 64)[None, :]
        rs = np.clip(qrow - 4, 0, 24)
        cs = np.clip(qcol - 8, 0, 48)
        valid = (krow >= rs) & (krow < rs + 8) & (kcol >= cs) & (kcol < cs + 16)
        DR[ty] = np.clip(krow - qrow, -7, 7) + 7
        DC[ty] = np.clip(kcol - qcol, -15, 15) + 15
        msk_s[ty] = np.where(valid, 0.0, -1e30)
        msk_p[ty] = 0.0 if (kt // 2 == i // 2) else -1e30
    msk_s[26] = 0.0
    msk_p[26] = -1e30
    return DR, DC, msk_s, msk_p


_CACHE = {}


def kernel(x_prompt, x_sample, cache_na_k, cache_na_v, state_gla_fwd, state_gla_bwd, c, c_ctx,
           w_ada, b_ada, norm1_w, norm2_w, w_in, w_af, b_af, w_ab, b_ab, gla_norm_w,
           conv_w, conv_b, na_rpb, w_out, peer_wq, peer_k1, peer_k2, peer_u, peer_v, final_norm_w):
    f = lambda a: np.ascontiguousarray(np.asarray(a), dtype=np.float32)
    x_prompt, x_sample = f(x_prompt), f(x_sample)
    DR, DC, msk_s, msk_p = _na_tables()
    rpb = f(na_rpb)
    nab_s = np.ascontiguousarray(rpb[:, :, DR, DC])
    nab_p = np.zeros_like(nab_s)
    a = np.arange(128)
    glam = np.stack([(a[:, None] <= a[None, :]), (a[:, None] >= a[None, :])]).astype(np.float32)
    shared = {
        "ident": np.eye(128).astype(NPBF), "glamask": glam,
        "w_ada": f(w_ada), "b_ada": f(b_ada), "norm1_w": f(norm1_w), "norm2_w": f(norm2_w), "w_in": f(w_in),
        "w_af": f(w_af), "b_af": f(b_af), "w_ab": f(w_ab), "b_ab": f(b_ab), "gla_norm_w": f(gla_norm_w),
        "conv_w": f(conv_w), "conv_b": f(conv_b), "w_out": f(w_out), "peer_wq": f(peer_wq),
        "peer_k1": f(peer_k1), "peer_k2": f(peer_k2), "peer_u": f(peer_u), "peer_v": f(peer_v),
        "final_norm_w": f(final_norm_w),
    }
    zk = np.zeros((DEPTH, 6, 256, 64), np.float32)
    zs = np.zeros((DEPTH, 6, 64, 64), np.float32)
    in_maps = []
    for core in range(8):
        m = dict(shared)
        if core < 4:
            b = core
            m.update(x=x_sample[b], cvec=f(c)[b], kctx=f(cache_na_k)[b], vctx=f(cache_na_v)[b],
                     s0f=f(state_gla_fwd)[b], s0b=f(state_gla_bwd)[b], carry=np.ones((128, 1), np.float32),
                     nabias=nab_s, namask=msk_s)
        else:
            b0 = (core - 4) * 4
            xp = x_prompt[b0:b0 + 4].reshape(1024, D)
            m.update(x=np.ascontiguousarray(np.concatenate([xp, xp], 0)), cvec=f(c_ctx), kctx=zk, vctx=zk,
                     s0f=zs, s0b=zs, carry=np.zeros((128, 1), np.float32), nabias=nab_p, namask=msk_p)
        in_maps.append(m)
    if _CACHE.get("only_maps"):
        return in_maps
    if "nc" not in _CACHE:
        _CACHE["nc"] = build()
    nc, P = _CACHE["nc"]
    res = run_bass_kernel_spmd(nc, in_maps, core_ids=list(range(8)))
    R = res.results
    y_sample = np.stack([R[b]["y"] for b in range(4)], 0)
    y_prompt = np.concatenate([R[4 + j]["y"][:1024].reshape(4, 256, D) for j in range(4)], 0)

    def kvout(name):
        o = np.concatenate([R[4 + j][name][:, :1024].reshape(DEPTH, 4, 256, 6, 64) for j in range(4)], 1)
        return np.ascontiguousarray(o.transpose(1, 0, 3, 2, 4))

    def stout(name):
        o = np.concatenate([R[4 + j][name][:, :4] for j in range(4)], 1)
        return np.ascontiguousarray(o.transpose(1, 0, 2, 3, 4))
    return (y_prompt.astype(np.float32), y_sample.astype(np.float32), kvout("nk"), kvout("nv"), stout("sf"), stout("sbo"))
```

```python
import os
import numpy as np
from contextlib import ExitStack
import ml_dtypes
import concourse.bass as bass
import concourse.mybir as mybir
from concourse.bass_utils import run_bass_kernel_spmd


F32 = mybir.dt.float32
BF16 = mybir.dt.bfloat16
AF = mybir.ActivationFunctionType
ALU = mybir.AluOpType
AX = mybir.AxisListType
NPBF = ml_dtypes.bfloat16


class Res:
    __slots__ = ("name", "w", "rd")

    def __init__(self, name="r"):
        self.name = name
        self.w = None
        self.rd = []


class Op:
    __slots__ = ("eng", "fn", "dma", "waits", "signal", "sem", "val", "prevsame")

    def __init__(self, eng, fn, dma):
        self.eng = eng
        self.fn = fn
        self.dma = dma
        self.waits = []
        self.signal = False
        self.sem = None
        self.val = None
        self.prevsame = None


ENGINES = ("tensor", "vector", "scalar", "gpsimd", "sync")
NDMASEM = 16
SEMCH = 30000


class Prog:
    def __init__(self, nc):
        self.nc = nc
        self.streams = {e: [] for e in ENGINES}
        self.es = ExitStack()
        self.nops = 0
        self.pending = {e: [] for e in ENGINES}
        self.out_dma = []
        self.mute = False

    def sb(self, name, shape, dt):
        return self.es.enter_context(self.nc.sbuf_tensor("sb_" + name, list(shape), dt))

    def ps(self, name, shape, dt=F32):
        return self.es.enter_context(self.nc.psum_tensor(name, list(shape), dt))

    def op(self, eng, fn, reads=(), writes=(), dma=False):
        o = Op(eng, fn, dma)
        if self.mute:
            return o
        deps = []
        for r in reads:
            if r.w is not None:
                deps.append(r.w)
        for w in writes:
            if w.w is not None:
                deps.append(w.w)
            deps.extend(w.rd)
        if self.pending[eng]:
            deps.extend(self.pending[eng])
            self.pending[eng] = []
        seen = set()
        for d in deps:
            if d is o or id(d) in seen:
                continue
            seen.add(id(d))
            if d.eng == "tensor" and eng == "tensor" and not d.dma and not dma:
                continue
            d.signal = True
            o.waits.append(d)
        for r in reads:
            r.rd.append(o)
        for w in writes:
            w.w = o
            w.rd = []
        self.streams[eng].append(o)
        self.nops += 1
        if dma:
            self.out_dma.append(o)
        return o

    def barrier(self):
        lasts = []
        for e in ENGINES:
            st = self.streams[e]
            last_c = None
            for o in reversed(st):
                if not o.dma:
                    last_c = o
                    break
            if last_c is not None:
                lasts.append(last_c)
        lasts.extend(self.out_dma)
        self.out_dma = []
        for e in ENGINES:
            self.pending[e] = list(lasts)

    def dma(self, out, in_, reads=(), writes=(), q="sync", **kw):
        return self.op(q, lambda e: e.dma_start(out=out, in_=in_, **kw), reads, writes, dma=True)

    def mm(self, out, lhsT, rhs, start, stop, reads=(), writes=()):
        return self.op("tensor", lambda e: e.matmul(out, lhsT, rhs, start=start, stop=stop), reads, writes)

    def act(self, out, in_, func, reads=(), writes=(), eng="scalar", **kw):
        return self.op(eng, lambda e: e.activation(out=out, in_=in_, func=func, **kw), reads, writes)

    def tt(self, out, in0, in1, op, reads=(), writes=(), eng="vector"):
        return self.op(eng, lambda e: e.tensor_tensor(out=out, in0=in0, in1=in1, op=op), reads, writes)

    def ts(self, out, in0, s1, s2, op0, op1=None, reads=(), writes=(), eng="vector"):
        if op1 is None:
            return self.op(eng, lambda e: e.tensor_scalar(out=out, in0=in0, scalar1=s1, scalar2=None, op0=op0), reads, writes)
        return self.op(eng, lambda e: e.tensor_scalar(out=out, in0=in0, scalar1=s1, scalar2=s2, op0=op0, op1=op1), reads, writes)

    def stt(self, out, in0, scalar, in1, op0, op1, reads=(), writes=(), eng="vector"):
        return self.op(eng, lambda e: e.scalar_tensor_tensor(out=out, in0=in0, scalar=scalar, in1=in1, op0=op0, op1=op1), reads, writes)

    def cp(self, out, in_, reads=(), writes=(), eng="vector"):
        if eng == "scalar":
            return self.op(eng, lambda e: e.copy(out=out, in_=in_), reads, writes)
        return self.op(eng, lambda e: e.tensor_copy(out=out, in_=in_), reads, writes)

    def memset(self, ap, val, writes=(), eng="vector"):
        return self.op(eng, lambda e: e.memset(ap, val), (), writes)

    def emit(self):
        nc = self.nc
        es = self.es
        csems = {}
        dsems = {}
        for e in ENGINES:
            st = self.streams[e]
            nsig = sum(1 for o in st if (o.signal and not o.dma))
            csems[e] = [es.enter_context(nc.semaphore(f"c_{e}_{i}")) for i in range(nsig // SEMCH + 1)]
            ndma = sum(1 for o in st if o.dma)
            dsems[e] = [es.enter_context(nc.semaphore(f"d_{e}_{i}")) for i in range(min(NDMASEM, ndma))]
            cnt = 0
            dcnt = 0
            hist = []
            for o in st:
                if o.dma:
                    k = dcnt
                    o.sem = dsems[e][k % NDMASEM]
                    o.val = 16 * (k // NDMASEM + 1)
                    if k >= NDMASEM:
                        o.prevsame = hist[k - NDMASEM]
                    hist.append(o)
                    dcnt += 1
                elif o.signal:
                    o.sem = csems[e][cnt // SEMCH]
                    o.val = cnt % SEMCH + 1
                    cnt += 1
        finals = []
        for e in ENGINES:
            last = {}
            for o in self.streams[e]:
                if o.dma:
                    last[id(o.sem)] = (o.sem, o.val)
            finals.extend(last.values())
        streams = self.streams
        block = es.enter_context(nc.Block())

        def make(e):
            def run(engh):
                waited = {}
                for o in streams[e]:
                    need = {}
                    for d in o.waits:
                        k = id(d.sem)
                        if k not in need or need[k][1] < d.val:
                            need[k] = (d.sem, d.val)
                    if o.prevsame is not None:
                        k = id(o.sem)
                        v = o.prevsame.val
                        if k not in need or need[k][1] < v:
                            need[k] = (o.sem, v)
                    for k, (sem, val) in need.items():
                        if waited.get(k, 0) < val:
                            engh.wait_ge(sem, val)
                            waited[k] = val
                    ins = o.fn(engh)
                    if o.dma:
                        ins.then_inc(o.sem, 16)
                    elif o.signal:
                        ins.then_inc(o.sem, 1)
                if e == "sync":
                    for sem, val in finals:
                        engh.wait_ge(sem, val)
            return run

        for e in ENGINES:
            getattr(block, e)(make(e))


D = 1024
NT = 16
T = 2048
DEPTH = 4
INW = 3488
QA0, KA0, VA0, GA0, LRF0, LRB0 = 0, 384, 768, 1152, 1536, 1552
CH0, CB0, CC0 = 1568, 1824, 2080
QN0, KN0, VN0 = 2336, 2720, 3104
NTYPE = 27
ARENA_W = 25856


def na_blocks(i):
    if 2 <= i <= 13:
        return [(i - 2 + m, (i % 2) * 5 + m) for m in range(5)]
    e = {0: 0, 1: 1, 14: 2, 15: 3}[i]
    base = 0 if i < 2 else 12
    return [(base + m, 10 + e * 4 + m) for m in range(4)]


class Arena:
    def __init__(self, P):
        self.t = P.sb("arena", [128, ARENA_W], F32)
        self.off = 0

    def reset(self):
        self.off = 0

    def f32(self, shape):
        n = int(np.prod(shape[1:]))
        v = self.t[:, self.off:self.off + n]
        self.off += n
        assert self.off <= ARENA_W, self.off
        return self._shape(v, shape)

    def bf16(self, shape):
        n = int(np.prod(shape[1:]))
        assert n % 2 == 0
        v = self.t[:, self.off:self.off + n // 2].bitcast(BF16)
        self.off += n // 2
        assert self.off <= ARENA_W, self.off
        return self._shape(v, shape)

    @staticmethod
    def _shape(v, shape):
        if len(shape) == 2:
            return v
        if len(shape) == 3:
            return v.rearrange("p (a b) -> p a b", a=shape[1], b=shape[2])
        if len(shape) == 4:
            return v.rearrange("p (a b c) -> p a b c", a=shape[1], b=shape[2], c=shape[3])
        raise ValueError(shape)


def build(nlayers=DEPTH, phases="CNGP", raw=False):
    nc = bass.Bass("TRN2", target_bir_lowering=False)

    def din(name, shape, dt=F32):
        return nc.dram_tensor(name, list(shape), dt, kind="ExternalInput").ap()

    def dout(name, shape):
        return nc.dram_tensor(name, list(shape), F32, kind="ExternalOutput").ap()

    x_d = din("x", [T, D]); cvec_d = din("cvec", [D])
    kctx_d = din("kctx", [DEPTH, 6, 256, 64]); vctx_d = din("vctx", [DEPTH, 6, 256, 64])
    s0f_d = din("s0f", [DEPTH, 6, 64, 64]); s0b_d = din("s0b", [DEPTH, 6, 64, 64])
    carry_d = din("carry", [128, 1])
    nabias_d = din("nabias", [DEPTH, 6, 26, 128, 128]); namask_d = din("namask", [NTYPE, 128, 128])
    ident_d = din("ident", [128, 128], BF16)
    glam_d = din("glamask", [2, 128, 128])
    w_ada_d = din("w_ada", [DEPTH, D, 6 * D]); b_ada_d = din("b_ada", [DEPTH, 6 * D])
    n1_d = din("norm1_w", [DEPTH, D]); n2_d = din("norm2_w", [DEPTH, D])
    w_in_d = din("w_in", [DEPTH, D, INW])
    waf_d = din("w_af", [DEPTH, 16, 384]); baf_d = din("b_af", [DEPTH, 384])
    wab_d = din("w_ab", [DEPTH, 16, 384]); bab_d = din("b_ab", [DEPTH, 384])
    gnw_d = din("gla_norm_w", [DEPTH, 384])
    cw_d = din("conv_w", [DEPTH, 3, 256]); cb_d = din("conv_b", [DEPTH, 256])
    wout_d = din("w_out", [DEPTH, D, D]); wq_d = din("peer_wq", [DEPTH, D, D])
    k1_d = din("peer_k1", [DEPTH, 8, 128, 64]); k2_d = din("peer_k2", [DEPTH, 8, 128, 64])
    pu_d = din("peer_u", [DEPTH, 16384, D]); pv_d = din("peer_v", [DEPTH, 16384, D])
    fnw_d = din("final_norm_w", [D])
    y_d = dout("y", [T, D]); nk_d = dout("nk", [DEPTH, T, 384]); nv_d = dout("nv", [DEPTH, T, 384])
    sf_d = dout("sf", [DEPTH, 8, 6, 64, 64]); sbo_d = dout("sbo", [DEPTH, 8, 6, 64, 64])

    P = Prog(nc)
    NSC = dict(allow_slow_non_contiguous=True)
    x = P.sb("x", [128, NT, D], F32); rx = [Res() for _ in range(NT)]
    hT = P.sb("hT", [128, 8, T], BF16); rhT = Res()
    ident = P.sb("ident", [128, 128], BF16); rid = Res()
    identf = P.sb("identf", [128, 128], F32)
    onesf = P.sb("onesf", [128, 128], F32)
    glam = P.sb("glam", [128, 2, 128], F32)
    carry = P.sb("carry", [128, 1], F32)
    carm1 = P.sb("carm1", [128, 1], F32)
    scol = P.sb("scol", [128, 8], F32)
    modcol = P.sb("modcol", [128, 4, 8], F32); rmod = Res()
    acol = P.sb("acol", [128, 2, 8], F32)
    nwcol = P.sb("nwcol", [128, 2, 8], F32)
    gb = P.sb("gb", [128, 2, D], F32); rgb = Res()
    small = P.sb("small", [128, 64], F32); rsmall = Res()
    rconst = Res()
    ar = Arena(P)
    banks = [P.ps(f"bank{i}", [128, 512], F32) for i in range(8)]
    rbank = [Res() for _ in range(8)]
    rr = {"s": 0, "a": 0, "b": 0}

    def bank(pool):
        if pool == "s":
            i = rr["s"] % 3
        elif pool == "a":
            i = 3 + rr["a"] % 3
        else:
            i = 6 + rr["b"] % 2
        rr[pool] += 1
        return banks[i], rbank[i]

    for i in range(NT):
        P.dma(x[:, i, :], x_d[i * 128:(i + 1) * 128, :], writes=[rx[i]])
    P.dma(ident[:], ident_d[:, :], writes=[rid])
    P.dma(glam[:], glam_d.rearrange("a s t -> s a t"), writes=[rconst])
    P.dma(carry[:], carry_d[:, :], writes=[rconst])
    P.dma(scol[:], cvec_d.rearrange("(k p) -> p k", p=128), writes=[rconst], **NSC)
    P.cp(identf[:], ident[:], reads=[rid], writes=[rconst])
    P.memset(onesf[:], 1.0, writes=[rconst])
    P.ts(carm1[:], carry[:], -1.0, None, ALU.add, reads=[rconst], writes=[rconst])
    P.act(scol[:], scol[:], AF.Silu, reads=[rconst], writes=[rconst])

    def rstd_of(src_ap, n, out_ap, rsrc, scratch, rscratch):
        P.act(scratch, src_ap, AF.Square, reads=rsrc, writes=[rscratch, rsmall], accum_out=out_ap)
        P.ts(out_ap, out_ap, 1.0 / n, 1e-6, ALU.mult, ALU.add, reads=[rsmall], writes=[rsmall])
        P.act(out_ap, out_ap, AF.Sqrt, reads=[rsmall], writes=[rsmall])
        P.op("vector", lambda e: e.reciprocal(out=out_ap, in_=out_ap), [rsmall], [rsmall])

    def norm_to_hT(which):
        ar.reset()
        sq = ar.f32([128, D]); rsq = Res()
        xn = ar.bf16([128, 2, D]); rxn = [Res(), Res()]
        for i in range(NT):
            rstd_of(x[:, i, :], D, small[:, 0:1], [rx[i]], sq, rsq)
            P.ts(xn[:, i % 2, :], x[:, i, :], small[:, 0:1], None, ALU.mult, reads=[rx[i], rsmall], writes=[rxn[i % 2]])
            for hf in range(2):
                bk, rb = bank("s")
                for c4 in range(4):
                    c = hf * 4 + c4
                    P.mm(bk[:, c4 * 128:(c4 + 1) * 128], xn[:, i % 2, c * 128:(c + 1) * 128], ident[:], True, True,
                         reads=[rxn[i % 2], rid], writes=[rb])
                for c4 in range(4):
                    c = hf * 4 + c4
                    P.ts(hT[:, c, i * 128:(i + 1) * 128], bk[:, c4 * 128:(c4 + 1) * 128], acol[:, which, c:c + 1],
                         modcol[:, 2 * which, c:c + 1], ALU.mult, ALU.add, reads=[rb, rmod], writes=[rhT],
                         eng=("vector" if c4 % 2 == 0 else "gpsimd") if False else "vector")

    wst = [None, None]; rwst = [Res(), Res()]; wbf = [None, None]; rwbf = [Res(), Res()]
    wcnt = [0]

    def alloc_w():
        for k in range(2):
            wst[k] = ar.f32([128, 8, 128])
            wbf[k] = ar.bf16([128, 8, 128])

    def wcols(dram_ap, n, dst=None, rdst=None, dcol=0):
        k = wcnt[0] % 2
        wcnt[0] += 1
        P.dma(wst[k][:, :, 0:n], dram_ap.rearrange("(c p) n -> p c n", p=128), writes=[rwst[k]])
        if dst is None:
            dst, rdst, dcol = wbf[k], rwbf[k], 0
        P.cp(dst[:, :, dcol:dcol + n], wst[k][:, :, 0:n], reads=[rwst[k]], writes=[rdst], eng="gpsimd")
        return dst, rdst

    def proj_fm(wb, rwb, wcol0, m, evac):
        for tg in range(4):
            bk, rb = bank("s")
            for c in range(8):
                P.mm(bk[0:m, :], wb[:, c, wcol0:wcol0 + m], hT[:, c, tg * 512:(tg + 1) * 512], c == 0, c == 7,
                     reads=[rwb, rhT], writes=[rb])
            evac(tg, bk, rb)

    def outproj(cat, rcat, row0, nch, l):
        wo_st = ar.f32([128, D]); rwo_st = Res()
        wob = ar.bf16([128, nch, D]); rwob = Res()
        for j in range(nch):
            P.dma(wo_st[:], wout_d[l, row0 + j * 128: row0 + (j + 1) * 128, :], writes=[rwo_st])
            P.tt(wob[:, j, :], wo_st[:], gb[:, 0, :], ALU.mult, reads=[rwo_st, rgb], writes=[rwob], eng="gpsimd")
        for i in range(NT):
            for hf in range(2):
                bk, rb = bank("b")
                for j in range(nch):
                    P.mm(bk[:, :], cat[:, j, i * 128:(i + 1) * 128], wob[:, j, hf * 512:(hf + 1) * 512], j == 0, j == nch - 1,
                         reads=[rcat, rwob], writes=[rb])
                P.tt(x[:, i, hf * 512:(hf + 1) * 512], x[:, i, hf * 512:(hf + 1) * 512], bk[:, :], ALU.add,
                     reads=[rx[i], rb], writes=[rx[i]])

    for l in range(nlayers):
        P.barrier()
        ar.reset()
        wa = [ar.f32([128, 8, 512]), ar.f32([128, 8, 512])]; rwa = [Res(), Res()]
        row = ar.f32([1, 512]); rrow = Res()
        brow = ar.f32([1, 512]); rbrow = Res()
        P.dma(nwcol[:, 0, :], n1_d[l].rearrange("(k p) -> p k", p=128), writes=[rmod], **NSC)
        P.dma(nwcol[:, 1, :], n2_d[l].rearrange("(k p) -> p k", p=128), writes=[rmod], **NSC)
        for g in range(12):
            k = g % 2
            P.dma(wa[k][:], w_ada_d[l, :, g * 512:(g + 1) * 512].rearrange("(c p) n -> p c n", p=128), writes=[rwa[k]])
            P.dma(brow[0:1, :], b_ada_d[l:l + 1, g * 512:(g + 1) * 512], writes=[rbrow])
            bk, rb = bank("s")
            for c in range(8):
                P.mm(bk[0:1, :], scol[:, c:c + 1], wa[k][:, c, :], c == 0, c == 7, reads=[rconst, rwa[k]], writes=[rb])
            P.tt(row[0:1, :], bk[0:1, :], brow[0:1, :], ALU.add, reads=[rb, rbrow], writes=[rrow])
            vec = g // 2
            if vec in (0, 1, 3, 4):
                mi = {0: 0, 1: 1, 3: 2, 4: 3}[vec]
                bk2, rb2 = bank("s")
                for j in range(4):
                    P.mm(bk2[:, j:j + 1], row[0:1, j * 128:(j + 1) * 128], onesf[0:1, 0:1], True, True,
                         reads=[rrow, rconst], writes=[rb2])
                P.cp(modcol[:, mi, k * 4:(k + 1) * 4], bk2[:, 0:4], reads=[rb2], writes=[rmod])
            else:
                gi = 0 if vec == 2 else 1
                bk2, rb2 = bank("s")
                P.mm(bk2[:, :], onesf[0:1, 0:128], row[0:1, :], True, True, reads=[rrow, rconst], writes=[rb2])
                P.cp(gb[:, gi, k * 512:(k + 1) * 512], bk2[:, :], reads=[rb2], writes=[rgb])
        for w_ in range(2):
            P.stt(acol[:, w_, :], modcol[:, 2 * w_ + 1, :], 1.0, nwcol[:, w_, :], ALU.add, ALU.mult,
                  reads=[rmod], writes=[rmod])
        P.barrier()
        norm_to_hT(0)

        P.mute = "C" not in phases
        P.barrier()
        ar.reset()
        alloc_w()
        zc = ar.f32([128, 3, T]); rzc = Res()
        u = ar.f32([128, T]); ru = Res()
        yv = ar.f32([128, T]); ryv = Res()
        catC = ar.bf16([128, 2, T]); rcatC = Res()
        cwc = ar.f32([128, 2, 4]); rcwc = Res()
        wn = ar.f32([128, 2, 2]); rwn = Res()
        for j in range(2):
            for k_ in range(3):
                P.dma(cwc[:, j, k_:k_ + 1], cw_d[l, k_, j * 128:(j + 1) * 128].rearrange("(p o) -> p o", o=1), writes=[rcwc], **NSC)
            P.dma(cwc[:, j, 3:4], cb_d[l, j * 128:(j + 1) * 128].rearrange("(p o) -> p o", o=1), writes=[rcwc], **NSC)
        for j in range(2):
            P.ts(wn[:, j, 0:1], cwc[:, j, 0:1], carm1[:, 0:1], None, ALU.mult, reads=[rcwc, rconst], writes=[rwn])
            P.ts(wn[:, j, 1:2], cwc[:, j, 2:3], carm1[:, 0:1], None, ALU.mult, reads=[rcwc, rconst], writes=[rwn])
        for j in range(2):
            for wi, c0 in enumerate((CH0, CB0, CC0)):
                wb, rwb = wcols(w_in_d[l, :, c0 + j * 128:c0 + (j + 1) * 128], 128)
                proj_fm(wb, rwb, 0, 128, lambda tg, bk, rb, wi=wi: P.cp(zc[:, wi, tg * 512:(tg + 1) * 512], bk[:, :],
                                                                      reads=[rb], writes=[rzc], eng="scalar"))
            P.tt(u[:], zc[:, 2, :], zc[:, 0, :], ALU.mult, reads=[rzc], writes=[ru], eng="gpsimd")
            P.ts(yv[:], u[:], cwc[:, j, 1:2], None, ALU.mult, reads=[ru, rcwc], writes=[ryv])
            P.stt(yv[:, 1:T], u[:, 0:T - 1], cwc[:, j, 0:1], yv[:, 1:T], ALU.mult, ALU.add, reads=[ru, rcwc, ryv], writes=[ryv])
            P.stt(yv[:, 0:T - 1], u[:, 1:T], cwc[:, j, 2:3], yv[:, 0:T - 1], ALU.mult, ALU.add, reads=[ru, rcwc, ryv], writes=[ryv])
            P.stt(yv[:, 256:T:256], u[:, 255:T - 1:256], wn[:, j, 0:1], yv[:, 256:T:256], ALU.mult, ALU.add,
                  reads=[ru, rwn, ryv], writes=[ryv])
            P.stt(yv[:, 255:T - 1:256], u[:, 256:T:256], wn[:, j, 1:2], yv[:, 255:T - 1:256], ALU.mult, ALU.add,
                  reads=[ru, rwn, ryv], writes=[ryv])
            P.stt(catC[:, j, :], yv[:], cwc[:, j, 3:4], zc[:, 1, :], ALU.add, ALU.mult, reads=[ryv, rcwc, rzc], writes=[rcatC])
        outproj(catC, rcatC, 384, 2, l)

        P.mute = "N" not in phases
        P.barrier()
        ar.reset()
        alloc_w()
        namask = ar.f32([128, NTYPE, 128]); rnamask = Res()
        P.dma(namask[:], namask_d.rearrange("t k q -> k t q"), writes=[rnamask])
        nab = ar.f32([128, 26, 128]); rnab = Res()
        BM = ar.bf16([128, 2, NTYPE, 128]); rBM = Res()
        qT = ar.bf16([128, T]); rqT = Res()
        kT = ar.bf16([128, T]); rkT = Res()
        wkv = ar.bf16([128, 8, 256]); rwkv = Res()
        kvst = [ar.f32([128, 256]), ar.f32([128, 256])]; rkvst = [Res(), Res()]
        vaug = ar.bf16([128, NT, 2, 66]); rvaug = Res()
        cst = ar.f32([128, 2, 2, 64]); rcst = Res()
        kcb = ar.bf16([128, 2, 128]); rkcb = Res()
        kcT = ar.bf16([128, 256]); rkcT = Res()
        vca = ar.bf16([128, 2, 2, 66]); rvca = Res()
        pTw = [ar.bf16([128, 8, 128]) for _ in range(2)]; rpTw = [Res() for _ in range(2)]
        opair = [ar.bf16([128, 128]) for _ in range(2)]; ropair = [Res(), Res()]
        catN = ar.bf16([128, 1, T]); rcatN = Res()
        P.memset(vaug[:, :, :, 64:66], 1.0, writes=[rvaug])
        P.memset(vca[:, :, :, 64:66], 1.0, writes=[rvca])
        pcnt = 0
        for p in range(3):
            wb, rwb = wcols(w_in_d[l, :, QN0 + p * 128:QN0 + (p + 1) * 128], 128)
            proj_fm(wb, rwb, 0, 128, lambda tg, bk, rb: P.act(qT[:, tg * 512:(tg + 1) * 512], bk[:, :], AF.Copy,
                                                             reads=[rb], writes=[rqT], scale=0.125))
            wb, rwb = wcols(w_in_d[l, :, KN0 + p * 128:KN0 + (p + 1) * 128], 128)
            proj_fm(wb, rwb, 0, 128, lambda tg, bk, rb: P.cp(kT[:, tg * 512:(tg + 1) * 512], bk[:, :],
                                                            reads=[rb], writes=[rkT], eng="scalar"))
            wcols(w_in_d[l, :, KN0 + p * 128:KN0 + (p + 1) * 128], 128, wkv, rwkv, 0)
            wcols(w_in_d[l, :, VN0 + p * 128:VN0 + (p + 1) * 128], 128, wkv, rwkv, 128)
            for i in range(NT):
                bk, rb = bank("s")
                for c in range(8):
                    P.mm(bk[:, 0:256], hT[:, c, i * 128:(i + 1) * 128], wkv[:, c, :], c == 0, c == 7,
                         reads=[rhT, rwkv], writes=[rb])
                kk_ = i % 2
                P.cp(kvst[kk_][:], bk[:, 0:256], reads=[rb], writes=[rkvst[kk_]], eng="scalar")
                P.dma(nk_d[l, i * 128:(i + 1) * 128, p * 128:(p + 1) * 128], kvst[kk_][:, 0:128], reads=[rkvst[kk_]], q="gpsimd")
                P.dma(nv_d[l, i * 128:(i + 1) * 128, p * 128:(p + 1) * 128], kvst[kk_][:, 128:256], reads=[rkvst[kk_]], q="gpsimd")
                P.cp(vaug[:, i, :, 0:64], kvst[kk_][:, 128:256].rearrange("p (h d) -> p h d", h=2),
                     reads=[rkvst[kk_]], writes=[rvaug])
            for kt in range(2):
                P.dma(cst[:, kt, :, :], kctx_d[l, 2 * p:2 * p + 2, kt * 128:(kt + 1) * 128, :].rearrange("h q d -> q h d"), writes=[rcst])
            P.cp(kcb[:].rearrange("p kt (h d) -> p kt h d", h=2), cst[:], reads=[rcst], writes=[rkcb])
            for kt in range(2):
                bk, rb = bank("s")
                P.mm(bk[:, 0:128], kcb[:, kt, :], ident[:], True, True, reads=[rkcb, rid], writes=[rb])
                P.cp(kcT[:, kt * 128:(kt + 1) * 128], bk[:, 0:128], reads=[rb], writes=[rkcT], eng="scalar")
            for kt in range(2):
                P.dma(cst[:, kt, :, :], vctx_d[l, 2 * p:2 * p + 2, kt * 128:(kt + 1) * 128, :].rearrange("h q d -> q h d"), writes=[rcst])
            P.cp(vca[:, :, :, 0:64], cst[:], reads=[rcst], writes=[rvca])
            for hh in range(2):
                P.dma(nab[:], nabias_d[l, 2 * p + hh].rearrange("t k q -> k t q"), writes=[rnab])
                P.tt(BM[:, hh, 0:26, :], nab[:], namask[:, 0:26, :], ALU.add, reads=[rnab, rnamask], writes=[rBM])
                P.cp(BM[:, hh, 26, :], namask[:, 26, :], reads=[rnamask], writes=[rBM])
            for i in range(NT):
                for hh in range(2):
                    hp = slice(hh * 64, hh * 64 + 64)
                    blocks = [("l", kt, ty) for kt, ty in na_blocks(i)] + [("c", 0, 26), ("c", 1, 26)]
                    nb_ = len(blocks)
                    pw = pTw[pcnt % 2]; rpw = rpTw[pcnt % 2]
                    pcnt += 1
                    for g0 in range(0, nb_, 4):
                        grp = blocks[g0:g0 + 4]
                        bs, rbs = bank("s")
                        for j, (src, kt, ty) in enumerate(grp):
                            ksrc, rks = (kT, rkT) if src == "l" else (kcT, rkcT)
                            P.mm(bs[:, j * 128:(j + 1) * 128], ksrc[hp, kt * 128:(kt + 1) * 128], qT[hp, i * 128:(i + 1) * 128], True, False,
                                 reads=[rks, rqT], writes=[rbs])
                            P.mm(bs[:, j * 128:(j + 1) * 128], ident[:], BM[:, hh, ty, :], False, True, reads=[rid, rBM], writes=[rbs])
                        n_ = len(grp)
                        P.act(pw[:, g0:g0 + n_, :], bs[:, 0:n_ * 128].rearrange("p (a b) -> p a b", a=n_), AF.Exp,
                              reads=[rbs], writes=[rpw])
                    bo, rbo = bank("a")
                    for bi, (src, kt, ty) in enumerate(blocks):
                        vsrc, rvs = (vaug[:, kt, hh, 0:65], rvaug) if src == "l" else (vca[:, kt, hh, 0:65], rvca)
                        P.mm(bo[:, 0:65], pw[:, bi, :], vsrc, bi == 0, bi == nb_ - 1, reads=[rpw, rvs], writes=[rbo])
                    P.op("vector", lambda e, bo=bo: e.reciprocal(out=small[:, 1:2], in_=bo[:, 64:65]), [rbo], [rsmall])
                    P.ts(opair[i % 2][:, hp], bo[:, 0:64], small[:, 1:2], None, ALU.mult, reads=[rbo, rsmall], writes=[ropair[i % 2]])
                bk, rb = bank("b")
                P.mm(bk[:, 0:128], opair[i % 2][:], ident[:], True, True, reads=[ropair[i % 2], rid], writes=[rb])
                P.cp(catN[:, 0, i * 128:(i + 1) * 128], bk[:, 0:128], reads=[rb], writes=[rcatN], eng="scalar")
            off_save = ar.off
            outproj(catN, rcatN, 640 + p * 128, 1, l)
            ar.off = off_save
            P.barrier()

        P.mute = "G" not in phases
        ar.reset()
        alloc_w()
        lrT = [ar.bf16([16, T]), ar.bf16([16, T])]; rlrT = Res()
        wa16 = ar.f32([16, 2, 384]); rwa16 = Res()
        wab16 = ar.bf16([16, 2, 384]); rwab16 = Res()
        P.dma(wa16[0:16, 0, :], waf_d[l], writes=[rwa16])
        P.dma(wa16[0:16, 1, :], wab_d[l], writes=[rwa16])
        P.cp(wab16[0:16], wa16[0:16], reads=[rwa16], writes=[rwab16])
        for d_ in range(2):
            wb, rwb = wcols(w_in_d[l, :, LRF0 + d_ * 16:LRF0 + (d_ + 1) * 16], 16)
            proj_fm(wb, rwb, 0, 16, lambda tg, bk, rb, d_=d_: P.cp(lrT[d_][0:16, tg * 512:(tg + 1) * 512], bk[0:16, :],
                                                                  reads=[rb], writes=[rlrT], eng="scalar"))
        gqT = ar.bf16([128, T]); rgq = Res()
        gkT = ar.bf16([128, T]); rgk = Res()
        a_off = ar.off
        A_ = ar.f32([128, T]); rA = Res()
        B_ = ar.f32([128, T]); rB = Res()
        Eb = ar.bf16([128, T]); rEb = Res()
        totc = ar.f32([128, NT]); rtotc = Res()
        _qd = ar.bf16([128, T]); _rqd = Res(); qd = [_qd, _qd]; rqd = [_rqd, _rqd]
        _kd = ar.bf16([128, T]); _rkd = Res(); kd = [_kd, _kd]; rkd = [_rkd, _rkd]
        _kdec = ar.bf16([128, T]); _rkdec = Res(); kdec = [_kdec, _kdec]; rkdec = [_rkdec, _rkdec]
        _kdt = ar.bf16([128, NT, 128]); _rkdt = Res(); kdtok = [_kdt, _kdt]; rkdtok = [_rkdt, _rkdt]
        elast = ar.f32([128, 2, NT]); rel = Res()
        wvg = ar.bf16([128, 8, 256]); rwvg = Res()
        v_b = ar.bf16([128, NT, 128]); rvb = Res()
        sg = ar.bf16([128, NT, 128]); rsg = Res()
        of = ar.f32([128, NT, 128]); rof = Res()
        ycat = sg; rycat = rsg
        catG = gqT.rearrange("p (o t) -> p o t", o=1); rcatG = rgq
        nb = ar.f32([128, 2]); rnb = Res()
        gwb = ar.f32([128, 128]); rgwb = Res()
        S = ar.f32([128, 64]); rS = Res()
        Sb = ar.bf16([128, 64]); rSb = Res()
        s0 = ar.f32([128, 2, 64]); rs0 = Res()
        sst = ar.f32([128, 2, 8, 64]); rsst = Res()
        attm = [ar.bf16([128, 2, 128]), ar.bf16([128, 2, 128])]; rattm = [Res(), Res()]
        ot = ar.f32([128, 128]); rot = Res()
        osq = ar.f32([128, 128]); rosq = Res()
        Bv = B_.rearrange("p (c j) -> p c j", j=128)
        totb = totc[:, :].unsqueeze(2).to_broadcast([128, NT, 128])
        Av = A_.rearrange("p (c j) -> p c j", j=128)
        for p in range(3):
            cs = slice(p * 128, (p + 1) * 128)
            wb, rwb = wcols(w_in_d[l, :, QA0 + p * 128:QA0 + (p + 1) * 128], 128)
            proj_fm(wb, rwb, 0, 128, lambda tg, bk, rb: P.act(gqT[:, tg * 512:(tg + 1) * 512], bk[:, :], AF.Copy,
                                                             reads=[rb], writes=[rgq], scale=0.125))
            wb, rwb = wcols(w_in_d[l, :, KA0 + p * 128:KA0 + (p + 1) * 128], 128)
            proj_fm(wb, rwb, 0, 128, lambda tg, bk, rb: P.cp(gkT[:, tg * 512:(tg + 1) * 512], bk[:, :],
                                                            reads=[rb], writes=[rgk], eng="scalar"))
            wcols(w_in_d[l, :, VA0 + p * 128:VA0 + (p + 1) * 128], 128, wvg, rwvg, 0)
            wcols(w_in_d[l, :, GA0 + p * 128:GA0 + (p + 1) * 128], 128, wvg, rwvg, 128)
            for i in range(NT):
                bk, rb = bank("s")
                for c in range(8):
                    P.mm(bk[:, 0:256], hT[:, c, i * 128:(i + 1) * 128], wvg[:, c, :], c == 0, c == 7,
                         reads=[rhT, rwvg], writes=[rb])
                P.cp(v_b[:, i, :], bk[:, 0:128], reads=[rb], writes=[rvb], eng="scalar")
                P.act(sg[:, i, :], bk[:, 128:256], AF.Silu, reads=[rb], writes=[rsg])
            if os.environ.get("GSTOP") == "b":
                P.mute = True
            P.dma(nb[:, 0:1], baf_d[l, cs].rearrange("(p o) -> p o", o=1), writes=[rnb], **NSC)
            P.dma(nb[:, 1:2], bab_d[l, cs].rearrange("(p o) -> p o", o=1), writes=[rnb], **NSC)
            P.ts(nb[:], nb[:], -1.0, None, ALU.mult, reads=[rnb], writes=[rnb])
            P.dma(gwb[:], gnw_d[l, cs].partition_broadcast(128), writes=[rgwb])
            P.dma(s0[:, 0, :], s0f_d[l, 2 * p:2 * p + 2].rearrange("h k v -> (h k) v"), writes=[rs0])
            P.dma(s0[:, 1, :], s0b_d[l, 2 * p:2 * p + 2].rearrange("h k v -> (h k) v"), writes=[rs0])
            for d_ in range(2):
                for tg in range(4):
                    bk, rb = bank("s")
                    P.mm(bk[:, :], wab16[0:16, d_, cs], lrT[d_][0:16, tg * 512:(tg + 1) * 512], True, True,
                         reads=[rwab16, rlrT], writes=[rb])
                    P.act(A_[:, tg * 512:(tg + 1) * 512], bk[:, :], AF.Exp, reads=[rb, rnb], writes=[rA],
                          scale=-1.0, bias=nb[:, d_:d_ + 1])
                P.act(A_[:], A_[:], AF.Ln, reads=[rA], writes=[rA], bias=1.0)
                for c in range(NT):
                    P.op("vector", lambda e, c=c: e.tensor_tensor_scan(out=B_[:, c * 128:(c + 1) * 128], data0=onesf[:, :],
                                                                      data1=A_[:, c * 128:(c + 1) * 128], initial=0.0,
                                                                      op0=ALU.mult, op1=ALU.add), [rA, rconst], [rB])
                P.cp(totc[:, :], B_[:, 127:T:128], reads=[rB], writes=[rtotc])
                if d_ == 0:
                    RC, rRC, FR, rFR, FRv = B_, rB, A_, rA, Av
                else:
                    P.tt(A_[:], A_[:], B_[:], ALU.subtract, reads=[rA, rB], writes=[rA], eng="gpsimd")
                    P.tt(Av, Av, totb, ALU.add, reads=[rA, rtotc], writes=[rA], eng="gpsimd")
                    RC, rRC, FR, rFR, FRv = A_, rA, B_, rB, Bv
                RCv = RC.rearrange("p (c j) -> p c j", j=128)
                P.act(Eb[:], RC[:], AF.Exp, reads=[rRC], writes=[rEb], scale=-1.0 / 16)
                P.tt(qd[d_][:], gqT[:], Eb[:], ALU.mult, reads=[rgq, rEb], writes=[rqd[d_]])
                P.act(Eb[:], RC[:], AF.Exp, reads=[rRC], writes=[rEb], scale=1.0 / 16)
                P.tt(kd[d_][:], gkT[:], Eb[:], ALU.mult, reads=[rgk, rEb], writes=[rkd[d_]])
                P.tt(FRv, RCv, totb, ALU.subtract, reads=[rRC, rtotc], writes=[rFR], eng="gpsimd")
                P.act(Eb[:], FR[:], AF.Exp, reads=[rFR], writes=[rEb], scale=1.0 / 16)
                P.tt(kdec[d_][:], gkT[:], Eb[:], ALU.mult, reads=[rgk, rEb], writes=[rkdec[d_]])
                P.act(elast[:, d_, :], totc[:, :], AF.Exp, reads=[rtotc], writes=[rel], scale=-1.0 / 16)
                for c in range(NT):
                    bk, rb = bank("s")
                    P.mm(bk[:, 0:128], kdec[d_][:, c * 128:(c + 1) * 128], ident[:], True, True, reads=[rkdec[d_], rid], writes=[rb])
                    P.cp(kdtok[d_][:, c, :], bk[:, 0:128], reads=[rb], writes=[rkdtok[d_]], eng="scalar")
                if os.environ.get("GSTOP") == "c":
                    P.mute = True
                order = list(range(NT)) if d_ == 0 else list(range(NT - 1, -1, -1))
                P.cp(S[:], s0[:, d_, :], reads=[rs0], writes=[rS])
                for n_, c in enumerate(order):
                    ccs = slice(c * 128, (c + 1) * 128)
                    if n_ > 0 and n_ % 2 == 0:
                        P.ts(S[:], S[:], carry[:, 0:1], None, ALU.mult, reads=[rS, rconst], writes=[rS])
                    P.cp(Sb[:], S[:], reads=[rS], writes=[rSb], eng="scalar")
                    am = attm[n_ % 2]; ram = rattm[n_ % 2]
                    GSK = os.environ.get("GSKIP", "")
                    _m0 = P.mute
                    bos = []
                    for hh in range(2):
                        hp = slice(hh * 64, hh * 64 + 64)
                        ba, rba = bank("s")
                        P.mm(ba[:, 0:128], kd[d_][hp, ccs], qd[d_][hp, ccs], True, True,
                             reads=[rkd[d_], rqd[d_]], writes=[rba])
                        P.tt(am[:, hh, :], ba[:, 0:128], glam[:, d_, :], ALU.mult, reads=[rba, rconst], writes=[ram])
                    for hh in range(2):
                        hp = slice(hh * 64, hh * 64 + 64)
                        bo, rbo = bank("a")
                        bos.append((bo, rbo))
                        P.mm(bo[:, 0:64], am[:, hh, :], v_b[:, c, hh * 64:(hh + 1) * 64], True, False,
                             reads=[ram, rvb], writes=[rbo])
                        P.mm(bo[:, 0:64], qd[d_][hp, ccs], Sb[hp, :], False, True,
                             reads=[rqd[d_], rSb], writes=[rbo])
                    if d_ == 0:
                        for hh in range(2):
                            P.cp(of[:, c, hh * 64:(hh + 1) * 64], bos[hh][0][:, 0:64], reads=[bos[hh][1]], writes=[rof], eng="scalar")
                    else:
                        for hh in range(2):
                            P.tt(ot[:, hh * 64:(hh + 1) * 64], of[:, c, hh * 64:(hh + 1) * 64], bos[hh][0][:, 0:64], ALU.add,
                                 reads=[rof, bos[hh][1]], writes=[rot])
                        P.tt(osq[:], ot[:], ot[:], ALU.mult, reads=[rot], writes=[rosq], eng="gpsimd")
                        P.op("vector", lambda e: e.tensor_reduce(out=small[:, 2:4], in_=osq[:].rearrange("p (h d) -> p h d", h=2),
                                                                 axis=AX.X, op=ALU.add), [rosq], [rsmall])
                        P.ts(small[:, 2:4], small[:, 2:4], 1.0 / 64, 1e-6, ALU.mult, ALU.add, reads=[rsmall], writes=[rsmall])
                        P.act(small[:, 2:4], small[:, 2:4], AF.Sqrt, reads=[rsmall], writes=[rsmall])
                        P.op("vector", lambda e: e.reciprocal(out=small[:, 2:4], in_=small[:, 2:4]), [rsmall], [rsmall])
                        P.tt(ot[:].rearrange("p (h d) -> p h d", h=2), ot[:].rearrange("p (h d) -> p h d", h=2),
                             small[:, 2:4].unsqueeze(2).to_broadcast([128, 2, 64]), ALU.mult, reads=[rot, rsmall], writes=[rot])
                        P.tt(ot[:], ot[:], gwb[:], ALU.mult, reads=[rot, rgwb], writes=[rot])
                        P.tt(ycat[:, c, :], ot[:], sg[:, c, :], ALU.mult, reads=[rot, rsg], writes=[rycat])
                    P.mute = _m0 or ("s" in GSK)
                    bd, rbd = bank("b")
                    P.mm(bd[:, 0:128], kdtok[d_][:, c, :], v_b[:, c, :], True, True, reads=[rkdtok[d_], rvb], writes=[rbd])
                    for hh in range(2):
                        hp = slice(hh * 64, hh * 64 + 64)
                        P.stt(S[hp, :], S[hp, :], elast[hp, d_, c:c + 1], bd[hp, hh * 64:(hh + 1) * 64], ALU.mult, ALU.add,
                              reads=[rS, rel, rbd], writes=[rS])
                    if n_ % 2 == 1:
                        P.cp(sst[:, d_, c // 2, :], S[:], reads=[rS], writes=[rsst], eng="gpsimd")
                    P.mute = _m0
            if os.environ.get("GSTOP") == "d":
                P.mute = True
            for c in range(NT):
                bk, rb = bank("s")
                P.mm(bk[:, 0:128], ycat[:, c, :], ident[:], True, True, reads=[rycat, rid], writes=[rb])
                P.cp(catG[:, 0, c * 128:(c + 1) * 128], bk[:, 0:128], reads=[rb], writes=[rcatG], eng="scalar")
            P.dma(sf_d[l, :, 2 * p:2 * p + 2].rearrange("g h k v -> (h k) g v"), sst[:, 0, :, :], reads=[rsst], q="gpsimd")
            P.dma(sbo_d[l, :, 2 * p:2 * p + 2].rearrange("g h k v -> (h k) g v"), sst[:, 1, :, :], reads=[rsst], q="gpsimd")
            P.barrier()
            off_save = ar.off
            ar.off = a_off
            outproj(catG, rcatG, p * 128, 1, l)
            ar.off = off_save
            P.barrier()

        P.mute = "P" not in phases
        P.barrier()
        norm_to_hT(1)
        P.barrier()
        ar.reset()
        pq = ar.bf16([128, 8, T]); rpq = Res()
        kk = ar.bf16([128, 8, 128]); rkk = Res()
        Kc = ar.bf16([128, 8, 4, 128]); rKc = Res()
        tauS = ar.f32([128, NT, 8]); rtau = Res()
        negcS = ar.f32([128, NT, 8]); rnegc = Res()
        pk_off = ar.off
        alloc_w()
        kst = ar.f32([128, 8, 128]); rkst = Res()
        kstb = ar.bf16([128, 8, 128]); rkstb = Res()
        for hq in range(8):
            wb, rwb = wcols(wq_d[l, :, hq * 128:(hq + 1) * 128], 128)
            proj_fm(wb, rwb, 0, 128, lambda tg, bk, rb, hq=hq: P.cp(pq[:, hq, tg * 512:(tg + 1) * 512], bk[:, :],
                                                                   reads=[rb], writes=[rpq], eng="scalar"))
        P.dma(kst[:, :, 0:64], k1_d[l].rearrange("h n d -> n h d"), writes=[rkst])
        P.dma(kst[:, :, 64:128], k2_d[l].rearrange("h n d -> n h d"), writes=[rkst])
        P.cp(kstb[:], kst[:], reads=[rkst], writes=[rkstb])
        for h in range(8):
            bk, rb = bank("s")
            P.mm(bk[:, 0:128], kstb[:, h, :], ident[:], True, True, reads=[rkstb, rid], writes=[rb])
            P.cp(kk[:, h, :], bk[:, 0:128], reads=[rb], writes=[rkk], eng="scalar")
        P.cp(Kc[64:128], kk[64:128, :, :].unsqueeze(2).to_broadcast([64, 8, 4, 128]), reads=[rkk], writes=[rKc], eng="gpsimd")
        ssb = ar.f32([128, 16, 128]); rssb = Res()
        tmpa = ar.f32([128, 128]); rtmpa = Res()
        v16 = ar.f32([128, 16, 16]); rv16 = Res()
        cand = ar.f32([128, 8, 256]); rcand = Res()
        cta = ar.f32([128, 256]); rcta = Res()
        c16 = ar.f32([128, 8, 16]); rc16 = Res()
        e16 = ar.f32([128, 8, 16]); re16 = Res()
        zz = ar.f32([128, 8]); rzz = Res()
        for i in range(NT):
            tcs = slice(i * 128, (i + 1) * 128)
            for hg in range(2):
                b0, rb0 = bank("s")
                b1, rb1 = bank("s")
                for h4 in range(4):
                    h = hg * 4 + h4
                    P.mm(b0[:, h4 * 128:(h4 + 1) * 128], pq[0:64, h, tcs], kk[0:64, h, :], True, True, reads=[rpq, rkk], writes=[rb0])
                for h4 in range(4):
                    h = hg * 4 + h4
                    P.mm(b1[:, h4 * 128:(h4 + 1) * 128], pq[64:128, h, tcs], kk[64:128, h, :], True, True, reads=[rpq, rkk], writes=[rb1])
                sv = ssb[:, hg * 8:hg * 8 + 8, :].rearrange("p (h s) k -> p h s k", s=2)
                P.cp(sv[:, :, 0, :], b0[:, :].rearrange("p (a b) -> p a b", a=4), reads=[rb0], writes=[rssb], eng="scalar")
                P.cp(sv[:, :, 1, :], b1[:, :].rearrange("p (a b) -> p a b", a=4), reads=[rb1], writes=[rssb], eng="scalar")
            for r in range(16):
                P.op("vector", lambda e, r=r: e.max(out=v16[:, r, 0:8], in_=ssb[:, r, :]), [rssb], [rv16])
                P.op("vector", lambda e, r=r: e.match_replace(out=tmpa[:], in_to_replace=v16[:, r, 0:8], in_values=ssb[:, r, :], imm_value=-1e30),
                     [rssb, rv16], [rtmpa])
                P.op("vector", lambda e, r=r: e.max(out=v16[:, r, 8:16], in_=tmpa[:]), [rtmpa], [rv16])
            v4 = v16[:, :, :].rearrange("p (h s) k -> p h s k", s=2)
            P.tt(cand[:].rearrange("p h (a b) -> p h a b", a=16),
                 v4[:, :, 0, :].unsqueeze(3).to_broadcast([128, 8, 16, 16]),
                 v4[:, :, 1, :].unsqueeze(2).to_broadcast([128, 8, 16, 16]), ALU.add, reads=[rv16], writes=[rcand], eng="gpsimd")
            for h in range(8):
                P.op("vector", lambda e, h=h: e.max(out=c16[:, h, 0:8], in_=cand[:, h, :]), [rcand], [rc16])
                P.op("vector", lambda e, h=h: e.match_replace(out=cta[:], in_to_replace=c16[:, h, 0:8], in_values=cand[:, h, :], imm_value=-1e30),
                     [rcand, rc16], [rcta])
                P.op("vector", lambda e, h=h: e.max(out=c16[:, h, 8:16], in_=cta[:]), [rcta], [rc16])
            P.ts(tauS[:, i, :], c16[:, :, 15], -2e-5, None, ALU.add, reads=[rc16], writes=[rtau])
            P.tt(e16[:], c16[:, :, :], c16[:, :, 0:1].to_broadcast([128, 8, 16]), ALU.subtract, reads=[rc16], writes=[re16])
            P.act(e16[:], e16[:], AF.Exp, reads=[re16], writes=[re16])
            P.op("vector", lambda e: e.tensor_reduce(out=zz[:], in_=e16[:], axis=AX.X, op=ALU.add), [re16], [rzz])
            P.act(zz[:], zz[:], AF.Ln, reads=[rzz], writes=[rzz])
            P.tt(zz[:], zz[:], c16[:, :, 0], ALU.add, reads=[rzz, rc16], writes=[rzz])
            P.ts(negcS[:, i, :], zz[:], -1.0, None, ALU.mult, reads=[rzz], writes=[rnegc])
        P.tt(tauS[:], tauS[:], negcS[:], ALU.add, reads=[rtau, rnegc], writes=[rtau])
        P.act(tauS[:], tauS[:], AF.Exp, reads=[rtau], writes=[rtau])
        P.barrier()
        ar.off = pk_off
        ust = ar.f32([128, D]); rust = [Res(), Res()]
        vst = ar.f32([128, D]); rvst = [Res(), Res()]
        ub = ar.bf16([128, 2, D]); rub = [Res(), Res()]
        UT = ar.bf16([128, 8, 512]); rUT = Res()
        Vb = ar.bf16([128, 4, D]); rVb = Res()
        gel = [ar.bf16([128, 4, 512]), ar.bf16([128, 4, 512])]; rgel = [Res(), Res()]
        Ebf = [ar.f32([128, 512]) for _ in range(3)]; rEbf = [Res() for _ in range(3)]
        _Gm = ar.bf16([128, 8, 512]); _rGm = [Res() for _ in range(8)]; Gm = [_Gm, _Gm]; rGm = [_rGm, _rGm]
        _AT = ar.bf16([128, 4, 128]); _rAT = Res(); AT = [_AT, _AT]; rAT = [_rAT, _rAT]
        _Gsb = ar.bf16([128, 512]); _rGsb = Res(); Gsb = [_Gsb, _Gsb]; rGsb = [_rGsb, _rGsb]
        ecnt = 0
        ucnt = 0
        tl = {}

        def gate_head(i, gs, h):
            nonlocal ecnt
            tcs = slice(i * 128, (i + 1) * 128)
            bk, rb = bank("s")
            P.mm(bk[:, :], pq[:, h, tcs], Kc[:, h, :, :].rearrange("p a b -> p (a b)"), True, True,
                 reads=[rpq, rKc], writes=[rb])
            ek = ecnt % 3
            ecnt += 1
            P.act(Ebf[ek][:], bk[:, :], AF.Exp, reads=[rb, rnegc], writes=[rEbf[ek]], bias=negcS[:, i, h:h + 1], scale=1.0)
            P.stt(Gm[gs][:, h, :], Ebf[ek][:], tauS[:, i, h:h + 1], Ebf[ek][:], ALU.is_ge, ALU.mult,
                  reads=[rtau, rEbf[ek]], writes=[rGm[gs][h]])

        def tail_a(i, gs, gl, t4):
            bsum, rbsum = bank("a")
            for h in range(8):
                P.mm(bsum[:, :], ident[:], Gm[gs][:, h, :], h == 0, h == 7, reads=[rGm[gs][h], rid], writes=[rbsum])
            P.cp(Gsb[gs][:], bsum[:, :], reads=[rbsum], writes=[rGsb[gs]], eng="scalar")

        def tail_b(i, gs, gl, t4):
            bg, rbg = bank("a")
            for k4 in range(4):
                P.mm(bg[:, k4 * 128:(k4 + 1) * 128], Gsb[gs][:, k4 * 128:(k4 + 1) * 128], ident[:], True, True,
                     reads=[rGsb[gs], rid], writes=[rbg])
            P.tt(AT[gs][:], gel[gl][:, :, t4 * 128:(t4 + 1) * 128], bg[:, :].rearrange("p (a b) -> p a b", a=4), ALU.mult,
                 reads=[rgel[gl], rbg], writes=[rAT[gs]])

        def tail_c(i, gs, gl, t4):
            for hf in range(2):
                bo, rbo = bank("b")
                for k4 in range(4):
                    P.mm(bo[:, :], AT[gs][:, k4, :], Vb[:, k4, hf * 512:(hf + 1) * 512], k4 == 0, k4 == 3,
                         reads=[rAT[gs], rVb], writes=[rbo])
                P.tt(x[:, i, hf * 512:(hf + 1) * 512], x[:, i, hf * 512:(hf + 1) * 512], bo[:, :], ALU.add,
                     reads=[rx[i], rbo], writes=[rx[i]])

        for quad in range(32):
            for k4 in range(4):
                r0 = (quad * 4 + k4) * 128
                kb = k4 % 2
                for hf in range(2):
                    hs = slice(hf * 512, (hf + 1) * 512)
                    P.dma(ust[:, hs], pu_d[l, r0:r0 + 128, hs], writes=[rust[hf]])
                    P.dma(vst[:, hs], pv_d[l, r0:r0 + 128, hs], writes=[rvst[hf]])
                for hf in range(2):
                    hs = slice(hf * 512, (hf + 1) * 512)
                    P.act(ub[:, kb, hs], ust[:, hs], AF.Copy, reads=[rust[hf]], writes=[rub[kb]])
                    P.tt(Vb[:, k4, hs], vst[:, hs], gb[:, 1, hs], ALU.mult, reads=[rvst[hf], rgb], writes=[rVb], eng="vector")
                for c2 in range(2):
                    bk, rb = bank("s")
                    for c4 in range(4):
                        c = c2 * 4 + c4
                        P.mm(bk[:, c4 * 128:(c4 + 1) * 128], ub[:, kb, c * 128:(c + 1) * 128], ident[:], True, True,
                             reads=[rub[kb], rid], writes=[rb])
                    P.cp(UT[:, c2 * 4:(c2 + 1) * 4, k4 * 128:(k4 + 1) * 128], bk[:, :].rearrange("p (a b) -> p a b", a=4),
                         reads=[rb], writes=[rUT], eng="vector")
            for h2 in range(2):
                P.cp(Kc[0:64, h2 * 4:(h2 + 1) * 4], kk[0:64, h2 * 4:(h2 + 1) * 4, quad * 4:quad * 4 + 4].unsqueeze(3).to_broadcast([64, 4, 4, 128]),
                     reads=[rkk], writes=[rKc], eng=("vector" if h2 == 0 else "scalar"))
            prev = None
            for tg in range(4):
                gl = tg % 2
                for k4 in range(4):
                    bk, rb = bank("s")
                    for c in range(8):
                        P.mm(bk[:, :], UT[:, c, k4 * 128:(k4 + 1) * 128], hT[:, c, tg * 512:(tg + 1) * 512], c == 0, c == 7,
                             reads=[rUT, rhT], writes=[rb])
                    P.act(gel[gl][:, k4, :], bk[:, :], AF.Gelu, reads=[rb], writes=[rgel[gl]])
                for t4 in range(4):
                    i = tg * 4 + t4
                    gs = ucnt % 2
                    ucnt += 1
                    if prev is not None:
                        tail_a(*prev)
                    for h in range(8):
                        gate_head(i, gs, h)
                        if prev is not None:
                            if h == 3:
                                tail_b(*prev)
                            elif h == 7:
                                tail_c(*prev)
                    prev = (i, gs, gl, t4)
            tail_a(*prev)
            tail_b(*prev)
            tail_c(*prev)

    P.mute = False
    P.barrier()
    ar.reset()
    fwb = ar.f32([128, D]); rfwb = Res()
    sq = ar.f32([128, D]); rsq = Res()
    yo = [ar.f32([128, D]), ar.f32([128, D])]; ryo = [Res(), Res()]
    P.dma(fwb[:], fnw_d.partition_broadcast(128), writes=[rfwb])
    for i in range(NT):
        if raw:
            P.cp(yo[i % 2][:], x[:, i, :], reads=[rx[i]], writes=[ryo[i % 2]])
        else:
            rstd_of(x[:, i, :], D, small[:, 0:1], [rx[i]], sq, rsq)
            P.stt(yo[i % 2][:], x[:, i, :], small[:, 0:1], fwb[:], ALU.mult, ALU.mult, reads=[rx[i], rsmall, rfwb], writes=[ryo[i % 2]])
        P.dma(y_d[i * 128:(i + 1) * 128, :], yo[i % 2][:], reads=[ryo[i % 2]], q="gpsimd")
    P.emit()
    P.es.close()
    return nc, P


def _na_tables():
    a = np.arange(128)
    types = {}
    for i in range(NT):
        for kt, ty in na_blocks(i):
            types.setdefault(ty, (i, kt))
    DR = np.zeros((26, 128, 128), np.int64); DC = np.zeros((26, 128, 128), np.int64)
    msk_s = np.zeros((NTYPE, 128, 128), np.float32); msk_p = np.zeros((NTYPE, 128, 128), np.float32)
    for ty, (i, kt) in types.items():
        krow = (2 * kt + a // 64)[:, None]; kcol = (a % 64)[:, None]
        qrow = (2 * i + a // 64)[None, :]; qcol = (a % 64)[None, :]
        rs = np.clip(qrow - 4, 0, 24)
        cs = np.clip(qcol - 8, 0, 48)
        valid = (krow >= rs) & (krow < rs + 8) & (kcol >= cs) & (kcol < cs + 16)
        DR[ty] = np.clip(krow - qrow, -7, 7) + 7
        DC[ty] = np.clip(kcol - qcol, -15, 15) + 15
        msk_s[ty] = np.where(valid, 0.0, -1e30)
        msk_p[ty] = 0.0 if (kt // 2 == i // 2) else -1e30
    msk_s[26] = 0.0
    msk_p[26] = -1e30
    return DR, DC, msk_s, msk_p


_CACHE = {}


def kernel(x_prompt, x_sample, cache_na_k, cache_na_v, state_gla_fwd, state_gla_bwd, c, c_ctx,
           w_ada, b_ada, norm1_w, norm2_w, w_in, w_af, b_af, w_ab, b_ab, gla_norm_w,
           conv_w, conv_b, na_rpb, w_out, peer_wq, peer_k1, peer_k2, peer_u, peer_v, final_norm_w):
    f = lambda a: np.ascontiguousarray(np.asarray(a), dtype=np.float32)
    x_prompt, x_sample = f(x_prompt), f(x_sample)
    DR, DC, msk_s, msk_p = _na_tables()
    rpb = f(na_rpb)
    nab_s = np.ascontiguousarray(rpb[:, :, DR, DC])
    nab_p = np.zeros_like(nab_s)
    a = np.arange(128)
    glam = np.stack([(a[:, None] <= a[None, :]), (a[:, None] >= a[None, :])]).astype(np.float32)
    shared = {
        "ident": np.eye(128).astype(NPBF), "glamask": glam,
        "w_ada": f(w_ada), "b_ada": f(b_ada), "norm1_w": f(norm1_w), "norm2_w": f(norm2_w), "w_in": f(w_in),
        "w_af": f(w_af), "b_af": f(b_af), "w_ab": f(w_ab), "b_ab": f(b_ab), "gla_norm_w": f(gla_norm_w),
        "conv_w": f(conv_w), "conv_b": f(conv_b), "w_out": f(w_out), "peer_wq": f(peer_wq),
        "peer_k1": f(peer_k1), "peer_k2": f(peer_k2), "peer_u": f(peer_u), "peer_v": f(peer_v),
        "final_norm_w": f(final_norm_w),
    }
    zk = np.zeros((DEPTH, 6, 256, 64), np.float32)
    zs = np.zeros((DEPTH, 6, 64, 64), np.float32)
    in_maps = []
    for core in range(8):
        m = dict(shared)
        if core < 4:
            b = core
            m.update(x=x_sample[b], cvec=f(c)[b], kctx=f(cache_na_k)[b], vctx=f(cache_na_v)[b],
                     s0f=f(state_gla_fwd)[b], s0b=f(state_gla_bwd)[b], carry=np.ones((128, 1), np.float32),
                     nabias=nab_s, namask=msk_s)
        else:
            b0 = (core - 4) * 4
            xp = x_prompt[b0:b0 + 4].reshape(1024, D)
            m.update(x=np.ascontiguousarray(np.concatenate([xp, xp], 0)), cvec=f(c_ctx), kctx=zk, vctx=zk,
                     s0f=zs, s0b=zs, carry=np.zeros((128, 1), np.float32), nabias=nab_p, namask=msk_p)
        in_maps.append(m)
    if _CACHE.get("only_maps"):
        return in_maps
    if "nc" not in _CACHE:
        _CACHE["nc"] = build()
    nc, P = _CACHE["nc"]
    res = run_bass_kernel_spmd(nc, in_maps, core_ids=list(range(8)))
    R = res.results
    y_sample = np.stack([R[b]["y"] for b in range(4)], 0)
    y_prompt = np.concatenate([R[4 + j]["y"][:1024].reshape(4, 256, D) for j in range(4)], 0)

    def kvout(name):
        o = np.concatenate([R[4 + j][name][:, :1024].reshape(DEPTH, 4, 256, 6, 64) for j in range(4)], 1)
        return np.ascontiguousarray(o.transpose(1, 0, 3, 2, 4))

    def stout(name):
        o = np.concatenate([R[4 + j][name][:, :4] for j in range(4)], 1)
        return np.ascontiguousarray(o.transpose(1, 0, 2, 3, 4))
    return (y_prompt.astype(np.float32), y_sample.astype(np.float32), kvout("nk"), kvout("nv"), stout("sf"), stout("sbo"))
```

```python
import os
import numpy as np
from contextlib import ExitStack
import ml_dtypes
import concourse.bass as bass
import concourse.mybir as mybir
from concourse.bass_utils import run_bass_kernel_spmd


F32 = mybir.dt.float32
BF16 = mybir.dt.bfloat16
AF = mybir.ActivationFunctionType
ALU = mybir.AluOpType
AX = mybir.AxisListType
NPBF = ml_dtypes.bfloat16


class Res:
    __slots__ = ("name", "w", "rd")

    def __init__(self, name="r"):
        self.name = name
        self.w = None
        self.rd = []


class Op:
    __slots__ = ("eng", "fn", "dma", "waits", "signal", "sem", "val", "prevsame")

    def __init__(self, eng, fn, dma):
        self.eng = eng
        self.fn = fn
        self.dma = dma
        self.waits = []
        self.signal = False
        self.sem = None
        self.val = None
        self.prevsame = None


ENGINES = ("tensor", "vector", "scalar", "gpsimd", "sync")
NDMASEM = 16
SEMCH = 30000


class Prog:
    def __init__(self, nc):
        self.nc = nc
        self.streams = {e: [] for e in ENGINES}
        self.es = ExitStack()
        self.nops = 0
        self.pending = {e: [] for e in ENGINES}
        self.out_dma = []
        self.mute = False

    def sb(self, name, shape, dt):
        return self.es.enter_context(self.nc.sbuf_tensor("sb_" + name, list(shape), dt))

    def ps(self, name, shape, dt=F32):
        return self.es.enter_context(self.nc.psum_tensor(name, list(shape), dt))

    def op(self, eng, fn, reads=(), writes=(), dma=False):
        o = Op(eng, fn, dma)
        if self.mute:
            return o
        deps = []
        for r in reads:
            if r.w is not None:
                deps.append(r.w)
        for w in writes:
            if w.w is not None:
                deps.append(w.w)
            deps.extend(w.rd)
        if self.pending[eng]:
            deps.extend(self.pending[eng])
            self.pending[eng] = []
        seen = set()
        for d in deps:
            if d is o or id(d) in seen:
                continue
            seen.add(id(d))
            if d.eng == "tensor" and eng == "tensor" and not d.dma and not dma:
                continue
            d.signal = True
            o.waits.append(d)
        for r in reads:
            r.rd.append(o)
        for w in writes:
            w.w = o
            w.rd = []
        self.streams[eng].append(o)
        self.nops += 1
        if dma:
            self.out_dma.append(o)
        return o

    def barrier(self):
        lasts = []
        for e in ENGINES:
            st = self.streams[e]
            last_c = None
            for o in reversed(st):
                if not o.dma:
                    last_c = o
                    break
            if last_c is not None:
                lasts.append(last_c)
        lasts.extend(self.out_dma)
        self.out_dma = []
        for e in ENGINES:
            self.pending[e] = list(lasts)

    def dma(self, out, in_, reads=(), writes=(), q="sync", **kw):
        return self.op(q, lambda e: e.dma_start(out=out, in_=in_, **kw), reads, writes, dma=True)

    def mm(self, out, lhsT, rhs, start, stop, reads=(), writes=()):
        return self.op("tensor", lambda e: e.matmul(out, lhsT, rhs, start=start, stop=stop), reads, writes)

    def act(self, out, in_, func, reads=(), writes=(), eng="scalar", **kw):
        return self.op(eng, lambda e: e.activation(out=out, in_=in_, func=func, **kw), reads, writes)

    def tt(self, out, in0, in1, op, reads=(), writes=(), eng="vector"):
        return self.op(eng, lambda e: e.tensor_tensor(out=out, in0=in0, in1=in1, op=op), reads, writes)

    def ts(self, out, in0, s1, s2, op0, op1=None, reads=(), writes=(), eng="vector"):
        if op1 is None:
            return self.op(eng, lambda e: e.tensor_scalar(out=out, in0=in0, scalar1=s1, scalar2=None, op0=op0), reads, writes)
        return self.op(eng, lambda e: e.tensor_scalar(out=out, in0=in0, scalar1=s1, scalar2=s2, op0=op0, op1=op1), reads, writes)

    def stt(self, out, in0, scalar, in1, op0, op1, reads=(), writes=(), eng="vector"):
        return self.op(eng, lambda e: e.scalar_tensor_tensor(out=out, in0=in0, scalar=scalar, in1=in1, op0=op0, op1=op1), reads, writes)

    def cp(self, out, in_, reads=(), writes=(), eng="vector"):
        if eng == "scalar":
            return self.op(eng, lambda e: e.copy(out=out, in_=in_), reads, writes)
        return self.op(eng, lambda e: e.tensor_copy(out=out, in_=in_), reads, writes)

    def memset(self, ap, val, writes=(), eng="vector"):
        return self.op(eng, lambda e: e.memset(ap, val), (), writes)

    def emit(self):
        nc = self.nc
        es = self.es
        csems = {}
        dsems = {}
        for e in ENGINES:
            st = self.streams[e]
            nsig = sum(1 for o in st if (o.signal and not o.dma))
            csems[e] = [es.enter_context(nc.semaphore(f"c_{e}_{i}")) for i in range(nsig // SEMCH + 1)]
            ndma = sum(1 for o in st if o.dma)
            dsems[e] = [es.enter_context(nc.semaphore(f"d_{e}_{i}")) for i in range(min(NDMASEM, ndma))]
            cnt = 0
            dcnt = 0
            hist = []
            for o in st:
                if o.dma:
                    k = dcnt
                    o.sem = dsems[e][k % NDMASEM]
                    o.val = 16 * (k // NDMASEM + 1)
                    if k >= NDMASEM:
                        o.prevsame = hist[k - NDMASEM]
                    hist.append(o)
                    dcnt += 1
                elif o.signal:
                    o.sem = csems[e][cnt // SEMCH]
                    o.val = cnt % SEMCH + 1
                    cnt += 1
        finals = []
        for e in ENGINES:
            last = {}
            for o in self.streams[e]:
                if o.dma:
                    last[id(o.sem)] = (o.sem, o.val)
            finals.extend(last.values())
        streams = self.streams
        block = es.enter_context(nc.Block())

        def make(e):
            def run(engh):
                waited = {}
                for o in streams[e]:
                    need = {}
                    for d in o.waits:
                        k = id(d.sem)
                        if k not in need or need[k][1] < d.val:
                            need[k] = (d.sem, d.val)
                    if o.prevsame is not None:
                        k = id(o.sem)
                        v = o.prevsame.val
                        if k not in need or need[k][1] < v:
                            need[k] = (o.sem, v)
                    for k, (sem, val) in need.items():
                        if waited.get(k, 0) < val:
                            engh.wait_ge(sem, val)
                            waited[k] = val
                    ins = o.fn(engh)
                    if o.dma:
                        ins.then_inc(o.sem, 16)
                    elif o.signal:
                        ins.then_inc(o.sem, 1)
                if e == "sync":
                    for sem, val in finals:
                        engh.wait_ge(sem, val)
            return run

        for e in ENGINES:
            getattr(block, e)(make(e))


D = 1024
NT = 16
T = 2048
DEPTH = 4
INW = 3488
QA0, KA0, VA0, GA0, LRF0, LRB0 = 0, 384, 768, 1152, 1536, 1552
CH0, CB0, CC0 = 1568, 1824, 2080
QN0, KN0, VN0 = 2336, 2720, 3104
NTYPE = 27
ARENA_W = 25856


def na_blocks(i):
    if 2 <= i <= 13:
        return [(i - 2 + m, (i % 2) * 5 + m) for m in range(5)]
    e = {0: 0, 1: 1, 14: 2, 15: 3}[i]
    base = 0 if i < 2 else 12
    return [(base + m, 10 + e * 4 + m) for m in range(4)]


class Arena:
    def __init__(self, P):
        self.t = P.sb("arena", [128, ARENA_W], F32)
        self.off = 0

    def reset(self):
        self.off = 0

    def f32(self, shape):
        n = int(np.prod(shape[1:]))
        v = self.t[:, self.off:self.off + n]
        self.off += n
        assert self.off <= ARENA_W, self.off
        return self._shape(v, shape)

    def bf16(self, shape):
        n = int(np.prod(shape[1:]))
        assert n % 2 == 0
        v = self.t[:, self.off:self.off + n // 2].bitcast(BF16)
        self.off += n // 2
        assert self.off <= ARENA_W, self.off
        return self._shape(v, shape)

    @staticmethod
    def _shape(v, shape):
        if len(shape) == 2:
            return v
        if len(shape) == 3:
            return v.rearrange("p (a b) -> p a b", a=shape[1], b=shape[2])
        if len(shape) == 4:
            return v.rearrange("p (a b c) -> p a b c", a=shape[1], b=shape[2], c=shape[3])
        raise ValueError(shape)


def build(nlayers=DEPTH, phases="CNGP", raw=False):
    nc = bass.Bass("TRN2", target_bir_lowering=False)

    def din(name, shape, dt=F32):
        return nc.dram_tensor(name, list(shape), dt, kind="ExternalInput").ap()

    def dout(name, shape):
        return nc.dram_tensor(name, list(shape), F32, kind="ExternalOutput").ap()

    x_d = din("x", [T, D]); cvec_d = din("cvec", [D])
    kctx_d = din("kctx", [DEPTH, 6, 256, 64]); vctx_d = din("vctx", [DEPTH, 6, 256, 64])
    s0f_d = din("s0f", [DEPTH, 6, 64, 64]); s0b_d = din("s0b", [DEPTH, 6, 64, 64])
    carry_d = din("carry", [128, 1])
    nabias_d = din("nabias", [DEPTH, 6, 26, 128, 128]); namask_d = din("namask", [NTYPE, 128, 128])
    ident_d = din("ident", [128, 128], BF16)
    glam_d = din("glamask", [2, 128, 128])
    w_ada_d = din("w_ada", [DEPTH, D, 6 * D]); b_ada_d = din("b_ada", [DEPTH, 6 * D])
    n1_d = din("norm1_w", [DEPTH, D]); n2_d = din("norm2_w", [DEPTH, D])
    w_in_d = din("w_in", [DEPTH, D, INW])
    waf_d = din("w_af", [DEPTH, 16, 384]); baf_d = din("b_af", [DEPTH, 384])
    wab_d = din("w_ab", [DEPTH, 16, 384]); bab_d = din("b_ab", [DEPTH, 384])
    gnw_d = din("gla_norm_w", [DEPTH, 384])
    cw_d = din("conv_w", [DEPTH, 3, 256]); cb_d = din("conv_b", [DEPTH, 256])
    wout_d = din("w_out", [DEPTH, D, D]); wq_d = din("peer_wq", [DEPTH, D, D])
    k1_d = din("peer_k1", [DEPTH, 8, 128, 64]); k2_d = din("peer_k2", [DEPTH, 8, 128, 64])
    pu_d = din("peer_u", [DEPTH, 16384, D]); pv_d = din("peer_v", [DEPTH, 16384, D])
    fnw_d = din("final_norm_w", [D])
    y_d = dout("y", [T, D]); nk_d = dout("nk", [DEPTH, T, 384]); nv_d = dout("nv", [DEPTH, T, 384])
    sf_d = dout("sf", [DEPTH, 8, 6, 64, 64]); sbo_d = dout("sbo", [DEPTH, 8, 6, 64, 64])

    P = Prog(nc)
    NSC = dict(allow_slow_non_contiguous=True)
    x = P.sb("x", [128, NT, D], F32); rx = [Res() for _ in range(NT)]
    hT = P.sb("hT", [128, 8, T], BF16); rhT = Res()
    ident = P.sb("ident", [128, 128], BF16); rid = Res()
    identf = P.sb("identf", [128, 128], F32)
    onesf = P.sb("onesf", [128, 128], F32)
    glam = P.sb("glam", [128, 2, 128], F32)
    carry = P.sb("carry", [128, 1], F32)
    carm1 = P.sb("carm1", [128, 1], F32)
    scol = P.sb("scol", [128, 8], F32)
    modcol = P.sb("modcol", [128, 4, 8], F32); rmod = Res()
    acol = P.sb("acol", [128, 2, 8], F32)
    nwcol = P.sb("nwcol", [128, 2, 8], F32)
    gb = P.sb("gb", [128, 2, D], F32); rgb = Res()
    small = P.sb("small", [128, 64], F32); rsmall = Res()
    rconst = Res()
    ar = Arena(P)
    banks = [P.ps(f"bank{i}", [128, 512], F32) for i in range(8)]
    rbank = [Res() for _ in range(8)]
    rr = {"s": 0, "a": 0, "b": 0}

    def bank(pool):
        if pool == "s":
            i = rr["s"] % 3
        elif pool == "a":
            i = 3 + rr["a"] % 3
        else:
            i = 6 + rr["b"] % 2
        rr[pool] += 1
        return banks[i], rbank[i]

    for i in range(NT):
        P.dma(x[:, i, :], x_d[i * 128:(i + 1) * 128, :], writes=[rx[i]])
    P.dma(ident[:], ident_d[:, :], writes=[rid])
    P.dma(glam[:], glam_d.rearrange("a s t -> s a t"), writes=[rconst])
    P.dma(carry[:], carry_d[:, :], writes=[rconst])
    P.dma(scol[:], cvec_d.rearrange("(k p) -> p k", p=128), writes=[rconst], **NSC)
    P.cp(identf[:], ident[:], reads=[rid], writes=[rconst])
    P.memset(onesf[:], 1.0, writes=[rconst])
    P.ts(carm1[:], carry[:], -1.0, None, ALU.add, reads=[rconst], writes=[rconst])
    P.act(scol[:], scol[:], AF.Silu, reads=[rconst], writes=[rconst])

    def rstd_of(src_ap, n, out_ap, rsrc, scratch, rscratch):
        P.act(scratch, src_ap, AF.Square, reads=rsrc, writes=[rscratch, rsmall], accum_out=out_ap)
        P.ts(out_ap, out_ap, 1.0 / n, 1e-6, ALU.mult, ALU.add, reads=[rsmall], writes=[rsmall])
        P.act(out_ap, out_ap, AF.Sqrt, reads=[rsmall], writes=[rsmall])
        P.op("vector", lambda e: e.reciprocal(out=out_ap, in_=out_ap), [rsmall], [rsmall])

    def norm_to_hT(which):
        ar.reset()
        sq = ar.f32([128, D]); rsq = Res()
        xn = ar.bf16([128, 2, D]); rxn = [Res(), Res()]
        for i in range(NT):
            rstd_of(x[:, i, :], D, small[:, 0:1], [rx[i]], sq, rsq)
            P.ts(xn[:, i % 2, :], x[:, i, :], small[:, 0:1], None, ALU.mult, reads=[rx[i], rsmall], writes=[rxn[i % 2]])
            for hf in range(2):
                bk, rb = bank("s")
                for c4 in range(4):
                    c = hf * 4 + c4
                    P.mm(bk[:, c4 * 128:(c4 + 1) * 128], xn[:, i % 2, c * 128:(c + 1) * 128], ident[:], True, True,
                         reads=[rxn[i % 2], rid], writes=[rb])
                for c4 in range(4):
                    c = hf * 4 + c4
                    P.ts(hT[:, c, i * 128:(i + 1) * 128], bk[:, c4 * 128:(c4 + 1) * 128], acol[:, which, c:c + 1],
                         modcol[:, 2 * which, c:c + 1], ALU.mult, ALU.add, reads=[rb, rmod], writes=[rhT],
                         eng=("vector" if c4 % 2 == 0 else "gpsimd") if False else "vector")

    wst = [None, None]; rwst = [Res(), Res()]; wbf = [None, None]; rwbf = [Res(), Res()]
    wcnt = [0]

    def alloc_w():
        for k in range(2):
            wst[k] = ar.f32([128, 8, 128])
            wbf[k] = ar.bf16([128, 8, 128])

    def wcols(dram_ap, n, dst=None, rdst=None, dcol=0):
        k = wcnt[0] % 2
        wcnt[0] += 1
        P.dma(wst[k][:, :, 0:n], dram_ap.rearrange("(c p) n -> p c n", p=128), writes=[rwst[k]])
        if dst is None:
            dst, rdst, dcol = wbf[k], rwbf[k], 0
        P.cp(dst[:, :, dcol:dcol + n], wst[k][:, :, 0:n], reads=[rwst[k]], writes=[rdst], eng="gpsimd")
        return dst, rdst

    def proj_fm(wb, rwb, wcol0, m, evac):
        for tg in range(4):
            bk, rb = bank("s")
            for c in range(8):
                P.mm(bk[0:m, :], wb[:, c, wcol0:wcol0 + m], hT[:, c, tg * 512:(tg + 1) * 512], c == 0, c == 7,
                     reads=[rwb, rhT], writes=[rb])
            evac(tg, bk, rb)

    def outproj(cat, rcat, row0, nch, l):
        wo_st = ar.f32([128, D]); rwo_st = Res()
        wob = ar.bf16([128, nch, D]); rwob = Res()
        for j in range(nch):
            P.dma(wo_st[:], wout_d[l, row0 + j * 128: row0 + (j + 1) * 128, :], writes=[rwo_st])
            P.tt(wob[:, j, :], wo_st[:], gb[:, 0, :], ALU.mult, reads=[rwo_st, rgb], writes=[rwob], eng="gpsimd")
        for i in range(NT):
            for hf in range(2):
                bk, rb = bank("b")
                for j in range(nch):
                    P.mm(bk[:, :], cat[:, j, i * 128:(i + 1) * 128], wob[:, j, hf * 512:(hf + 1) * 512], j == 0, j == nch - 1,
                         reads=[rcat, rwob], writes=[rb])
                P.tt(x[:, i, hf * 512:(hf + 1) * 512], x[:, i, hf * 512:(hf + 1) * 512], bk[:, :], ALU.add,
                     reads=[rx[i], rb], writes=[rx[i]])

    for l in range(nlayers):
        P.barrier()
        ar.reset()
        wa = [ar.f32([128, 8, 512]), ar.f32([128, 8, 512])]; rwa = [Res(), Res()]
        row = ar.f32([1, 512]); rrow = Res()
        brow = ar.f32([1, 512]); rbrow = Res()
        P.dma(nwcol[:, 0, :], n1_d[l].rearrange("(k p) -> p k", p=128), writes=[rmod], **NSC)
        P.dma(nwcol[:, 1, :], n2_d[l].rearrange("(k p) -> p k", p=128), writes=[rmod], **NSC)
        for g in range(12):
            k = g % 2
            P.dma(wa[k][:], w_ada_d[l, :, g * 512:(g + 1) * 512].rearrange("(c p) n -> p c n", p=128), writes=[rwa[k]])
            P.dma(brow[0:1, :], b_ada_d[l:l + 1, g * 512:(g + 1) * 512], writes=[rbrow])
            bk, rb = bank("s")
            for c in range(8):
                P.mm(bk[0:1, :], scol[:, c:c + 1], wa[k][:, c, :], c == 0, c == 7, reads=[rconst, rwa[k]], writes=[rb])
            P.tt(row[0:1, :], bk[0:1, :], brow[0:1, :], ALU.add, reads=[rb, rbrow], writes=[rrow])
            vec = g // 2
            if vec in (0, 1, 3, 4):
                mi = {0: 0, 1: 1, 3: 2, 4: 3}[vec]
                bk2, rb2 = bank("s")
                for j in range(4):
                    P.mm(bk2[:, j:j + 1], row[0:1, j * 128:(j + 1) * 128], onesf[0:1, 0:1], True, True,
                         reads=[rrow, rconst], writes=[rb2])
                P.cp(modcol[:, mi, k * 4:(k + 1) * 4], bk2[:, 0:4], reads=[rb2], writes=[rmod])
            else:
                gi = 0 if vec == 2 else 1
                bk2, rb2 = bank("s")
                P.mm(bk2[:, :], onesf[0:1, 0:128], row[0:1, :], True, True, reads=[rrow, rconst], writes=[rb2])
                P.cp(gb[:, gi, k * 512:(k + 1) * 512], bk2[:, :], reads=[rb2], writes=[rgb])
        for w_ in range(2):
            P.stt(acol[:, w_, :], modcol[:, 2 * w_ + 1, :], 1.0, nwcol[:, w_, :], ALU.add, ALU.mult,
                  reads=[rmod], writes=[rmod])
        P.barrier()
        norm_to_hT(0)

        P.mute = "C" not in phases
        P.barrier()
        ar.reset()
        alloc_w()
        zc = ar.f32([128, 3, T]); rzc = Res()
        u = ar.f32([128, T]); ru = Res()
        yv = ar.f32([128, T]); ryv = Res()
        catC = ar.bf16([128, 2, T]); rcatC = Res()
        cwc = ar.f32([128, 2, 4]); rcwc = Res()
        wn = ar.f32([128, 2, 2]); rwn = Res()
        for j in range(2):
            for k_ in range(3):
                P.dma(cwc[:, j, k_:k_ + 1], cw_d[l, k_, j * 128:(j + 1) * 128].rearrange("(p o) -> p o", o=1), writes=[rcwc], **NSC)
            P.dma(cwc[:, j, 3:4], cb_d[l, j * 128:(j + 1) * 128].rearrange("(p o) -> p o", o=1), writes=[rcwc], **NSC)
        for j in range(2):
            P.ts(wn[:, j, 0:1], cwc[:, j, 0:1], carm1[:, 0:1], None, ALU.mult, reads=[rcwc, rconst], writes=[rwn])
            P.ts(wn[:, j, 1:2], cwc[:, j, 2:3], carm1[:, 0:1], None, ALU.mult, reads=[rcwc, rconst], writes=[rwn])
        for j in range(2):
            for wi, c0 in enumerate((CH0, CB0, CC0)):
                wb, rwb = wcols(w_in_d[l, :, c0 + j * 128:c0 + (j + 1) * 128], 128)
                proj_fm(wb, rwb, 0, 128, lambda tg, bk, rb, wi=wi: P.cp(zc[:, wi, tg * 512:(tg + 1) * 512], bk[:, :],
                                                                      reads=[rb], writes=[rzc], eng="scalar"))
            P.tt(u[:], zc[:, 2, :], zc[:, 0, :], ALU.mult, reads=[rzc], writes=[ru], eng="gpsimd")
            P.ts(yv[:], u[:], cwc[:, j, 1:2], None, ALU.mult, reads=[ru, rcwc], writes=[ryv])
            P.stt(yv[:, 1:T], u[:, 0:T - 1], cwc[:, j, 0:1], yv[:, 1:T], ALU.mult, ALU.add, reads=[ru, rcwc, ryv], writes=[ryv])
            P.stt(yv[:, 0:T - 1], u[:, 1:T], cwc[:, j, 2:3], yv[:, 0:T - 1], ALU.mult, ALU.add, reads=[ru, rcwc, ryv], writes=[ryv])
            P.stt(yv[:, 256:T:256], u[:, 255:T - 1:256], wn[:, j, 0:1], yv[:, 256:T:256], ALU.mult, ALU.add,
                  reads=[ru, rwn, ryv], writes=[ryv])
            P.stt(yv[:, 255:T - 1:256], u[:, 256:T:256], wn[:, j, 1:2], yv[:, 255:T - 1:256], ALU.mult, ALU.add,
                  reads=[ru, rwn, ryv], writes=[ryv])
            P.stt(catC[:, j, :], yv[:], cwc[:, j, 3:4], zc[:, 1, :], ALU.add, ALU.mult, reads=[ryv, rcwc, rzc], writes=[rcatC])
        outproj(catC, rcatC, 384, 2, l)

        P.mute = "N" not in phases
        P.barrier()
        ar.reset()
        alloc_w()
        namask = ar.f32([128, NTYPE, 128]); rnamask = Res()
        P.dma(namask[:], namask_d.rearrange("t k q -> k t q"), writes=[rnamask])
        nab = ar.f32([128, 26, 128]); rnab = Res()
        BM = ar.bf16([128, 2, NTYPE, 128]); rBM = Res()
        qT = ar.bf16([128, T]); rqT = Res()
        kT = ar.bf16([128, T]); rkT = Res()
        wkv = ar.bf16([128, 8, 256]); rwkv = Res()
        kvst = [ar.f32([128, 256]), ar.f32([128, 256])]; rkvst = [Res(), Res()]
        vaug = ar.bf16([128, NT, 2, 66]); rvaug = Res()
        cst = ar.f32([128, 2, 2, 64]); rcst = Res()
        kcb = ar.bf16([128, 2, 128]); rkcb = Res()
        kcT = ar.bf16([128, 256]); rkcT = Res()
        vca = ar.bf16([128, 2, 2, 66]); rvca = Res()
        pTw = [ar.bf16([128, 8, 128]) for _ in range(2)]; rpTw = [Res() for _ in range(2)]
        opair = [ar.bf16([128, 128]) for _ in range(2)]; ropair = [Res(), Res()]
        catN = ar.bf16([128, 1, T]); rcatN = Res()
        P.memset(vaug[:, :, :, 64:66], 1.0, writes=[rvaug])
        P.memset(vca[:, :, :, 64:66], 1.0, writes=[rvca])
        pcnt = 0
        for p in range(3):
            wb, rwb = wcols(w_in_d[l, :, QN0 + p * 128:QN0 + (p + 1) * 128], 128)
            proj_fm(wb, rwb, 0, 128, lambda tg, bk, rb: P.act(qT[:, tg * 512:(tg + 1) * 512], bk[:, :], AF.Copy,
                                                             reads=[rb], writes=[rqT], scale=0.125))
            wb, rwb = wcols(w_in_d[l, :, KN0 + p * 128:KN0 + (p + 1) * 128], 128)
            proj_fm(wb, rwb, 0, 128, lambda tg, bk, rb: P.cp(kT[:, tg * 512:(tg + 1) * 512], bk[:, :],
                                                            reads=[rb], writes=[rkT], eng="scalar"))
            wcols(w_in_d[l, :, KN0 + p * 128:KN0 + (p + 1) * 128], 128, wkv, rwkv, 0)
            wcols(w_in_d[l, :, VN0 + p * 128:VN0 + (p + 1) * 128], 128, wkv, rwkv, 128)
            for i in range(NT):
                bk, rb = bank("s")
                for c in range(8):
                    P.mm(bk[:, 0:256], hT[:, c, i * 128:(i + 1) * 128], wkv[:, c, :], c == 0, c == 7,
                         reads=[rhT, rwkv], writes=[rb])
                kk_ = i % 2
                P.cp(kvst[kk_][:], bk[:, 0:256], reads=[rb], writes=[rkvst[kk_]], eng="scalar")
                P.dma(nk_d[l, i * 128:(i + 1) * 128, p * 128:(p + 1) * 128], kvst[kk_][:, 0:128], reads=[rkvst[kk_]], q="gpsimd")
                P.dma(nv_d[l, i * 128:(i + 1) * 128, p * 128:(p + 1) * 128], kvst[kk_][:, 128:256], reads=[rkvst[kk_]], q="gpsimd")
                P.cp(vaug[:, i, :, 0:64], kvst[kk_][:, 128:256].rearrange("p (h d) -> p h d", h=2),
                     reads=[rkvst[kk_]], writes=[rvaug])
            for kt in range(2):
                P.dma(cst[:, kt, :, :], kctx_d[l, 2 * p:2 * p + 2, kt * 128:(kt + 1) * 128, :].rearrange("h q d -> q h d"), writes=[rcst])
            P.cp(kcb[:].rearrange("p kt (h d) -> p kt h d", h=2), cst[:], reads=[rcst], writes=[rkcb])
            for kt in range(2):
                bk, rb = bank("s")
                P.mm(bk[:, 0:128], kcb[:, kt, :], ident[:], True, True, reads=[rkcb, rid], writes=[rb])
                P.cp(kcT[:, kt * 128:(kt + 1) * 128], bk[:, 0:128], reads=[rb], writes=[rkcT], eng="scalar")
            for kt in range(2):
                P.dma(cst[:, kt, :, :], vctx_d[l, 2 * p:2 * p + 2, kt * 128:(kt + 1) * 128, :].rearrange("h q d -> q h d"), writes=[rcst])
            P.cp(vca[:, :, :, 0:64], cst[:], reads=[rcst], writes=[rvca])
            for hh in range(2):
                P.dma(nab[:], nabias_d[l, 2 * p + hh].rearrange("t k q -> k t q"), writes=[rnab])
                P.tt(BM[:, hh, 0:26, :], nab[:], namask[:, 0:26, :], ALU.add, reads=[rnab, rnamask], writes=[rBM])
                P.cp(BM[:, hh, 26, :], namask[:, 26, :], reads=[rnamask], writes=[rBM])
            for i in range(NT):
                for hh in range(2):
                    hp = slice(hh * 64, hh * 64 + 64)
                    blocks = [("l", kt, ty) for kt, ty in na_blocks(i)] + [("c", 0, 26), ("c", 1, 26)]
                    nb_ = len(blocks)
                    pw = pTw[pcnt % 2]; rpw = rpTw[pcnt % 2]
                    pcnt += 1
                    for g0 in range(0, nb_, 4):
                        grp = blocks[g0:g0 + 4]
                        bs, rbs = bank("s")
                        for j, (src, kt, ty) in enumerate(grp):
                            ksrc, rks = (kT, rkT) if src == "l" else (kcT, rkcT)
                            P.mm(bs[:, j * 128:(j + 1) * 128], ksrc[hp, kt * 128:(kt + 1) * 128], qT[hp, i * 128:(i + 1) * 128], True, False,
                                 reads=[rks, rqT], writes=[rbs])
                            P.mm(bs[:, j * 128:(j + 1) * 128], ident[:], BM[:, hh, ty, :], False, True, reads=[rid, rBM], writes=[rbs])
                        n_ = len(grp)
                        P.act(pw[:, g0:g0 + n_, :], bs[:, 0:n_ * 128].rearrange("p (a b) -> p a b", a=n_), AF.Exp,
                              reads=[rbs], writes=[rpw])
                    bo, rbo = bank("a")
                    for bi, (src, kt, ty) in enumerate(blocks):
                        vsrc, rvs = (vaug[:, kt, hh, 0:65], rvaug) if src == "l" else (vca[:, kt, hh, 0:65], rvca)
                        P.mm(bo[:, 0:65], pw[:, bi, :], vsrc, bi == 0, bi == nb_ - 1, reads=[rpw, rvs], writes=[rbo])
                    P.op("vector", lambda e, bo=bo: e.reciprocal(out=small[:, 1:2], in_=bo[:, 64:65]), [rbo], [rsmall])
                    P.ts(opair[i % 2][:, hp], bo[:, 0:64], small[:, 1:2], None, ALU.mult, reads=[rbo, rsmall], writes=[ropair[i % 2]])
                bk, rb = bank("b")
                P.mm(bk[:, 0:128], opair[i % 2][:], ident[:], True, True, reads=[ropair[i % 2], rid], writes=[rb])
                P.cp(catN[:, 0, i * 128:(i + 1) * 128], bk[:, 0:128], reads=[rb], writes=[rcatN], eng="scalar")
            off_save = ar.off
            outproj(catN, rcatN, 640 + p * 128, 1, l)
            ar.off = off_save
            P.barrier()

        P.mute = "G" not in phases
        ar.reset()
        alloc_w()
        lrT = [ar.bf16([16, T]), ar.bf16([16, T])]; rlrT = Res()
        wa16 = ar.f32([16, 2, 384]); rwa16 = Res()
        wab16 = ar.bf16([16, 2, 384]); rwab16 = Res()
        P.dma(wa16[0:16, 0, :], waf_d[l], writes=[rwa16])
        P.dma(wa16[0:16, 1, :], wab_d[l], writes=[rwa16])
        P.cp(wab16[0:16], wa16[0:16], reads=[rwa16], writes=[rwab16])
        for d_ in range(2):
            wb, rwb = wcols(w_in_d[l, :, LRF0 + d_ * 16:LRF0 + (d_ + 1) * 16], 16)
            proj_fm(wb, rwb, 0, 16, lambda tg, bk, rb, d_=d_: P.cp(lrT[d_][0:16, tg * 512:(tg + 1) * 512], bk[0:16, :],
                                                                  reads=[rb], writes=[rlrT], eng="scalar"))
        gqT = ar.bf16([128, T]); rgq = Res()
        gkT = ar.bf16([128, T]); rgk = Res()
        a_off = ar.off
        A_ = ar.f32([128, T]); rA = Res()
        B_ = ar.f32([128, T]); rB = Res()
        Eb = ar.bf16([128, T]); rEb = Res()
        totc = ar.f32([128, NT]); rtotc = Res()
        _qd = ar.bf16([128, T]); _rqd = Res(); qd = [_qd, _qd]; rqd = [_rqd, _rqd]
        _kd = ar.bf16([128, T]); _rkd = Res(); kd = [_kd, _kd]; rkd = [_rkd, _rkd]
        _kdec = ar.bf16([128, T]); _rkdec = Res(); kdec = [_kdec, _kdec]; rkdec = [_rkdec, _rkdec]
        _kdt = ar.bf16([128, NT, 128]); _rkdt = Res(); kdtok = [_kdt, _kdt]; rkdtok = [_rkdt, _rkdt]
        elast = ar.f32([128, 2, NT]); rel = Res()
        wvg = ar.bf16([128, 8, 256]); rwvg = Res()
        v_b = ar.bf16([128, NT, 128]); rvb = Res()
        sg = ar.bf16([128, NT, 128]); rsg = Res()
        of = ar.f32([128, NT, 128]); rof = Res()
        ycat = sg; rycat = rsg
        catG = gqT.rearrange("p (o t) -> p o t", o=1); rcatG = rgq
        nb = ar.f32([128, 2]); rnb = Res()
        gwb = ar.f32([128, 128]); rgwb = Res()
        S = ar.f32([128, 64]); rS = Res()
        Sb = ar.bf16([128, 64]); rSb = Res()
        s0 = ar.f32([128, 2, 64]); rs0 = Res()
        sst = ar.f32([128, 2, 8, 64]); rsst = Res()
        attm = [ar.bf16([128, 2, 128]), ar.bf16([128, 2, 128])]; rattm = [Res(), Res()]
        ot = ar.f32([128, 128]); rot = Res()
        osq = ar.f32([128, 128]); rosq = Res()
        Bv = B_.rearrange("p (c j) -> p c j", j=128)
        totb = totc[:, :].unsqueeze(2).to_broadcast([128, NT, 128])
        Av = A_.rearrange("p (c j) -> p c j", j=128)
        for p in range(3):
            cs = slice(p * 128, (p + 1) * 128)
            wb, rwb = wcols(w_in_d[l, :, QA0 + p * 128:QA0 + (p + 1) * 128], 128)
            proj_fm(wb, rwb, 0, 128, lambda tg, bk, rb: P.act(gqT[:, tg * 512:(tg + 1) * 512], bk[:, :], AF.Copy,
                                                             reads=[rb], writes=[rgq], scale=0.125))
            wb, rwb = wcols(w_in_d[l, :, KA0 + p * 128:KA0 + (p + 1) * 128], 128)
            proj_fm(wb, rwb, 0, 128, lambda tg, bk, rb: P.cp(gkT[:, tg * 512:(tg + 1) * 512], bk[:, :],
                                                            reads=[rb], writes=[rgk], eng="scalar"))
            wcols(w_in_d[l, :, VA0 + p * 128:VA0 + (p + 1) * 128], 128, wvg, rwvg, 0)
            wcols(w_in_d[l, :, GA0 + p * 128:GA0 + (p + 1) * 128], 128, wvg, rwvg, 128)
            for i in range(NT):
                bk, rb = bank("s")
                for c in range(8):
                    P.mm(bk[:, 0:256], hT[:, c, i * 128:(i + 1) * 128], wvg[:, c, :], c == 0, c == 7,
                         reads=[rhT, rwvg], writes=[rb])
                P.cp(v_b[:, i, :], bk[:, 0:128], reads=[rb], writes=[rvb], eng="scalar")
                P.act(sg[:, i, :], bk[:, 128:256], AF.Silu, reads=[rb], writes=[rsg])
            if os.environ.get("GSTOP") == "b":
                P.mute = True
            P.dma(nb[:, 0:1], baf_d[l, cs].rearrange("(p o) -> p o", o=1), writes=[rnb], **NSC)
            P.dma(nb[:, 1:2], bab_d[l, cs].rearrange("(p o) -> p o", o=1), writes=[rnb], **NSC)
            P.ts(nb[:], nb[:], -1.0, None, ALU.mult, reads=[rnb], writes=[rnb])
            P.dma(gwb[:], gnw_d[l, cs].partition_broadcast(128), writes=[rgwb])
            P.dma(s0[:, 0, :], s0f_d[l, 2 * p:2 * p + 2].rearrange("h k v -> (h k) v"), writes=[rs0])
            P.dma(s0[:, 1, :], s0b_d[l, 2 * p:2 * p + 2].rearrange("h k v -> (h k) v"), writes=[rs0])
            for d_ in range(2):
                for tg in range(4):
                    bk, rb = bank("s")
                    P.mm(bk[:, :], wab16[0:16, d_, cs], lrT[d_][0:16, tg * 512:(tg + 1) * 512], True, True,
                         reads=[rwab16, rlrT], writes=[rb])
                    P.act(A_[:, tg * 512:(tg + 1) * 512], bk[:, :], AF.Exp, reads=[rb, rnb], writes=[rA],
                          scale=-1.0, bias=nb[:, d_:d_ + 1])
                P.act(A_[:], A_[:], AF.Ln, reads=[rA], writes=[rA], bias=1.0)
                for c in range(NT):
                    P.op("vector", lambda e, c=c: e.tensor_tensor_scan(out=B_[:, c * 128:(c + 1) * 128], data0=onesf[:, :],
                                                                      data1=A_[:, c * 128:(c + 1) * 128], initial=0.0,
                                                                      op0=ALU.mult, op1=ALU.add), [rA, rconst], [rB])
                P.cp(totc[:, :], B_[:, 127:T:128], reads=[rB], writes=[rtotc])
                if d_ == 0:
                    RC, rRC, FR, rFR, FRv = B_, rB, A_, rA, Av
                else:
                    P.tt(A_[:], A_[:], B_[:], ALU.subtract, reads=[rA, rB], writes=[rA], eng="gpsimd")
                    P.tt(Av, Av, totb, ALU.add, reads=[rA, rtotc], writes=[rA], eng="gpsimd")
                    RC, rRC, FR, rFR, FRv = A_, rA, B_, rB, Bv
                RCv = RC.rearrange("p (c j) -> p c j", j=128)
                P.act(Eb[:], RC[:], AF.Exp, reads=[rRC], writes=[rEb], scale=-1.0 / 16)
                P.tt(qd[d_][:], gqT[:], Eb[:], ALU.mult, reads=[rgq, rEb], writes=[rqd[d_]])
                P.act(Eb[:], RC[:], AF.Exp, reads=[rRC], writes=[rEb], scale=1.0 / 16)
                P.tt(kd[d_][:], gkT[:], Eb[:], ALU.mult, reads=[rgk, rEb], writes=[rkd[d_]])
                P.tt(FRv, RCv, totb, ALU.subtract, reads=[rRC, rtotc], writes=[rFR], eng="gpsimd")
                P.act(Eb[:], FR[:], AF.Exp, reads=[rFR], writes=[rEb], scale=1.0 / 16)
                P.tt(kdec[d_][:], gkT[:], Eb[:], ALU.mult, reads=[rgk, rEb], writes=[rkdec[d_]])
                P.act(elast[:, d_, :], totc[:, :], AF.Exp, reads=[rtotc], writes=[rel], scale=-1.0 / 16)
                for c in range(NT):
                    bk, rb = bank("s")
                    P.mm(bk[:, 0:128], kdec[d_][:, c * 128:(c + 1) * 128], ident[:], True, True, reads=[rkdec[d_], rid], writes=[rb])
                    P.cp(kdtok[d_][:, c, :], bk[:, 0:128], reads=[rb], writes=[rkdtok[d_]], eng="scalar")
                if os.environ.get("GSTOP") == "c":
                    P.mute = True
                order = list(range(NT)) if d_ == 0 else list(range(NT - 1, -1, -1))
                P.cp(S[:], s0[:, d_, :], reads=[rs0], writes=[rS])
                for n_, c in enumerate(order):
                    ccs = slice(c * 128, (c + 1) * 128)
                    if n_ > 0 and n_ % 2 == 0:
                        P.ts(S[:], S[:], carry[:, 0:1], None, ALU.mult, reads=[rS, rconst], writes=[rS])
                    P.cp(Sb[:], S[:], reads=[rS], writes=[rSb], eng="scalar")
                    am = attm[n_ % 2]; ram = rattm[n_ % 2]
                    GSK = os.environ.get("GSKIP", "")
                    _m0 = P.mute
                    bos = []
                    for hh in range(2):
                        hp = slice(hh * 64, hh * 64 + 64)
                        ba, rba = bank("s")
                        P.mm(ba[:, 0:128], kd[d_][hp, ccs], qd[d_][hp, ccs], True, True,
                             reads=[rkd[d_], rqd[d_]], writes=[rba])
                        P.tt(am[:, hh, :], ba[:, 0:128], glam[:, d_, :], ALU.mult, reads=[rba, rconst], writes=[ram])
                    for hh in range(2):
                        hp = slice(hh * 64, hh * 64 + 64)
                        bo, rbo = bank("a")
                        bos.append((bo, rbo))
                        P.mm(bo[:, 0:64], am[:, hh, :], v_b[:, c, hh * 64:(hh + 1) * 64], True, False,
                             reads=[ram, rvb], writes=[rbo])
                        P.mm(bo[:, 0:64], qd[d_][hp, ccs], Sb[hp, :], False, True,
                             reads=[rqd[d_], rSb], writes=[rbo])
                    if d_ == 0:
                        for hh in range(2):
                            P.cp(of[:, c, hh * 64:(hh + 1) * 64], bos[hh][0][:, 0:64], reads=[bos[hh][1]], writes=[rof], eng="scalar")
                    else:
                        for hh in range(2):
                            P.tt(ot[:, hh * 64:(hh + 1) * 64], of[:, c, hh * 64:(hh + 1) * 64], bos[hh][0][:, 0:64], ALU.add,
                                 reads=[rof, bos[hh][1]], writes=[rot])
                        P.tt(osq[:], ot[:], ot[:], ALU.mult, reads=[rot], writes=[rosq], eng="gpsimd")
                        P.op("vector", lambda e: e.tensor_reduce(out=small[:, 2:4], in_=osq[:].rearrange("p (h d) -> p h d", h=2),
                                                                 axis=AX.X, op=ALU.add), [rosq], [rsmall])
                        P.ts(small[:, 2:4], small[:, 2:4], 1.0 / 64, 1e-6, ALU.mult, ALU.add, reads=[rsmall], writes=[rsmall])
                        P.act(small[:, 2:4], small[:, 2:4], AF.Sqrt, reads=[rsmall], writes=[rsmall])
                        P.op("vector", lambda e: e.reciprocal(out=small[:, 2:4], in_=small[:, 2:4]), [rsmall], [rsmall])
                        P.tt(ot[:].rearrange("p (h d) -> p h d", h=2), ot[:].rearrange("p (h d) -> p h d", h=2),
                             small[:, 2:4].unsqueeze(2).to_broadcast([128, 2, 64]), ALU.mult, reads=[rot, rsmall], writes=[rot])
                        P.tt(ot[:], ot[:], gwb[:], ALU.mult, reads=[rot, rgwb], writes=[rot])
                        P.tt(ycat[:, c, :], ot[:], sg[:, c, :], ALU.mult, reads=[rot, rsg], writes=[rycat])
                    P.mute = _m0 or ("s" in GSK)
                    bd, rbd = bank("b")
                    P.mm(bd[:, 0:128], kdtok[d_][:, c, :], v_b[:, c, :], True, True, reads=[rkdtok[d_], rvb], writes=[rbd])
                    for hh in range(2):
                        hp = slice(hh * 64, hh * 64 + 64)
                        P.stt(S[hp, :], S[hp, :], elast[hp, d_, c:c + 1], bd[hp, hh * 64:(hh + 1) * 64], ALU.mult, ALU.add,
                              reads=[rS, rel, rbd], writes=[rS])
                    if n_ % 2 == 1:
                        P.cp(sst[:, d_, c // 2, :], S[:], reads=[rS], writes=[rsst], eng="gpsimd")
                    P.mute = _m0
            if os.environ.get("GSTOP") == "d":
                P.mute = True
            for c in range(NT):
                bk, rb = bank("s")
                P.mm(bk[:, 0:128], ycat[:, c, :], ident[:], True, True, reads=[rycat, rid], writes=[rb])
                P.cp(catG[:, 0, c * 128:(c + 1) * 128], bk[:, 0:128], reads=[rb], writes=[rcatG], eng="scalar")
            P.dma(sf_d[l, :, 2 * p:2 * p + 2].rearrange("g h k v -> (h k) g v"), sst[:, 0, :, :], reads=[rsst], q="gpsimd")
            P.dma(sbo_d[l, :, 2 * p:2 * p + 2].rearrange("g h k v -> (h k) g v"), sst[:, 1, :, :], reads=[rsst], q="gpsimd")
            P.barrier()
            off_save = ar.off
            ar.off = a_off
            outproj(catG, rcatG, p * 128, 1, l)
            ar.off = off_save
            P.barrier()

        P.mute = "P" not in phases
        P.barrier()
        norm_to_hT(1)
        P.barrier()
        ar.reset()
        pq = ar.bf16([128, 8, T]); rpq = Res()
        kk = ar.bf16([128, 8, 128]); rkk = Res()
        Kc = ar.bf16([128, 8, 4, 128]); rKc = Res()
        tauS = ar.f32([128, NT, 8]); rtau = Res()
        negcS = ar.f32([128, NT, 8]); rnegc = Res()
        pk_off = ar.off
        alloc_w()
        kst = ar.f32([128, 8, 128]); rkst = Res()
        kstb = ar.bf16([128, 8, 128]); rkstb = Res()
        for hq in range(8):
            wb, rwb = wcols(wq_d[l, :, hq * 128:(hq + 1) * 128], 128)
            proj_fm(wb, rwb, 0, 128, lambda tg, bk, rb, hq=hq: P.cp(pq[:, hq, tg * 512:(tg + 1) * 512], bk[:, :],
                                                                   reads=[rb], writes=[rpq], eng="scalar"))
        P.dma(kst[:, :, 0:64], k1_d[l].rearrange("h n d -> n h d"), writes=[rkst])
        P.dma(kst[:, :, 64:128], k2_d[l].rearrange("h n d -> n h d"), writes=[rkst])
        P.cp(kstb[:], kst[:], reads=[rkst], writes=[rkstb])
        for h in range(8):
            bk, rb = bank("s")
            P.mm(bk[:, 0:128], kstb[:, h, :], ident[:], True, True, reads=[rkstb, rid], writes=[rb])
            P.cp(kk[:, h, :], bk[:, 0:128], reads=[rb], writes=[rkk], eng="scalar")
        P.cp(Kc[64:128], kk[64:128, :, :].unsqueeze(2).to_broadcast([64, 8, 4, 128]), reads=[rkk], writes=[rKc], eng="gpsimd")
        ssb = ar.f32([128, 16, 128]); rssb = Res()
        tmpa2 = [ar.f32([128, 128]), ar.f32([128, 128])]; rtmpa2 = [Res(), Res()]
        v16 = ar.f32([128, 16, 16]); rv16r = [Res() for _ in range(16)]
        cand = ar.f32([128, 8, 256]); rcand = Res()
        cta2 = [ar.f32([128, 256]), ar.f32([128, 256])]; rcta2 = [Res(), Res()]
        c16 = ar.f32([128, 8, 16]); rc16r = [Res() for _ in range(8)]
        e16 = ar.f32([128, 8, 16]); re16 = Res()
        zz = ar.f32([128, 8]); rzz = Res()
        for i in range(NT):
            tcs = slice(i * 128, (i + 1) * 128)
            for hg in range(2):
                b0, rb0 = bank("s")
                b1, rb1 = bank("s")
                for h4 in range(4):
                    h = hg * 4 + h4
                    P.mm(b0[:, h4 * 128:(h4 + 1) * 128], pq[0:64, h, tcs], kk[0:64, h, :], True, True, reads=[rpq, rkk], writes=[rb0])
                for h4 in range(4):
                    h = hg * 4 + h4
                    P.mm(b1[:, h4 * 128:(h4 + 1) * 128], pq[64:128, h, tcs], kk[64:128, h, :], True, True, reads=[rpq, rkk], writes=[rb1])
                sv = ssb[:, hg * 8:hg * 8 + 8, :].rearrange("p (h s) k -> p h s k", s=2)
                P.cp(sv[:, :, 0, :], b0[:, :].rearrange("p (a b) -> p a b", a=4), reads=[rb0], writes=[rssb], eng="scalar")
                P.cp(sv[:, :, 1, :], b1[:, :].rearrange("p (a b) -> p a b", a=4), reads=[rb1], writes=[rssb], eng="scalar")
            for r0_ in range(0, 16, 2):
                rows = (r0_, r0_ + 1)
                for j, r in enumerate(rows):
                    P.op("vector", lambda e, r=r: e.max(out=v16[:, r, 0:8], in_=ssb[:, r, :]), [rssb], [rv16r[r]])
                for j, r in enumerate(rows):
                    P.op("vector", lambda e, r=r, j=j: e.match_replace(out=tmpa2[j][:], in_to_replace=v16[:, r, 0:8], in_values=ssb[:, r, :], imm_value=-1e30),
                         [rssb, rv16r[r]], [rtmpa2[j]])
                for j, r in enumerate(rows):
                    P.op("vector", lambda e, r=r, j=j: e.max(out=v16[:, r, 8:16], in_=tmpa2[j][:]), [rtmpa2[j]], [rv16r[r]])
            v4 = v16[:, :, :].rearrange("p (h s) k -> p h s k", s=2)
            P.tt(cand[:].rearrange("p h (a b) -> p h a b", a=16),
                 v4[:, :, 0, :].unsqueeze(3).to_broadcast([128, 8, 16, 16]),
                 v4[:, :, 1, :].unsqueeze(2).to_broadcast([128, 8, 16, 16]), ALU.add, reads=rv16r, writes=[rcand], eng="gpsimd")
            for h0_ in range(0, 8, 2):
                hs_ = (h0_, h0_ + 1)
                for j, h in enumerate(hs_):
                    P.op("vector", lambda e, h=h: e.max(out=c16[:, h, 0:8], in_=cand[:, h, :]), [rcand], [rc16r[h]])
                for j, h in enumerate(hs_):
                    P.op("vector", lambda e, h=h, j=j: e.match_replace(out=cta2[j][:], in_to_replace=c16[:, h, 0:8], in_values=cand[:, h, :], imm_value=-1e30),
                         [rcand, rc16r[h]], [rcta2[j]])
                for j, h in enumerate(hs_):
                    P.op("vector", lambda e, h=h, j=j: e.max(out=c16[:, h, 8:16], in_=cta2[j][:]), [rcta2[j]], [rc16r[h]])
            P.ts(tauS[:, i, :], c16[:, :, 15], -2e-5, None, ALU.add, reads=rc16r, writes=[rtau])
            P.tt(e16[:], c16[:, :, :], c16[:, :, 0:1].to_broadcast([128, 8, 16]), ALU.subtract, reads=rc16r, writes=[re16])
            P.act(e16[:], e16[:], AF.Exp, reads=[re16], writes=[re16])
            P.op("vector", lambda e: e.tensor_reduce(out=zz[:], in_=e16[:], axis=AX.X, op=ALU.add), [re16], [rzz])
            P.act(zz[:], zz[:], AF.Ln, reads=[rzz], writes=[rzz])
            P.tt(zz[:], zz[:], c16[:, :, 0], ALU.add, reads=[rzz] + rc16r, writes=[rzz])
            P.ts(negcS[:, i, :], zz[:], -1.0, None, ALU.mult, reads=[rzz], writes=[rnegc])
        P.tt(tauS[:], tauS[:], negcS[:], ALU.add, reads=[rtau, rnegc], writes=[rtau])
        P.act(tauS[:], tauS[:], AF.Exp, reads=[rtau], writes=[rtau])
        P.barrier()
        ar.off = pk_off
        ust = ar.f32([128, D]); rust = [Res(), Res()]
        vst = ar.f32([128, D]); rvst = [Res(), Res()]
        ub = ar.bf16([128, 2, D]); rub = [Res(), Res()]
        UT = ar.bf16([128, 8, 512]); rUT = Res()
        Vb = ar.bf16([128, 4, D]); rVb = Res()
        gel = [ar.bf16([128, 4, 512]), ar.bf16([128, 4, 512])]; rgel = [Res(), Res()]
        Ebf = [ar.f32([128, 512]) for _ in range(3)]; rEbf = [Res() for _ in range(3)]
        _Gm = ar.bf16([128, 8, 512]); _rGm = [Res() for _ in range(8)]; Gm = [_Gm, _Gm]; rGm = [_rGm, _rGm]
        _AT = ar.bf16([128, 4, 128]); _rAT = Res(); AT = [_AT, _AT]; rAT = [_rAT, _rAT]
        _Gsb = ar.bf16([128, 512]); _rGsb = Res(); Gsb = [_Gsb, _Gsb]; rGsb = [_rGsb, _rGsb]
        ecnt = 0
        ucnt = 0
        tl = {}

        def gate_head(i, gs, h):
            nonlocal ecnt
            tcs = slice(i * 128, (i + 1) * 128)
            bk, rb = bank("s")
            P.mm(bk[:, :], pq[:, h, tcs], Kc[:, h, :, :].rearrange("p a b -> p (a b)"), True, True,
                 reads=[rpq, rKc], writes=[rb])
            ek = ecnt % 3
            ecnt += 1
            P.act(Ebf[ek][:], bk[:, :], AF.Exp, reads=[rb, rnegc], writes=[rEbf[ek]], bias=negcS[:, i, h:h + 1], scale=1.0)
            P.stt(Gm[gs][:, h, :], Ebf[ek][:], tauS[:, i, h:h + 1], Ebf[ek][:], ALU.is_ge, ALU.mult,
                  reads=[rtau, rEbf[ek]], writes=[rGm[gs][h]])

        def tail_a(i, gs, gl, t4):
            bsum, rbsum = bank("a")
            for h in range(8):
                P.mm(bsum[:, :], ident[:], Gm[gs][:, h, :], h == 0, h == 7, reads=[rGm[gs][h], rid], writes=[rbsum])
            P.cp(Gsb[gs][:], bsum[:, :], reads=[rbsum], writes=[rGsb[gs]], eng="scalar")

        def tail_b(i, gs, gl, t4):
            bg, rbg = bank("a")
            for k4 in range(4):
                P.mm(bg[:, k4 * 128:(k4 + 1) * 128], Gsb[gs][:, k4 * 128:(k4 + 1) * 128], ident[:], True, True,
                     reads=[rGsb[gs], rid], writes=[rbg])
            P.tt(AT[gs][:], gel[gl][:, :, t4 * 128:(t4 + 1) * 128], bg[:, :].rearrange("p (a b) -> p a b", a=4), ALU.mult,
                 reads=[rgel[gl], rbg], writes=[rAT[gs]])

        def tail_c(i, gs, gl, t4):
            for hf in range(2):
                bo, rbo = bank("b")
                for k4 in range(4):
                    P.mm(bo[:, :], AT[gs][:, k4, :], Vb[:, k4, hf * 512:(hf + 1) * 512], k4 == 0, k4 == 3,
                         reads=[rAT[gs], rVb], writes=[rbo])
                P.tt(x[:, i, hf * 512:(hf + 1) * 512], x[:, i, hf * 512:(hf + 1) * 512], bo[:, :], ALU.add,
                     reads=[rx[i], rbo], writes=[rx[i]])

        for quad in range(32):
            for k4 in range(4):
                r0 = (quad * 4 + k4) * 128
                kb = k4 % 2
                for hf in range(2):
                    hs = slice(hf * 512, (hf + 1) * 512)
                    P.dma(ust[:, hs], pu_d[l, r0:r0 + 128, hs], writes=[rust[hf]])
                    P.dma(vst[:, hs], pv_d[l, r0:r0 + 128, hs], writes=[rvst[hf]])
                for hf in range(2):
                    hs = slice(hf * 512, (hf + 1) * 512)
                    P.act(ub[:, kb, hs], ust[:, hs], AF.Copy, reads=[rust[hf]], writes=[rub[kb]])
                    P.tt(Vb[:, k4, hs], vst[:, hs], gb[:, 1, hs], ALU.mult, reads=[rvst[hf], rgb], writes=[rVb], eng="vector")
                for c2 in range(2):
                    bk, rb = bank("s")
                    for c4 in range(4):
                        c = c2 * 4 + c4
                        P.mm(bk[:, c4 * 128:(c4 + 1) * 128], ub[:, kb, c * 128:(c + 1) * 128], ident[:], True, True,
                             reads=[rub[kb], rid], writes=[rb])
                    P.cp(UT[:, c2 * 4:(c2 + 1) * 4, k4 * 128:(k4 + 1) * 128], bk[:, :].rearrange("p (a b) -> p a b", a=4),
                         reads=[rb], writes=[rUT], eng="vector")
            for h2 in range(2):
                P.cp(Kc[0:64, h2 * 4:(h2 + 1) * 4], kk[0:64, h2 * 4:(h2 + 1) * 4, quad * 4:quad * 4 + 4].unsqueeze(3).to_broadcast([64, 4, 4, 128]),
                     reads=[rkk], writes=[rKc], eng=("vector" if h2 == 0 else "scalar"))
            prev = None
            for tg in range(4):
                gl = tg % 2
                for k4 in range(4):
                    bk, rb = bank("s")
                    for c in range(8):
                        P.mm(bk[:, :], UT[:, c, k4 * 128:(k4 + 1) * 128], hT[:, c, tg * 512:(tg + 1) * 512], c == 0, c == 7,
                             reads=[rUT, rhT], writes=[rb])
                    P.act(gel[gl][:, k4, :], bk[:, :], AF.Gelu, reads=[rb], writes=[rgel[gl]])
                for t4 in range(4):
                    i = tg * 4 + t4
                    gs = ucnt % 2
                    ucnt += 1
                    if prev is not None:
                        tail_a(*prev)
                    for h in range(8):
                        gate_head(i, gs, h)
                        if prev is not None:
                            if h == 3:
                                tail_b(*prev)
                            elif h == 7:
                                tail_c(*prev)
                    prev = (i, gs, gl, t4)
            tail_a(*prev)
            tail_b(*prev)
            tail_c(*prev)

    P.mute = False
    P.barrier()
    ar.reset()
    fwb = ar.f32([128, D]); rfwb = Res()
    sq = ar.f32([128, D]); rsq = Res()
    yo = [ar.f32([128, D]), ar.f32([128, D])]; ryo = [Res(), Res()]
    P.dma(fwb[:], fnw_d.partition_broadcast(128), writes=[rfwb])
    for i in range(NT):
        if raw:
            P.cp(yo[i % 2][:], x[:, i, :], reads=[rx[i]], writes=[ryo[i % 2]])
        else:
            rstd_of(x[:, i, :], D, small[:, 0:1], [rx[i]], sq, rsq)
            P.stt(yo[i % 2][:], x[:, i, :], small[:, 0:1], fwb[:], ALU.mult, ALU.mult, reads=[rx[i], rsmall, rfwb], writes=[ryo[i % 2]])
        P.dma(y_d[i * 128:(i + 1) * 128, :], yo[i % 2][:], reads=[ryo[i % 2]], q="gpsimd")
    P.emit()
    P.es.close()
    return nc, P


def _na_tables():
    a = np.arange(128)
    types = {}
    for i in range(NT):
        for kt, ty in na_blocks(i):
            types.setdefault(ty, (i, kt))
    DR = np.zeros((26, 128, 128), np.int64); DC = np.zeros((26, 128, 128), np.int64)
    msk_s = np.zeros((NTYPE, 128, 128), np.float32); msk_p = np.zeros((NTYPE, 128, 128), np.float32)
    for ty, (i, kt) in types.items():
        krow = (2 * kt + a // 64)[:, None]; kcol = (a % 64)[:, None]
        qrow = (2 * i + a // 64)[None, :]; qcol = (a % 64)[None, :]
        rs = np.clip(qrow - 4, 0, 24)
        cs = np.clip(qcol - 8, 0, 48)
        valid = (krow >= rs) & (krow < rs + 8) & (kcol >= cs) & (kcol < cs + 16)
        DR[ty] = np.clip(krow - qrow, -7, 7) + 7
        DC[ty] = np.clip(kcol - qcol, -15, 15) + 15
        msk_s[ty] = np.where(valid, 0.0, -1e30)
        msk_p[ty] = 0.0 if (kt // 2 == i // 2) else -1e30
    msk_s[26] = 0.0
    msk_p[26] = -1e30
    return DR, DC, msk_s, msk_p


_CACHE = {}


def kernel(x_prompt, x_sample, cache_na_k, cache_na_v, state_gla_fwd, state_gla_bwd, c, c_ctx,
           w_ada, b_ada, norm1_w, norm2_w, w_in, w_af, b_af, w_ab, b_ab, gla_norm_w,
           conv_w, conv_b, na_rpb, w_out, peer_wq, peer_k1, peer_k2, peer_u, peer_v, final_norm_w):
    f = lambda a: np.ascontiguousarray(np.asarray(a), dtype=np.float32)
    x_prompt, x_sample = f(x_prompt), f(x_sample)
    DR, DC, msk_s, msk_p = _na_tables()
    rpb = f(na_rpb)
    nab_s = np.ascontiguousarray(rpb[:, :, DR, DC])
    nab_p = np.zeros_like(nab_s)
    a = np.arange(128)
    glam = np.stack([(a[:, None] <= a[None, :]), (a[:, None] >= a[None, :])]).astype(np.float32)
    shared = {
        "ident": np.eye(128).astype(NPBF), "glamask": glam,
        "w_ada": f(w_ada), "b_ada": f(b_ada), "norm1_w": f(norm1_w), "norm2_w": f(norm2_w), "w_in": f(w_in),
        "w_af": f(w_af), "b_af": f(b_af), "w_ab": f(w_ab), "b_ab": f(b_ab), "gla_norm_w": f(gla_norm_w),
        "conv_w": f(conv_w), "conv_b": f(conv_b), "w_out": f(w_out), "peer_wq": f(peer_wq),
        "peer_k1": f(peer_k1), "peer_k2": f(peer_k2), "peer_u": f(peer_u), "peer_v": f(peer_v),
        "final_norm_w": f(final_norm_w),
    }
    zk = np.zeros((DEPTH, 6, 256, 64), np.float32)
    zs = np.zeros((DEPTH, 6, 64, 64), np.float32)
    in_maps = []
    for core in range(8):
        m = dict(shared)
        if core < 4:
            b = core
            m.update(x=x_sample[b], cvec=f(c)[b], kctx=f(cache_na_k)[b], vctx=f(cache_na_v)[b],
                     s0f=f(state_gla_fwd)[b], s0b=f(state_gla_bwd)[b], carry=np.ones((128, 1), np.float32),
                     nabias=nab_s, namask=msk_s)
        else:
            b0 = (core - 4) * 4
            xp = x_prompt[b0:b0 + 4].reshape(1024, D)
            m.update(x=np.ascontiguousarray(np.concatenate([xp, xp], 0)), cvec=f(c_ctx), kctx=zk, vctx=zk,
                     s0f=zs, s0b=zs, carry=np.zeros((128, 1), np.float32), nabias=nab_p, namask=msk_p)
        in_maps.append(m)
    if _CACHE.get("only_maps"):
        return in_maps
    if "nc" not in _CACHE:
        _CACHE["nc"] = build()
    nc, P = _CACHE["nc"]
    res = run_bass_kernel_spmd(nc, in_maps, core_ids=list(range(8)))
    R = res.results
    y_sample = np.stack([R[b]["y"] for b in range(4)], 0)
    y_prompt = np.concatenate([R[4 + j]["y"][:1024].reshape(4, 256, D) for j in range(4)], 0)

    def kvout(name):
        o = np.concatenate([R[4 + j][name][:, :1024].reshape(DEPTH, 4, 256, 6, 64) for j in range(4)], 1)
        return np.ascontiguousarray(o.transpose(1, 0, 3, 2, 4))

    def stout(name):
        o = np.concatenate([R[4 + j][name][:, :4] for j in range(4)], 1)
        return np.ascontiguousarray(o.transpose(1, 0, 2, 3, 4))
    return (y_prompt.astype(np.float32), y_sample.astype(np.float32), kvout("nk"), kvout("nv"), stout("sf"), stout("sbo"))
```

```python
import os
import numpy as np
from contextlib import ExitStack
import ml_dtypes
import concourse.bass as bass
import concourse.mybir as mybir
from concourse.bass_utils import run_bass_kernel_spmd


F32 = mybir.dt.float32
BF16 = mybir.dt.bfloat16
AF = mybir.ActivationFunctionType
ALU = mybir.AluOpType
AX = mybir.AxisListType
NPBF = ml_dtypes.bfloat16


class Res:
    __slots__ = ("name", "w", "rd")

    def __init__(self, name="r"):
        self.name = name
        self.w = None
        self.rd = []


class Op:
    __slots__ = ("eng", "fn", "dma", "waits", "signal", "sem", "val", "prevsame")

    def __init__(self, eng, fn, dma):
        self.eng = eng
        self.fn = fn
        self.dma = dma
        self.waits = []
        self.signal = False
        self.sem = None
        self.val = None
        self.prevsame = None


ENGINES = ("tensor", "vector", "scalar", "gpsimd", "sync")
NDMASEM = 16
SEMCH = 30000


class Prog:
    def __init__(self, nc):
        self.nc = nc
        self.streams = {e: [] for e in ENGINES}
        self.es = ExitStack()
        self.nops = 0
        self.pending = {e: [] for e in ENGINES}
        self.out_dma = []
        self.mute = False

    def sb(self, name, shape, dt):
        return self.es.enter_context(self.nc.sbuf_tensor("sb_" + name, list(shape), dt))

    def ps(self, name, shape, dt=F32):
        return self.es.enter_context(self.nc.psum_tensor(name, list(shape), dt))

    def op(self, eng, fn, reads=(), writes=(), dma=False):
        o = Op(eng, fn, dma)
        if self.mute:
            return o
        deps = []
        for r in reads:
            if r.w is not None:
                deps.append(r.w)
        for w in writes:
            if w.w is not None:
                deps.append(w.w)
            deps.extend(w.rd)
        if self.pending[eng]:
            deps.extend(self.pending[eng])
            self.pending[eng] = []
        seen = set()
        for d in deps:
            if d is o or id(d) in seen:
                continue
            seen.add(id(d))
            if d.eng == "tensor" and eng == "tensor" and not d.dma and not dma:
                continue
            d.signal = True
            o.waits.append(d)
        for r in reads:
            r.rd.append(o)
        for w in writes:
            w.w = o
            w.rd = []
        self.streams[eng].append(o)
        self.nops += 1
        if dma:
            self.out_dma.append(o)
        return o

    def barrier(self):
        lasts = []
        for e in ENGINES:
            st = self.streams[e]
            last_c = None
            for o in reversed(st):
                if not o.dma:
                    last_c = o
                    break
            if last_c is not None:
                lasts.append(last_c)
        lasts.extend(self.out_dma)
        self.out_dma = []
        for e in ENGINES:
            self.pending[e] = list(lasts)

    def dma(self, out, in_, reads=(), writes=(), q="sync", **kw):
        return self.op(q, lambda e: e.dma_start(out=out, in_=in_, **kw), reads, writes, dma=True)

    def mm(self, out, lhsT, rhs, start, stop, reads=(), writes=()):
        return self.op("tensor", lambda e: e.matmul(out, lhsT, rhs, start=start, stop=stop), reads, writes)

    def act(self, out, in_, func, reads=(), writes=(), eng="scalar", **kw):
        return self.op(eng, lambda e: e.activation(out=out, in_=in_, func=func, **kw), reads, writes)

    def tt(self, out, in0, in1, op, reads=(), writes=(), eng="vector"):
        return self.op(eng, lambda e: e.tensor_tensor(out=out, in0=in0, in1=in1, op=op), reads, writes)

    def ts(self, out, in0, s1, s2, op0, op1=None, reads=(), writes=(), eng="vector"):
        if op1 is None:
            return self.op(eng, lambda e: e.tensor_scalar(out=out, in0=in0, scalar1=s1, scalar2=None, op0=op0), reads, writes)
        return self.op(eng, lambda e: e.tensor_scalar(out=out, in0=in0, scalar1=s1, scalar2=s2, op0=op0, op1=op1), reads, writes)

    def stt(self, out, in0, scalar, in1, op0, op1, reads=(), writes=(), eng="vector"):
        return self.op(eng, lambda e: e.scalar_tensor_tensor(out=out, in0=in0, scalar=scalar, in1=in1, op0=op0, op1=op1), reads, writes)

    def cp(self, out, in_, reads=(), writes=(), eng="vector"):
        if eng == "scalar":
            return self.op(eng, lambda e: e.copy(out=out, in_=in_), reads, writes)
        return self.op(eng, lambda e: e.tensor_copy(out=out, in_=in_), reads, writes)

    def memset(self, ap, val, writes=(), eng="vector"):
        return self.op(eng, lambda e: e.memset(ap, val), (), writes)

    def emit(self):
        nc = self.nc
        es = self.es
        csems = {}
        dsems = {}
        for e in ENGINES:
            st = self.streams[e]
            nsig = sum(1 for o in st if (o.signal and not o.dma))
            csems[e] = [es.enter_context(nc.semaphore(f"c_{e}_{i}")) for i in range(nsig // SEMCH + 1)]
            ndma = sum(1 for o in st if o.dma)
            dsems[e] = [es.enter_context(nc.semaphore(f"d_{e}_{i}")) for i in range(min(NDMASEM, ndma))]
            cnt = 0
            dcnt = 0
            hist = []
            for o in st:
                if o.dma:
                    k = dcnt
                    o.sem = dsems[e][k % NDMASEM]
                    o.val = 16 * (k // NDMASEM + 1)
                    if k >= NDMASEM:
                        o.prevsame = hist[k - NDMASEM]
                    hist.append(o)
                    dcnt += 1
                elif o.signal:
                    o.sem = csems[e][cnt // SEMCH]
                    o.val = cnt % SEMCH + 1
                    cnt += 1
        finals = []
        for e in ENGINES:
            last = {}
            for o in self.streams[e]:
                if o.dma:
                    last[id(o.sem)] = (o.sem, o.val)
            finals.extend(last.values())
        streams = self.streams
        block = es.enter_context(nc.Block())

        def make(e):
            def run(engh):
                waited = {}
                for o in streams[e]:
                    need = {}
                    for d in o.waits:
                        k = id(d.sem)
                        if k not in need or need[k][1] < d.val:
                            need[k] = (d.sem, d.val)
                    if o.prevsame is not None:
                        k = id(o.sem)
                        v = o.prevsame.val
                        if k not in need or need[k][1] < v:
                            need[k] = (o.sem, v)
                    for k, (sem, val) in need.items():
                        if waited.get(k, 0) < val:
                            engh.wait_ge(sem, val)
                            waited[k] = val
                    ins = o.fn(engh)
                    if o.dma:
                        ins.then_inc(o.sem, 16)
                    elif o.signal:
                        ins.then_inc(o.sem, 1)
                if e == "sync":
                    for sem, val in finals:
                        engh.wait_ge(sem, val)
            return run

        for e in ENGINES:
            getattr(block, e)(make(e))


D = 1024
NT = 16
T = 2048
DEPTH = 4
INW = 3488
QA0, KA0, VA0, GA0, LRF0, LRB0 = 0, 384, 768, 1152, 1536, 1552
CH0, CB0, CC0 = 1568, 1824, 2080
QN0, KN0, VN0 = 2336, 2720, 3104
NTYPE = 27
ARENA_W = 25856


def na_blocks(i):
    if 2 <= i <= 13:
        return [(i - 2 + m, (i % 2) * 5 + m) for m in range(5)]
    e = {0: 0, 1: 1, 14: 2, 15: 3}[i]
    base = 0 if i < 2 else 12
    return [(base + m, 10 + e * 4 + m) for m in range(4)]


class Arena:
    def __init__(self, P):
        self.t = P.sb("arena", [128, ARENA_W], F32)
        self.off = 0

    def reset(self):
        self.off = 0

    def f32(self, shape):
        n = int(np.prod(shape[1:]))
        v = self.t[:, self.off:self.off + n]
        self.off += n
        assert self.off <= ARENA_W, self.off
        return self._shape(v, shape)

    def bf16(self, shape):
        n = int(np.prod(shape[1:]))
        assert n % 2 == 0
        v = self.t[:, self.off:self.off + n // 2].bitcast(BF16)
        self.off += n // 2
        assert self.off <= ARENA_W, self.off
        return self._shape(v, shape)

    @staticmethod
    def _shape(v, shape):
        if len(shape) == 2:
            return v
        if len(shape) == 3:
            return v.rearrange("p (a b) -> p a b", a=shape[1], b=shape[2])
        if len(shape) == 4:
            return v.rearrange("p (a b c) -> p a b c", a=shape[1], b=shape[2], c=shape[3])
        raise ValueError(shape)


def build(nlayers=DEPTH, phases="CNGP", raw=False):
    nc = bass.Bass("TRN2", target_bir_lowering=False)

    def din(name, shape, dt=F32):
        return nc.dram_tensor(name, list(shape), dt, kind="ExternalInput").ap()

    def dout(name, shape):
        return nc.dram_tensor(name, list(shape), F32, kind="ExternalOutput").ap()

    x_d = din("x", [T, D]); cvec_d = din("cvec", [D])
    kctx_d = din("kctx", [DEPTH, 6, 256, 64]); vctx_d = din("vctx", [DEPTH, 6, 256, 64])
    s0f_d = din("s0f", [DEPTH, 6, 64, 64]); s0b_d = din("s0b", [DEPTH, 6, 64, 64])
    carry_d = din("carry", [128, 1])
    nabias_d = din("nabias", [DEPTH, 6, 26, 128, 128]); namask_d = din("namask", [NTYPE, 128, 128])
    ident_d = din("ident", [128, 128], BF16)
    glam_d = din("glamask", [2, 128, 128])
    w_ada_d = din("w_ada", [DEPTH, D, 6 * D]); b_ada_d = din("b_ada", [DEPTH, 6 * D])
    n1_d = din("norm1_w", [DEPTH, D]); n2_d = din("norm2_w", [DEPTH, D])
    w_in_d = din("w_in", [DEPTH, D, INW])
    waf_d = din("w_af", [DEPTH, 16, 384]); baf_d = din("b_af", [DEPTH, 384])
    wab_d = din("w_ab", [DEPTH, 16, 384]); bab_d = din("b_ab", [DEPTH, 384])
    gnw_d = din("gla_norm_w", [DEPTH, 384])
    cw_d = din("conv_w", [DEPTH, 3, 256]); cb_d = din("conv_b", [DEPTH, 256])
    wout_d = din("w_out", [DEPTH, D, D]); wq_d = din("peer_wq", [DEPTH, D, D])
    k1_d = din("peer_k1", [DEPTH, 8, 128, 64]); k2_d = din("peer_k2", [DEPTH, 8, 128, 64])
    pu_d = din("peer_u", [DEPTH, 16384, D]); pv_d = din("peer_v", [DEPTH, 16384, D])
    fnw_d = din("final_norm_w", [D])
    y_d = dout("y", [T, D]); nk_d = dout("nk", [DEPTH, T, 384]); nv_d = dout("nv", [DEPTH, T, 384])
    sf_d = dout("sf", [DEPTH, 8, 6, 64, 64]); sbo_d = dout("sbo", [DEPTH, 8, 6, 64, 64])

    P = Prog(nc)
    NSC = dict(allow_slow_non_contiguous=True)
    x = P.sb("x", [128, NT, D], F32); rx = [Res() for _ in range(NT)]
    hT = P.sb("hT", [128, 8, T], BF16); rhT = Res()
    ident = P.sb("ident", [128, 128], BF16); rid = Res()
    identf = P.sb("identf", [128, 128], F32)
    onesf = P.sb("onesf", [128, 128], F32)
    glam = P.sb("glam", [128, 2, 128], F32)
    carry = P.sb("carry", [128, 1], F32)
    carm1 = P.sb("carm1", [128, 1], F32)
    scol = P.sb("scol", [128, 8], F32)
    modcol = P.sb("modcol", [128, 4, 8], F32); rmod = Res()
    acol = P.sb("acol", [128, 2, 8], F32)
    nwcol = P.sb("nwcol", [128, 2, 8], F32)
    gb = P.sb("gb", [128, 2, D], F32); rgb = Res()
    small = P.sb("small", [128, 64], F32); rsmall = Res()
    rconst = Res()
    ar = Arena(P)
    banks = [P.ps(f"bank{i}", [128, 512], F32) for i in range(8)]
    rbank = [Res() for _ in range(8)]
    rr = {"s": 0, "a": 0, "b": 0}

    def bank(pool):
        if pool == "s":
            i = rr["s"] % 3
        elif pool == "a":
            i = 3 + rr["a"] % 3
        else:
            i = 6 + rr["b"] % 2
        rr[pool] += 1
        return banks[i], rbank[i]

    for i in range(NT):
        P.dma(x[:, i, :], x_d[i * 128:(i + 1) * 128, :], writes=[rx[i]])
    P.dma(ident[:], ident_d[:, :], writes=[rid])
    P.dma(glam[:], glam_d.rearrange("a s t -> s a t"), writes=[rconst])
    P.dma(carry[:], carry_d[:, :], writes=[rconst])
    P.dma(scol[:], cvec_d.rearrange("(k p) -> p k", p=128), writes=[rconst], **NSC)
    P.cp(identf[:], ident[:], reads=[rid], writes=[rconst])
    P.memset(onesf[:], 1.0, writes=[rconst])
    P.ts(carm1[:], carry[:], -1.0, None, ALU.add, reads=[rconst], writes=[rconst])
    P.act(scol[:], scol[:], AF.Silu, reads=[rconst], writes=[rconst])

    def rstd_of(src_ap, n, out_ap, rsrc, scratch, rscratch):
        P.act(scratch, src_ap, AF.Square, reads=rsrc, writes=[rscratch, rsmall], accum_out=out_ap)
        P.ts(out_ap, out_ap, 1.0 / n, 1e-6, ALU.mult, ALU.add, reads=[rsmall], writes=[rsmall])
        P.act(out_ap, out_ap, AF.Sqrt, reads=[rsmall], writes=[rsmall])
        P.op("vector", lambda e: e.reciprocal(out=out_ap, in_=out_ap), [rsmall], [rsmall])

    def norm_to_hT(which):
        ar.reset()
        sq = ar.f32([128, D]); rsq = Res()
        xn = ar.bf16([128, 2, D]); rxn = [Res(), Res()]
        rhTc = [Res() for _ in range(8)]
        for i in range(NT):
            rstd_of(x[:, i, :], D, small[:, 0:1], [rx[i]], sq, rsq)
            P.ts(xn[:, i % 2, :], x[:, i, :], small[:, 0:1], None, ALU.mult, reads=[rx[i], rsmall], writes=[rxn[i % 2]])
            for hf in range(2):
                bk, rb = bank("s")
                for c4 in range(4):
                    c = hf * 4 + c4
                    P.mm(bk[:, c4 * 128:(c4 + 1) * 128], xn[:, i % 2, c * 128:(c + 1) * 128], ident[:], True, True,
                         reads=[rxn[i % 2], rid], writes=[rb])
                for c4 in range(4):
                    c = hf * 4 + c4
                    P.ts(hT[:, c, i * 128:(i + 1) * 128], bk[:, c4 * 128:(c4 + 1) * 128], acol[:, which, c:c + 1],
                         modcol[:, 2 * which, c:c + 1], ALU.mult, ALU.add, reads=[rb, rmod], writes=[rhTc[c]],
                         eng=("vector" if c4 % 2 == 0 else "gpsimd") if False else "vector")

    wst = [None, None]; rwst = [Res(), Res()]; wbf = [None, None]; rwbf = [Res(), Res()]
    wcnt = [0]

    def alloc_w():
        for k in range(2):
            wst[k] = ar.f32([128, 8, 128])
            wbf[k] = ar.bf16([128, 8, 128])

    def wcols(dram_ap, n, dst=None, rdst=None, dcol=0):
        k = wcnt[0] % 2
        wcnt[0] += 1
        P.dma(wst[k][:, :, 0:n], dram_ap.rearrange("(c p) n -> p c n", p=128), writes=[rwst[k]])
        if dst is None:
            dst, rdst, dcol = wbf[k], rwbf[k], 0
        P.cp(dst[:, :, dcol:dcol + n], wst[k][:, :, 0:n], reads=[rwst[k]], writes=[rdst], eng="gpsimd")
        return dst, rdst

    def proj_fm(wb, rwb, wcol0, m, evac):
        for tg in range(4):
            bk, rb = bank("s")
            for c in range(8):
                P.mm(bk[0:m, :], wb[:, c, wcol0:wcol0 + m], hT[:, c, tg * 512:(tg + 1) * 512], c == 0, c == 7,
                     reads=[rwb, rhT], writes=[rb])
            evac(tg, bk, rb)

    def outproj(cat, rcat, row0, nch, l):
        wo_st = ar.f32([128, D]); rwo_st = Res()
        wob = ar.bf16([128, nch, D]); rwob = Res()
        for j in range(nch):
            P.dma(wo_st[:], wout_d[l, row0 + j * 128: row0 + (j + 1) * 128, :], writes=[rwo_st])
            P.tt(wob[:, j, :], wo_st[:], gb[:, 0, :], ALU.mult, reads=[rwo_st, rgb], writes=[rwob], eng="gpsimd")
        for i in range(NT):
            for hf in range(2):
                bk, rb = bank("b")
                for j in range(nch):
                    P.mm(bk[:, :], cat[:, j, i * 128:(i + 1) * 128], wob[:, j, hf * 512:(hf + 1) * 512], j == 0, j == nch - 1,
                         reads=[rcat, rwob], writes=[rb])
                P.tt(x[:, i, hf * 512:(hf + 1) * 512], x[:, i, hf * 512:(hf + 1) * 512], bk[:, :], ALU.add,
                     reads=[rx[i], rb], writes=[rx[i]])

    for l in range(nlayers):
        P.barrier()
        ar.reset()
        wa = [ar.f32([128, 8, 512]), ar.f32([128, 8, 512])]; rwa = [Res(), Res()]
        row = ar.f32([1, 512]); rrow = Res()
        brow = ar.f32([1, 512]); rbrow = Res()
        P.dma(nwcol[:, 0, :], n1_d[l].rearrange("(k p) -> p k", p=128), writes=[rmod], **NSC)
        P.dma(nwcol[:, 1, :], n2_d[l].rearrange("(k p) -> p k", p=128), writes=[rmod], **NSC)
        for g in range(12):
            k = g % 2
            P.dma(wa[k][:], w_ada_d[l, :, g * 512:(g + 1) * 512].rearrange("(c p) n -> p c n", p=128), writes=[rwa[k]])
            P.dma(brow[0:1, :], b_ada_d[l:l + 1, g * 512:(g + 1) * 512], writes=[rbrow])
            bk, rb = bank("s")
            for c in range(8):
                P.mm(bk[0:1, :], scol[:, c:c + 1], wa[k][:, c, :], c == 0, c == 7, reads=[rconst, rwa[k]], writes=[rb])
            P.tt(row[0:1, :], bk[0:1, :], brow[0:1, :], ALU.add, reads=[rb, rbrow], writes=[rrow])
            vec = g // 2
            if vec in (0, 1, 3, 4):
                mi = {0: 0, 1: 1, 3: 2, 4: 3}[vec]
                bk2, rb2 = bank("s")
                for j in range(4):
                    P.mm(bk2[:, j:j + 1], row[0:1, j * 128:(j + 1) * 128], onesf[0:1, 0:1], True, True,
                         reads=[rrow, rconst], writes=[rb2])
                P.cp(modcol[:, mi, k * 4:(k + 1) * 4], bk2[:, 0:4], reads=[rb2], writes=[rmod])
            else:
                gi = 0 if vec == 2 else 1
                bk2, rb2 = bank("s")
                P.mm(bk2[:, :], onesf[0:1, 0:128], row[0:1, :], True, True, reads=[rrow, rconst], writes=[rb2])
                P.cp(gb[:, gi, k * 512:(k + 1) * 512], bk2[:, :], reads=[rb2], writes=[rgb])
        for w_ in range(2):
            P.stt(acol[:, w_, :], modcol[:, 2 * w_ + 1, :], 1.0, nwcol[:, w_, :], ALU.add, ALU.mult,
                  reads=[rmod], writes=[rmod])
        P.barrier()
        norm_to_hT(0)

        P.mute = "C" not in phases
        P.barrier()
        ar.reset()
        alloc_w()
        zc = ar.f32([128, 3, T]); rzc = Res()
        u = ar.f32([128, T]); ru = Res()
        yv = ar.f32([128, T]); ryv = Res()
        catC = ar.bf16([128, 2, T]); rcatC = Res()
        cwc = ar.f32([128, 2, 4]); rcwc = Res()
        wn = ar.f32([128, 2, 2]); rwn = Res()
        for j in range(2):
            for k_ in range(3):
                P.dma(cwc[:, j, k_:k_ + 1], cw_d[l, k_, j * 128:(j + 1) * 128].rearrange("(p o) -> p o", o=1), writes=[rcwc], **NSC)
            P.dma(cwc[:, j, 3:4], cb_d[l, j * 128:(j + 1) * 128].rearrange("(p o) -> p o", o=1), writes=[rcwc], **NSC)
        for j in range(2):
            P.ts(wn[:, j, 0:1], cwc[:, j, 0:1], carm1[:, 0:1], None, ALU.mult, reads=[rcwc, rconst], writes=[rwn])
            P.ts(wn[:, j, 1:2], cwc[:, j, 2:3], carm1[:, 0:1], None, ALU.mult, reads=[rcwc, rconst], writes=[rwn])
        for j in range(2):
            for wi, c0 in enumerate((CH0, CB0, CC0)):
                wb, rwb = wcols(w_in_d[l, :, c0 + j * 128:c0 + (j + 1) * 128], 128)
                proj_fm(wb, rwb, 0, 128, lambda tg, bk, rb, wi=wi: P.cp(zc[:, wi, tg * 512:(tg + 1) * 512], bk[:, :],
                                                                      reads=[rb], writes=[rzc], eng="scalar"))
            P.tt(u[:], zc[:, 2, :], zc[:, 0, :], ALU.mult, reads=[rzc], writes=[ru], eng="gpsimd")
            P.ts(yv[:], u[:], cwc[:, j, 1:2], None, ALU.mult, reads=[ru, rcwc], writes=[ryv])
            P.stt(yv[:, 1:T], u[:, 0:T - 1], cwc[:, j, 0:1], yv[:, 1:T], ALU.mult, ALU.add, reads=[ru, rcwc, ryv], writes=[ryv])
            P.stt(yv[:, 0:T - 1], u[:, 1:T], cwc[:, j, 2:3], yv[:, 0:T - 1], ALU.mult, ALU.add, reads=[ru, rcwc, ryv], writes=[ryv])
            P.stt(yv[:, 256:T:256], u[:, 255:T - 1:256], wn[:, j, 0:1], yv[:, 256:T:256], ALU.mult, ALU.add,
                  reads=[ru, rwn, ryv], writes=[ryv])
            P.stt(yv[:, 255:T - 1:256], u[:, 256:T:256], wn[:, j, 1:2], yv[:, 255:T - 1:256], ALU.mult, ALU.add,
                  reads=[ru, rwn, ryv], writes=[ryv])
            P.stt(catC[:, j, :], yv[:], cwc[:, j, 3:4], zc[:, 1, :], ALU.add, ALU.mult, reads=[ryv, rcwc, rzc], writes=[rcatC])
        outproj(catC, rcatC, 384, 2, l)

        P.mute = "N" not in phases
        P.barrier()
        ar.reset()
        alloc_w()
        namask = ar.f32([128, NTYPE, 128]); rnamask = Res()
        P.dma(namask[:], namask_d.rearrange("t k q -> k t q"), writes=[rnamask])
        nab = ar.f32([128, 26, 128]); rnab = Res()
        BM = ar.bf16([128, 2, NTYPE, 128]); rBM = Res()
        qT = ar.bf16([128, T]); rqT = Res()
        kT = ar.bf16([128, T]); rkT = Res()
        wkv = ar.bf16([128, 8, 256]); rwkv = Res()
        kvst = [ar.f32([128, 256]), ar.f32([128, 256])]; rkvst = [Res(), Res()]
        vaug = ar.bf16([128, NT, 2, 66]); rvaug = Res()
        cst = ar.f32([128, 2, 2, 64]); rcst = Res()
        kcb = ar.bf16([128, 2, 128]); rkcb = Res()
        kcT = ar.bf16([128, 256]); rkcT = Res()
        vca = ar.bf16([128, 2, 2, 66]); rvca = Res()
        pTw = [ar.bf16([128, 8, 128]) for _ in range(2)]; rpTw = [Res() for _ in range(2)]
        opair = [ar.bf16([128, 128]) for _ in range(2)]; ropair = [Res(), Res()]
        catN = ar.bf16([128, 1, T]); rcatN = Res()
        P.memset(vaug[:, :, :, 64:66], 1.0, writes=[rvaug])
        P.memset(vca[:, :, :, 64:66], 1.0, writes=[rvca])
        pcnt = 0
        for p in range(3):
            wb, rwb = wcols(w_in_d[l, :, QN0 + p * 128:QN0 + (p + 1) * 128], 128)
            proj_fm(wb, rwb, 0, 128, lambda tg, bk, rb: P.act(qT[:, tg * 512:(tg + 1) * 512], bk[:, :], AF.Copy,
                                                             reads=[rb], writes=[rqT], scale=0.125))
            wb, rwb = wcols(w_in_d[l, :, KN0 + p * 128:KN0 + (p + 1) * 128], 128)
            proj_fm(wb, rwb, 0, 128, lambda tg, bk, rb: P.cp(kT[:, tg * 512:(tg + 1) * 512], bk[:, :],
                                                            reads=[rb], writes=[rkT], eng="scalar"))
            wcols(w_in_d[l, :, KN0 + p * 128:KN0 + (p + 1) * 128], 128, wkv, rwkv, 0)
            wcols(w_in_d[l, :, VN0 + p * 128:VN0 + (p + 1) * 128], 128, wkv, rwkv, 128)
            for i in range(NT):
                bk, rb = bank("s")
                for c in range(8):
                    P.mm(bk[:, 0:256], hT[:, c, i * 128:(i + 1) * 128], wkv[:, c, :], c == 0, c == 7,
                         reads=[rhT, rwkv], writes=[rb])
                kk_ = i % 2
                P.cp(kvst[kk_][:], bk[:, 0:256], reads=[rb], writes=[rkvst[kk_]], eng="scalar")
                P.dma(nk_d[l, i * 128:(i + 1) * 128, p * 128:(p + 1) * 128], kvst[kk_][:, 0:128], reads=[rkvst[kk_]], q="gpsimd")
                P.dma(nv_d[l, i * 128:(i + 1) * 128, p * 128:(p + 1) * 128], kvst[kk_][:, 128:256], reads=[rkvst[kk_]], q="gpsimd")
                P.cp(vaug[:, i, :, 0:64], kvst[kk_][:, 128:256].rearrange("p (h d) -> p h d", h=2),
                     reads=[rkvst[kk_]], writes=[rvaug])
            for kt in range(2):
                P.dma(cst[:, kt, :, :], kctx_d[l, 2 * p:2 * p + 2, kt * 128:(kt + 1) * 128, :].rearrange("h q d -> q h d"), writes=[rcst])
            P.cp(kcb[:].rearrange("p kt (h d) -> p kt h d", h=2), cst[:], reads=[rcst], writes=[rkcb])
            for kt in range(2):
                bk, rb = bank("s")
                P.mm(bk[:, 0:128], kcb[:, kt, :], ident[:], True, True, reads=[rkcb, rid], writes=[rb])
                P.cp(kcT[:, kt * 128:(kt + 1) * 128], bk[:, 0:128], reads=[rb], writes=[rkcT], eng="scalar")
            for kt in range(2):
                P.dma(cst[:, kt, :, :], vctx_d[l, 2 * p:2 * p + 2, kt * 128:(kt + 1) * 128, :].rearrange("h q d -> q h d"), writes=[rcst])
            P.cp(vca[:, :, :, 0:64], cst[:], reads=[rcst], writes=[rvca])
            for hh in range(2):
                P.dma(nab[:], nabias_d[l, 2 * p + hh].rearrange("t k q -> k t q"), writes=[rnab])
                P.tt(BM[:, hh, 0:26, :], nab[:], namask[:, 0:26, :], ALU.add, reads=[rnab, rnamask], writes=[rBM])
                P.cp(BM[:, hh, 26, :], namask[:, 26, :], reads=[rnamask], writes=[rBM])
            for i in range(NT):
                for hh in range(2):
                    hp = slice(hh * 64, hh * 64 + 64)
                    blocks = [("l", kt, ty) for kt, ty in na_blocks(i)] + [("c", 0, 26), ("c", 1, 26)]
                    nb_ = len(blocks)
                    pw = pTw[pcnt % 2]; rpw = rpTw[pcnt % 2]
                    pcnt += 1
                    for g0 in range(0, nb_, 4):
                        grp = blocks[g0:g0 + 4]
                        bs, rbs = bank("s")
                        for j, (src, kt, ty) in enumerate(grp):
                            ksrc, rks = (kT, rkT) if src == "l" else (kcT, rkcT)
                            P.mm(bs[:, j * 128:(j + 1) * 128], ksrc[hp, kt * 128:(kt + 1) * 128], qT[hp, i * 128:(i + 1) * 128], True, False,
                                 reads=[rks, rqT], writes=[rbs])
                            P.mm(bs[:, j * 128:(j + 1) * 128], ident[:], BM[:, hh, ty, :], False, True, reads=[rid, rBM], writes=[rbs])
                        n_ = len(grp)
                        P.act(pw[:, g0:g0 + n_, :], bs[:, 0:n_ * 128].rearrange("p (a b) -> p a b", a=n_), AF.Exp,
                              reads=[rbs], writes=[rpw])
                    bo, rbo = bank("a")
                    for bi, (src, kt, ty) in enumerate(blocks):
                        vsrc, rvs = (vaug[:, kt, hh, 0:65], rvaug) if src == "l" else (vca[:, kt, hh, 0:65], rvca)
                        P.mm(bo[:, 0:65], pw[:, bi, :], vsrc, bi == 0, bi == nb_ - 1, reads=[rpw, rvs], writes=[rbo])
                    P.op("vector", lambda e, bo=bo: e.reciprocal(out=small[:, 1:2], in_=bo[:, 64:65]), [rbo], [rsmall])
                    P.ts(opair[i % 2][:, hp], bo[:, 0:64], small[:, 1:2], None, ALU.mult, reads=[rbo, rsmall], writes=[ropair[i % 2]])
                bk, rb = bank("b")
                P.mm(bk[:, 0:128], opair[i % 2][:], ident[:], True, True, reads=[ropair[i % 2], rid], writes=[rb])
                P.cp(catN[:, 0, i * 128:(i + 1) * 128], bk[:, 0:128], reads=[rb], writes=[rcatN], eng="scalar")
            off_save = ar.off
            outproj(catN, rcatN, 640 + p * 128, 1, l)
            ar.off = off_save
            P.barrier()

        P.mute = "G" not in phases
        ar.reset()
        alloc_w()
        lrT = [ar.bf16([16, T]), ar.bf16([16, T])]; rlrT = Res()
        wa16 = ar.f32([16, 2, 384]); rwa16 = Res()
        wab16 = ar.bf16([16, 2, 384]); rwab16 = Res()
        P.dma(wa16[0:16, 0, :], waf_d[l], writes=[rwa16])
        P.dma(wa16[0:16, 1, :], wab_d[l], writes=[rwa16])
        P.cp(wab16[0:16], wa16[0:16], reads=[rwa16], writes=[rwab16])
        for d_ in range(2):
            wb, rwb = wcols(w_in_d[l, :, LRF0 + d_ * 16:LRF0 + (d_ + 1) * 16], 16)
            proj_fm(wb, rwb, 0, 16, lambda tg, bk, rb, d_=d_: P.cp(lrT[d_][0:16, tg * 512:(tg + 1) * 512], bk[0:16, :],
                                                                  reads=[rb], writes=[rlrT], eng="scalar"))
        gqT = ar.bf16([128, T]); rgq = Res()
        gkT = ar.bf16([128, T]); rgk = Res()
        a_off = ar.off
        A_ = ar.f32([128, T]); rA = Res()
        B_ = ar.f32([128, T]); rB = Res()
        Eb = ar.bf16([128, T]); rEb = Res()
        totc = ar.f32([128, NT]); rtotc = Res()
        _qd = ar.bf16([128, T]); _rqd = Res(); qd = [_qd, _qd]; rqd = [_rqd, _rqd]
        _kd = ar.bf16([128, T]); _rkd = Res(); kd = [_kd, _kd]; rkd = [_rkd, _rkd]
        _kdec = ar.bf16([128, T]); _rkdec = Res(); kdec = [_kdec, _kdec]; rkdec = [_rkdec, _rkdec]
        _kdt = ar.bf16([128, NT, 128]); _rkdt = Res(); kdtok = [_kdt, _kdt]; rkdtok = [_rkdt, _rkdt]
        elast = ar.f32([128, 2, NT]); rel = Res()
        wvg = ar.bf16([128, 8, 256]); rwvg = Res()
        v_b = ar.bf16([128, NT, 128]); rvb = Res()
        sg = ar.bf16([128, NT, 128]); rsg = Res()
        of = ar.f32([128, NT, 128]); rof = Res()
        ycat = sg; rycat = rsg
        catG = gqT.rearrange("p (o t) -> p o t", o=1); rcatG = rgq
        nb = ar.f32([128, 2]); rnb = Res()
        gwb = ar.f32([128, 128]); rgwb = Res()
        S = ar.f32([128, 64]); rS = Res()
        Sb = ar.bf16([128, 64]); rSb = Res()
        s0 = ar.f32([128, 2, 64]); rs0 = Res()
        sst = ar.f32([128, 2, 8, 64]); rsst = Res()
        attm = [ar.bf16([128, 2, 128]), ar.bf16([128, 2, 128])]; rattm = [Res(), Res()]
        ot = ar.f32([128, 128]); rot = Res()
        osq = ar.f32([128, 128]); rosq = Res()
        Bv = B_.rearrange("p (c j) -> p c j", j=128)
        totb = totc[:, :].unsqueeze(2).to_broadcast([128, NT, 128])
        Av = A_.rearrange("p (c j) -> p c j", j=128)
        for p in range(3):
            cs = slice(p * 128, (p + 1) * 128)
            wb, rwb = wcols(w_in_d[l, :, QA0 + p * 128:QA0 + (p + 1) * 128], 128)
            proj_fm(wb, rwb, 0, 128, lambda tg, bk, rb: P.act(gqT[:, tg * 512:(tg + 1) * 512], bk[:, :], AF.Copy,
                                                             reads=[rb], writes=[rgq], scale=0.125))
            wb, rwb = wcols(w_in_d[l, :, KA0 + p * 128:KA0 + (p + 1) * 128], 128)
            proj_fm(wb, rwb, 0, 128, lambda tg, bk, rb: P.cp(gkT[:, tg * 512:(tg + 1) * 512], bk[:, :],
                                                            reads=[rb], writes=[rgk], eng="scalar"))
            wcols(w_in_d[l, :, VA0 + p * 128:VA0 + (p + 1) * 128], 128, wvg, rwvg, 0)
            wcols(w_in_d[l, :, GA0 + p * 128:GA0 + (p + 1) * 128], 128, wvg, rwvg, 128)
            for i in range(NT):
                bk, rb = bank("s")
                for c in range(8):
                    P.mm(bk[:, 0:256], hT[:, c, i * 128:(i + 1) * 128], wvg[:, c, :], c == 0, c == 7,
                         reads=[rhT, rwvg], writes=[rb])
                P.cp(v_b[:, i, :], bk[:, 0:128], reads=[rb], writes=[rvb], eng="scalar")
                P.act(sg[:, i, :], bk[:, 128:256], AF.Silu, reads=[rb], writes=[rsg])
            if os.environ.get("GSTOP") == "b":
                P.mute = True
            P.dma(nb[:, 0:1], baf_d[l, cs].rearrange("(p o) -> p o", o=1), writes=[rnb], **NSC)
            P.dma(nb[:, 1:2], bab_d[l, cs].rearrange("(p o) -> p o", o=1), writes=[rnb], **NSC)
            P.ts(nb[:], nb[:], -1.0, None, ALU.mult, reads=[rnb], writes=[rnb])
            P.dma(gwb[:], gnw_d[l, cs].partition_broadcast(128), writes=[rgwb])
            P.dma(s0[:, 0, :], s0f_d[l, 2 * p:2 * p + 2].rearrange("h k v -> (h k) v"), writes=[rs0])
            P.dma(s0[:, 1, :], s0b_d[l, 2 * p:2 * p + 2].rearrange("h k v -> (h k) v"), writes=[rs0])
            for d_ in range(2):
                for tg in range(4):
                    bk, rb = bank("s")
                    P.mm(bk[:, :], wab16[0:16, d_, cs], lrT[d_][0:16, tg * 512:(tg + 1) * 512], True, True,
                         reads=[rwab16, rlrT], writes=[rb])
                    P.act(A_[:, tg * 512:(tg + 1) * 512], bk[:, :], AF.Exp, reads=[rb, rnb], writes=[rA],
                          scale=-1.0, bias=nb[:, d_:d_ + 1])
                P.act(A_[:], A_[:], AF.Ln, reads=[rA], writes=[rA], bias=1.0)
                for c in range(NT):
                    P.op("vector", lambda e, c=c: e.tensor_tensor_scan(out=B_[:, c * 128:(c + 1) * 128], data0=onesf[:, :],
                                                                      data1=A_[:, c * 128:(c + 1) * 128], initial=0.0,
                                                                      op0=ALU.mult, op1=ALU.add), [rA, rconst], [rB])
                P.cp(totc[:, :], B_[:, 127:T:128], reads=[rB], writes=[rtotc])
                if d_ == 0:
                    RC, rRC, FR, rFR, FRv = B_, rB, A_, rA, Av
                else:
                    P.tt(A_[:], A_[:], B_[:], ALU.subtract, reads=[rA, rB], writes=[rA], eng="gpsimd")
                    P.tt(Av, Av, totb, ALU.add, reads=[rA, rtotc], writes=[rA], eng="gpsimd")
                    RC, rRC, FR, rFR, FRv = A_, rA, B_, rB, Bv
                RCv = RC.rearrange("p (c j) -> p c j", j=128)
                P.act(Eb[:], RC[:], AF.Exp, reads=[rRC], writes=[rEb], scale=-1.0 / 16)
                P.tt(qd[d_][:], gqT[:], Eb[:], ALU.mult, reads=[rgq, rEb], writes=[rqd[d_]])
                P.act(Eb[:], RC[:], AF.Exp, reads=[rRC], writes=[rEb], scale=1.0 / 16)
                P.tt(kd[d_][:], gkT[:], Eb[:], ALU.mult, reads=[rgk, rEb], writes=[rkd[d_]])
                P.tt(FRv, RCv, totb, ALU.subtract, reads=[rRC, rtotc], writes=[rFR], eng="gpsimd")
                P.act(Eb[:], FR[:], AF.Exp, reads=[rFR], writes=[rEb], scale=1.0 / 16)
                P.tt(kdec[d_][:], gkT[:], Eb[:], ALU.mult, reads=[rgk, rEb], writes=[rkdec[d_]])
                P.act(elast[:, d_, :], totc[:, :], AF.Exp, reads=[rtotc], writes=[rel], scale=-1.0 / 16)
                for c in range(NT):
                    bk, rb = bank("s")
                    P.mm(bk[:, 0:128], kdec[d_][:, c * 128:(c + 1) * 128], ident[:], True, True, reads=[rkdec[d_], rid], writes=[rb])
                    P.cp(kdtok[d_][:, c, :], bk[:, 0:128], reads=[rb], writes=[rkdtok[d_]], eng="scalar")
                if os.environ.get("GSTOP") == "c":
                    P.mute = True
                order = list(range(NT)) if d_ == 0 else list(range(NT - 1, -1, -1))
                P.cp(S[:], s0[:, d_, :], reads=[rs0], writes=[rS])
                for n_, c in enumerate(order):
                    ccs = slice(c * 128, (c + 1) * 128)
                    if n_ > 0 and n_ % 2 == 0:
                        P.ts(S[:], S[:], carry[:, 0:1], None, ALU.mult, reads=[rS, rconst], writes=[rS])
                    P.cp(Sb[:], S[:], reads=[rS], writes=[rSb], eng="scalar")
                    am = attm[n_ % 2]; ram = rattm[n_ % 2]
                    GSK = os.environ.get("GSKIP", "")
                    _m0 = P.mute
                    bos = []
                    for hh in range(2):
                        hp = slice(hh * 64, hh * 64 + 64)
                        ba, rba = bank("s")
                        P.mm(ba[:, 0:128], kd[d_][hp, ccs], qd[d_][hp, ccs], True, True,
                             reads=[rkd[d_], rqd[d_]], writes=[rba])
                        P.tt(am[:, hh, :], ba[:, 0:128], glam[:, d_, :], ALU.mult, reads=[rba, rconst], writes=[ram])
                    for hh in range(2):
                        hp = slice(hh * 64, hh * 64 + 64)
                        bo, rbo = bank("a")
                        bos.append((bo, rbo))
                        P.mm(bo[:, 0:64], am[:, hh, :], v_b[:, c, hh * 64:(hh + 1) * 64], True, False,
                             reads=[ram, rvb], writes=[rbo])
                        P.mm(bo[:, 0:64], qd[d_][hp, ccs], Sb[hp, :], False, True,
                             reads=[rqd[d_], rSb], writes=[rbo])
                    if d_ == 0:
                        for hh in range(2):
                            P.cp(of[:, c, hh * 64:(hh + 1) * 64], bos[hh][0][:, 0:64], reads=[bos[hh][1]], writes=[rof], eng="scalar")
                    else:
                        for hh in range(2):
                            P.tt(ot[:, hh * 64:(hh + 1) * 64], of[:, c, hh * 64:(hh + 1) * 64], bos[hh][0][:, 0:64], ALU.add,
                                 reads=[rof, bos[hh][1]], writes=[rot])
                        P.tt(osq[:], ot[:], ot[:], ALU.mult, reads=[rot], writes=[rosq], eng="gpsimd")
                        P.op("vector", lambda e: e.tensor_reduce(out=small[:, 2:4], in_=osq[:].rearrange("p (h d) -> p h d", h=2),
                                                                 axis=AX.X, op=ALU.add), [rosq], [rsmall])
                        P.ts(small[:, 2:4], small[:, 2:4], 1.0 / 64, 1e-6, ALU.mult, ALU.add, reads=[rsmall], writes=[rsmall])
                        P.act(small[:, 2:4], small[:, 2:4], AF.Sqrt, reads=[rsmall], writes=[rsmall])
                        P.op("vector", lambda e: e.reciprocal(out=small[:, 2:4], in_=small[:, 2:4]), [rsmall], [rsmall])
                        P.tt(ot[:].rearrange("p (h d) -> p h d", h=2), ot[:].rearrange("p (h d) -> p h d", h=2),
                             small[:, 2:4].unsqueeze(2).to_broadcast([128, 2, 64]), ALU.mult, reads=[rot, rsmall], writes=[rot])
                        P.tt(ot[:], ot[:], gwb[:], ALU.mult, reads=[rot, rgwb], writes=[rot])
                        P.tt(ycat[:, c, :], ot[:], sg[:, c, :], ALU.mult, reads=[rot, rsg], writes=[rycat])
                    P.mute = _m0 or ("s" in GSK)
                    bd, rbd = bank("b")
                    P.mm(bd[:, 0:128], kdtok[d_][:, c, :], v_b[:, c, :], True, True, reads=[rkdtok[d_], rvb], writes=[rbd])
                    for hh in range(2):
                        hp = slice(hh * 64, hh * 64 + 64)
                        P.stt(S[hp, :], S[hp, :], elast[hp, d_, c:c + 1], bd[hp, hh * 64:(hh + 1) * 64], ALU.mult, ALU.add,
                              reads=[rS, rel, rbd], writes=[rS])
                    if n_ % 2 == 1:
                        P.cp(sst[:, d_, c // 2, :], S[:], reads=[rS], writes=[rsst], eng="gpsimd")
                    P.mute = _m0
            if os.environ.get("GSTOP") == "d":
                P.mute = True
            for c in range(NT):
                bk, rb = bank("s")
                P.mm(bk[:, 0:128], ycat[:, c, :], ident[:], True, True, reads=[rycat, rid], writes=[rb])
                P.cp(catG[:, 0, c * 128:(c + 1) * 128], bk[:, 0:128], reads=[rb], writes=[rcatG], eng="scalar")
            P.dma(sf_d[l, :, 2 * p:2 * p + 2].rearrange("g h k v -> (h k) g v"), sst[:, 0, :, :], reads=[rsst], q="gpsimd")
            P.dma(sbo_d[l, :, 2 * p:2 * p + 2].rearrange("g h k v -> (h k) g v"), sst[:, 1, :, :], reads=[rsst], q="gpsimd")
            P.barrier()
            off_save = ar.off
            ar.off = a_off
            outproj(catG, rcatG, p * 128, 1, l)
            ar.off = off_save
            P.barrier()

        P.mute = "P" not in phases
        P.barrier()
        norm_to_hT(1)
        P.barrier()
        ar.reset()
        pq = ar.bf16([128, 8, T]); rpq = Res()
        kk = ar.bf16([128, 8, 128]); rkk = Res()
        Kc = ar.bf16([128, 8, 4, 128]); rKc = Res()
        tauS = ar.f32([128, NT, 8]); rtau = Res()
        negcS = ar.f32([128, NT, 8]); rnegc = Res()
        pk_off = ar.off
        alloc_w()
        kst = ar.f32([128, 8, 128]); rkst = Res()
        kstb = ar.bf16([128, 8, 128]); rkstb = Res()
        for hq in range(8):
            wb, rwb = wcols(wq_d[l, :, hq * 128:(hq + 1) * 128], 128)
            proj_fm(wb, rwb, 0, 128, lambda tg, bk, rb, hq=hq: P.cp(pq[:, hq, tg * 512:(tg + 1) * 512], bk[:, :],
                                                                   reads=[rb], writes=[rpq], eng="scalar"))
        P.dma(kst[:, :, 0:64], k1_d[l].rearrange("h n d -> n h d"), writes=[rkst])
        P.dma(kst[:, :, 64:128], k2_d[l].rearrange("h n d -> n h d"), writes=[rkst])
        P.cp(kstb[:], kst[:], reads=[rkst], writes=[rkstb])
        for h in range(8):
            bk, rb = bank("s")
            P.mm(bk[:, 0:128], kstb[:, h, :], ident[:], True, True, reads=[rkstb, rid], writes=[rb])
            P.cp(kk[:, h, :], bk[:, 0:128], reads=[rb], writes=[rkk], eng="scalar")
        P.cp(Kc[64:128], kk[64:128, :, :].unsqueeze(2).to_broadcast([64, 8, 4, 128]), reads=[rkk], writes=[rKc], eng="gpsimd")
        ssb = ar.f32([128, 16, 128]); rssb = Res()
        tmpa2 = [ar.f32([128, 128]), ar.f32([128, 128])]; rtmpa2 = [Res(), Res()]
        v16 = ar.f32([128, 16, 16]); rv16r = [Res() for _ in range(16)]
        cand = ar.f32([128, 8, 256]); rcand = Res()
        cta2 = [ar.f32([128, 256]), ar.f32([128, 256])]; rcta2 = [Res(), Res()]
        c16 = ar.f32([128, 8, 16]); rc16r = [Res() for _ in range(8)]
        e16 = ar.f32([128, 8, 16]); re16 = Res()
        zz = ar.f32([128, 8]); rzz = Res()
        for i in range(NT):
            tcs = slice(i * 128, (i + 1) * 128)
            for hg in range(2):
                b0, rb0 = bank("s")
                b1, rb1 = bank("s")
                for h4 in range(4):
                    h = hg * 4 + h4
                    P.mm(b0[:, h4 * 128:(h4 + 1) * 128], pq[0:64, h, tcs], kk[0:64, h, :], True, True, reads=[rpq, rkk], writes=[rb0])
                for h4 in range(4):
                    h = hg * 4 + h4
                    P.mm(b1[:, h4 * 128:(h4 + 1) * 128], pq[64:128, h, tcs], kk[64:128, h, :], True, True, reads=[rpq, rkk], writes=[rb1])
                sv = ssb[:, hg * 8:hg * 8 + 8, :].rearrange("p (h s) k -> p h s k", s=2)
                P.cp(sv[:, :, 0, :], b0[:, :].rearrange("p (a b) -> p a b", a=4), reads=[rb0], writes=[rssb], eng="scalar")
                P.cp(sv[:, :, 1, :], b1[:, :].rearrange("p (a b) -> p a b", a=4), reads=[rb1], writes=[rssb], eng="scalar")
            for r0_ in range(0, 16, 2):
                rows = (r0_, r0_ + 1)
                for j, r in enumerate(rows):
                    P.op("vector", lambda e, r=r: e.max(out=v16[:, r, 0:8], in_=ssb[:, r, :]), [rssb], [rv16r[r]])
                for j, r in enumerate(rows):
                    P.op("vector", lambda e, r=r, j=j: e.match_replace(out=tmpa2[j][:], in_to_replace=v16[:, r, 0:8], in_values=ssb[:, r, :], imm_value=-1e30),
                         [rssb, rv16r[r]], [rtmpa2[j]])
                for j, r in enumerate(rows):
                    P.op("vector", lambda e, r=r, j=j: e.max(out=v16[:, r, 8:16], in_=tmpa2[j][:]), [rtmpa2[j]], [rv16r[r]])
            v4 = v16[:, :, :].rearrange("p (h s) k -> p h s k", s=2)
            P.tt(cand[:].rearrange("p h (a b) -> p h a b", a=16),
                 v4[:, :, 0, :].unsqueeze(3).to_broadcast([128, 8, 16, 16]),
                 v4[:, :, 1, :].unsqueeze(2).to_broadcast([128, 8, 16, 16]), ALU.add, reads=rv16r, writes=[rcand], eng="gpsimd")
            for h0_ in range(0, 8, 2):
                hs_ = (h0_, h0_ + 1)
                for j, h in enumerate(hs_):
                    P.op("vector", lambda e, h=h: e.max(out=c16[:, h, 0:8], in_=cand[:, h, :]), [rcand], [rc16r[h]])
                for j, h in enumerate(hs_):
                    P.op("vector", lambda e, h=h, j=j: e.match_replace(out=cta2[j][:], in_to_replace=c16[:, h, 0:8], in_values=cand[:, h, :], imm_value=-1e30),
                         [rcand, rc16r[h]], [rcta2[j]])
                for j, h in enumerate(hs_):
                    P.op("vector", lambda e, h=h, j=j: e.max(out=c16[:, h, 8:16], in_=cta2[j][:]), [rcta2[j]], [rc16r[h]])
            P.ts(tauS[:, i, :], c16[:, :, 15], -2e-5, None, ALU.add, reads=rc16r, writes=[rtau])
            P.tt(e16[:], c16[:, :, :], c16[:, :, 0:1].to_broadcast([128, 8, 16]), ALU.subtract, reads=rc16r, writes=[re16])
            P.act(e16[:], e16[:], AF.Exp, reads=[re16], writes=[re16])
            P.op("vector", lambda e: e.tensor_reduce(out=zz[:], in_=e16[:], axis=AX.X, op=ALU.add), [re16], [rzz])
            P.act(zz[:], zz[:], AF.Ln, reads=[rzz], writes=[rzz])
            P.tt(zz[:], zz[:], c16[:, :, 0], ALU.add, reads=[rzz] + rc16r, writes=[rzz])
            P.ts(negcS[:, i, :], zz[:], -1.0, None, ALU.mult, reads=[rzz], writes=[rnegc])
        P.tt(tauS[:], tauS[:], negcS[:], ALU.add, reads=[rtau, rnegc], writes=[rtau])
        P.act(tauS[:], tauS[:], AF.Exp, reads=[rtau], writes=[rtau])
        P.barrier()
        ar.off = pk_off
        ust = ar.f32([128, D]); rust = [Res(), Res()]
        vst = ar.f32([128, D]); rvst = [Res(), Res()]
        ub = ar.bf16([128, 2, D]); rub = [Res(), Res()]
        UT = ar.bf16([128, 8, 512]); rUT = [Res() for _ in range(4)]
        Vb = ar.bf16([128, 4, D]); rVb = [Res() for _ in range(4)]
        gel = [ar.bf16([128, 4, 512]), ar.bf16([128, 4, 512])]; rgel = [Res(), Res()]
        Ebf = [ar.f32([128, 512]) for _ in range(3)]; rEbf = [Res() for _ in range(3)]
        _Gm = ar.bf16([128, 8, 512]); _rGm = [Res() for _ in range(8)]; Gm = [_Gm, _Gm]; rGm = [_rGm, _rGm]
        _AT = ar.bf16([128, 4, 128]); _rAT = Res(); AT = [_AT, _AT]; rAT = [_rAT, _rAT]
        _Gsb = ar.bf16([128, 512]); _rGsb = Res(); Gsb = [_Gsb, _Gsb]; rGsb = [_rGsb, _rGsb]
        ecnt = 0
        ucnt = 0
        tl = {}

        def gate_head(i, gs, h):
            nonlocal ecnt
            tcs = slice(i * 128, (i + 1) * 128)
            bk, rb = bank("s")
            P.mm(bk[:, :], pq[:, h, tcs], Kc[:, h, :, :].rearrange("p a b -> p (a b)"), True, True,
                 reads=[rpq, rKc], writes=[rb])
            ek = ecnt % 3
            ecnt += 1
            P.act(Ebf[ek][:], bk[:, :], AF.Exp, reads=[rb, rnegc], writes=[rEbf[ek]], bias=negcS[:, i, h:h + 1], scale=1.0)
            P.stt(Gm[gs][:, h, :], Ebf[ek][:], tauS[:, i, h:h + 1], Ebf[ek][:], ALU.is_ge, ALU.mult,
                  reads=[rtau, rEbf[ek]], writes=[rGm[gs][h]])

        def tail_a(i, gs, gl, t4):
            bsum, rbsum = bank("a")
            for h in range(8):
                P.mm(bsum[:, :], ident[:], Gm[gs][:, h, :], h == 0, h == 7, reads=[rGm[gs][h], rid], writes=[rbsum])
            P.cp(Gsb[gs][:], bsum[:, :], reads=[rbsum], writes=[rGsb[gs]], eng="scalar")

        def tail_b(i, gs, gl, t4):
            bg, rbg = bank("a")
            for k4 in range(4):
                P.mm(bg[:, k4 * 128:(k4 + 1) * 128], Gsb[gs][:, k4 * 128:(k4 + 1) * 128], ident[:], True, True,
                     reads=[rGsb[gs], rid], writes=[rbg])
            P.tt(AT[gs][:], gel[gl][:, :, t4 * 128:(t4 + 1) * 128], bg[:, :].rearrange("p (a b) -> p a b", a=4), ALU.mult,
                 reads=[rgel[gl], rbg], writes=[rAT[gs]])

        def tail_c(i, gs, gl, t4):
            for hf in range(2):
                bo, rbo = bank("b")
                for k4 in range(4):
                    P.mm(bo[:, :], AT[gs][:, k4, :], Vb[:, k4, hf * 512:(hf + 1) * 512], k4 == 0, k4 == 3,
                         reads=[rAT[gs], rVb[k4]], writes=[rbo])
                P.tt(x[:, i, hf * 512:(hf + 1) * 512], x[:, i, hf * 512:(hf + 1) * 512], bo[:, :], ALU.add,
                     reads=[rx[i], rbo], writes=[rx[i]])

        for quad in range(32):
            for k4 in range(4):
                r0 = (quad * 4 + k4) * 128
                kb = k4 % 2
                for hf in range(2):
                    hs = slice(hf * 512, (hf + 1) * 512)
                    P.dma(ust[:, hs], pu_d[l, r0:r0 + 128, hs], writes=[rust[hf]])
                    P.dma(vst[:, hs], pv_d[l, r0:r0 + 128, hs], writes=[rvst[hf]])
                for hf in range(2):
                    hs = slice(hf * 512, (hf + 1) * 512)
                    P.act(ub[:, kb, hs], ust[:, hs], AF.Copy, reads=[rust[hf]], writes=[rub[kb]])
                    P.tt(Vb[:, k4, hs], vst[:, hs], gb[:, 1, hs], ALU.mult, reads=[rvst[hf], rgb], writes=[rVb[k4]], eng="vector")
                for c2 in range(2):
                    bk, rb = bank("s")
                    for c4 in range(4):
                        c = c2 * 4 + c4
                        P.mm(bk[:, c4 * 128:(c4 + 1) * 128], ub[:, kb, c * 128:(c + 1) * 128], ident[:], True, True,
                             reads=[rub[kb], rid], writes=[rb])
                    P.cp(UT[:, c2 * 4:(c2 + 1) * 4, k4 * 128:(k4 + 1) * 128], bk[:, :].rearrange("p (a b) -> p a b", a=4),
                         reads=[rb], writes=[rUT[k4]], eng="vector")
            for h2 in range(2):
                P.cp(Kc[0:64, h2 * 4:(h2 + 1) * 4], kk[0:64, h2 * 4:(h2 + 1) * 4, quad * 4:quad * 4 + 4].unsqueeze(3).to_broadcast([64, 4, 4, 128]),
                     reads=[rkk], writes=[rKc], eng=("vector" if h2 == 0 else "scalar"))
            prev = None
            for tg in range(4):
                gl = tg % 2
                for k4 in range(4):
                    bk, rb = bank("s")
                    for c in range(8):
                        P.mm(bk[:, :], UT[:, c, k4 * 128:(k4 + 1) * 128], hT[:, c, tg * 512:(tg + 1) * 512], c == 0, c == 7,
                             reads=[rUT[k4], rhT], writes=[rb])
                    P.act(gel[gl][:, k4, :], bk[:, :], AF.Gelu, reads=[rb], writes=[rgel[gl]])
                for t4 in range(4):
                    i = tg * 4 + t4
                    gs = ucnt % 2
                    ucnt += 1
                    if prev is not None:
                        tail_a(*prev)
                    for h in range(8):
                        gate_head(i, gs, h)
                        if prev is not None:
                            if h == 3:
                                tail_b(*prev)
                            elif h == 7:
                                tail_c(*prev)
                    prev = (i, gs, gl, t4)
            tail_a(*prev)
            tail_b(*prev)
            tail_c(*prev)

    P.mute = False
    P.barrier()
    ar.reset()
    fwb = ar.f32([128, D]); rfwb = Res()
    sq = ar.f32([128, D]); rsq = Res()
    yo = [ar.f32([128, D]), ar.f32([128, D])]; ryo = [Res(), Res()]
    P.dma(fwb[:], fnw_d.partition_broadcast(128), writes=[rfwb])
    for i in range(NT):
        if raw:
            P.cp(yo[i % 2][:], x[:, i, :], reads=[rx[i]], writes=[ryo[i % 2]])
        else:
            rstd_of(x[:, i, :], D, small[:, 0:1], [rx[i]], sq, rsq)
            P.stt(yo[i % 2][:], x[:, i, :], small[:, 0:1], fwb[:], ALU.mult, ALU.mult, reads=[rx[i], rsmall, rfwb], writes=[ryo[i % 2]])
        P.dma(y_d[i * 128:(i + 1) * 128, :], yo[i % 2][:], reads=[ryo[i % 2]], q="gpsimd")
    P.emit()
    P.es.close()
    return nc, P


def _na_tables():
    a = np.arange(128)
    types = {}
    for i in range(NT):
        for kt, ty in na_blocks(i):
            types.setdefault(ty, (i, kt))
    DR = np.zeros((26, 128, 128), np.int64); DC = np.zeros((26, 128, 128), np.int64)
    msk_s = np.zeros((NTYPE, 128, 128), np.float32); msk_p = np.zeros((NTYPE, 128, 128), np.float32)
    for ty, (i, kt) in types.items():
        krow = (2 * kt + a // 64)[:, None]; kcol = (a % 64)[:, None]
        qrow = (2 * i + a // 64)[None, :]; qcol = (a % 64)[None, :]
        rs = np.clip(qrow - 4, 0, 24)
        cs = np.clip(qcol - 8, 0, 48)
        valid = (krow >= rs) & (krow < rs + 8) & (kcol >= cs) & (kcol < cs + 16)
        DR[ty] = np.clip(krow - qrow, -7, 7) + 7
        DC[ty] = np.clip(kcol - qcol, -15, 15) + 15
        msk_s[ty] = np.where(valid, 0.0, -1e30)
        msk_p[ty] = 0.0 if (kt // 2 == i // 2) else -1e30
    msk_s[26] = 0.0
    msk_p[26] = -1e30
    return DR, DC, msk_s, msk_p


_CACHE = {}


def kernel(x_prompt, x_sample, cache_na_k, cache_na_v, state_gla_fwd, state_gla_bwd, c, c_ctx,
           w_ada, b_ada, norm1_w, norm2_w, w_in, w_af, b_af, w_ab, b_ab, gla_norm_w,
           conv_w, conv_b, na_rpb, w_out, peer_wq, peer_k1, peer_k2, peer_u, peer_v, final_norm_w):
    f = lambda a: np.ascontiguousarray(np.asarray(a), dtype=np.float32)
    x_prompt, x_sample = f(x_prompt), f(x_sample)
    DR, DC, msk_s, msk_p = _na_tables()
    rpb = f(na_rpb)
    nab_s = np.ascontiguousarray(rpb[:, :, DR, DC])
    nab_p = np.zeros_like(nab_s)
    a = np.arange(128)
    glam = np.stack([(a[:, None] <= a[None, :]), (a[:, None] >= a[None, :])]).astype(np.float32)
    shared = {
        "ident": np.eye(128).astype(NPBF), "glamask": glam,
        "w_ada": f(w_ada), "b_ada": f(b_ada), "norm1_w": f(norm1_w), "norm2_w": f(norm2_w), "w_in": f(w_in),
        "w_af": f(w_af), "b_af": f(b_af), "w_ab": f(w_ab), "b_ab": f(b_ab), "gla_norm_w": f(gla_norm_w),
        "conv_w": f(conv_w), "conv_b": f(conv_b), "w_out": f(w_out), "peer_wq": f(peer_wq),
        "peer_k1": f(peer_k1), "peer_k2": f(peer_k2), "peer_u": f(peer_u), "peer_v": f(peer_v),
        "final_norm_w": f(final_norm_w),
    }
    zk = np.zeros((DEPTH, 6, 256, 64), np.float32)
    zs = np.zeros((DEPTH, 6, 64, 64), np.float32)
    in_maps = []
    for core in range(8):
        m = dict(shared)
        if core < 4:
            b = core
            m.update(x=x_sample[b], cvec=f(c)[b], kctx=f(cache_na_k)[b], vctx=f(cache_na_v)[b],
                     s0f=f(state_gla_fwd)[b], s0b=f(state_gla_bwd)[b], carry=np.ones((128, 1), np.float32),
                     nabias=nab_s, namask=msk_s)
        else:
            b0 = (core - 4) * 4
            xp = x_prompt[b0:b0 + 4].reshape(1024, D)
            m.update(x=np.ascontiguousarray(np.concatenate([xp, xp], 0)), cvec=f(c_ctx), kctx=zk, vctx=zk,
                     s0f=zs, s0b=zs, carry=np.zeros((128, 1), np.float32), nabias=nab_p, namask=msk_p)
        in_maps.append(m)
    if _CACHE.get("only_maps"):
        return in_maps
    if "nc" not in _CACHE:
        _CACHE["nc"] = build()
    nc, P = _CACHE["nc"]
    res = run_bass_kernel_spmd(nc, in_maps, core_ids=list(range(8)))
    R = res.results
    y_sample = np.stack([R[b]["y"] for b in range(4)], 0)
    y_prompt = np.concatenate([R[4 + j]["y"][:1024].reshape(4, 256, D) for j in range(4)], 0)

    def kvout(name):
        o = np.concatenate([R[4 + j][name][:, :1024].reshape(DEPTH, 4, 256, 6, 64) for j in range(4)], 1)
        return np.ascontiguousarray(o.transpose(1, 0, 3, 2, 4))

    def stout(name):
        o = np.concatenate([R[4 + j][name][:, :4] for j in range(4)], 1)
        return np.ascontiguousarray(o.transpose(1, 0, 2, 3, 4))
    return (y_prompt.astype(np.float32), y_sample.astype(np.float32), kvout("nk"), kvout("nv"), stout("sf"), stout("sbo"))
```
